# Optimizing a Trainium2 kernel written in Bass

```python
import jax, jax.numpy as jnp
from jax import lax
import numpy as np

D_MODEL = 1024
BATCH = 8
SEQ = 2048
DEPTH = 2
DEC_BATCH = 128
DEC_SEQ = 1
PAST_LEN = 2048
PAGE_SIZE = 128

D_CONV = 256
D_POOL = 256
N_HEADS = 8
N_KV_HEADS = 2
HEAD_DIM = 64
D_ATTN = N_HEADS * HEAD_DIM
D_MIX = D_CONV + D_POOL + D_ATTN
CONV_WIDTH = 31
CONV_STATE = CONV_WIDTH - 1
POOL_WINDOWS = (2, 4, 8, 16)
N_POOL_GROUPS = len(POOL_WINDOWS)
POOL_GROUP = D_POOL // N_POOL_GROUPS
POOL_STATE = max(POOL_WINDOWS) - 1
CMP_BLOCK = 32
CMP_STRIDE = 16
SLC_BLOCK = 64
N_SEL = 8
WINDOW = 256
Q_BLOCK = 128
ROT_DIM = HEAD_DIM // 4
ROPE_THETA = 500000.0
N_BRANCH = 3
D_KV = N_BRANCH * 2 * N_KV_HEADS * HEAD_DIM
D_IN = 3 * D_CONV + 2 * D_POOL + 2 * D_ATTN + D_KV + N_BRANCH * N_HEADS
EPS = 1e-6
NEG = -1e30
BIG = 1e4

kernel_name = 'hymba_conv_pool_nsa_step'


def rmsnorm(x, g):
    xf = x.astype(jnp.float32)
    y = xf * lax.rsqrt(jnp.mean(xf * xf, axis=-1, keepdims=True) + EPS)
    return (y * g.astype(jnp.float32)).astype(x.dtype)


def layernorm(x, g, b):
    xf = x.astype(jnp.float32)
    mu = jnp.mean(xf, axis=-1, keepdims=True)
    xc = xf - mu
    var = jnp.mean(xc * xc, axis=-1, keepdims=True)
    return (xc * lax.rsqrt(var + EPS) * g.astype(jnp.float32) + b.astype(jnp.float32)).astype(x.dtype)


def rope(x, pos):
    half = ROT_DIM // 2
    inv = ROPE_THETA ** (-jnp.arange(half, dtype=jnp.float32) * 2.0 / ROT_DIM)
    ang = pos.astype(jnp.float32)[:, None] * inv[None, :]
    shp = (pos.shape[0],) + (1,) * (x.ndim - 3) + (half,)
    cos = jnp.cos(ang).reshape(shp)
    sin = jnp.sin(ang).reshape(shp)
    x1 = x[..., :half].astype(jnp.float32)
    x2 = x[..., half:ROT_DIM].astype(jnp.float32)
    out = jnp.concatenate([x1 * cos - x2 * sin, x2 * cos + x1 * sin, x[..., ROT_DIM:].astype(jnp.float32)], axis=-1)
    return out.astype(x.dtype)


def masked_probs(s, mask):
    s = jnp.where(mask, s, NEG)
    m = jnp.max(s, axis=-1, keepdims=True)
    p = jnp.where(mask, jnp.exp(s - m), 0.0)
    return p / jnp.maximum(jnp.sum(p, axis=-1, keepdims=True), 1e-30)


def conv_mixer(u, buf, w_dw, b_dw, ln_g, ln_b, w_pw, b_pw):
    xcat = jnp.concatenate([buf.astype(u.dtype), u], axis=1)
    y = lax.conv_general_dilated(xcat, w_dw[:, None, :].astype(u.dtype), (1,), 'VALID',
                                 dimension_numbers=('NWC', 'WIO', 'NWC'), feature_group_count=u.shape[-1])
    y = layernorm(y + b_dw, ln_g, ln_b)
    y = jax.nn.silu(y) @ w_pw + b_pw
    return y.astype(u.dtype), xcat[:, -CONV_STATE:]


def pool_mixer(u, buf, pos, w_pool, pool_scale):
    N, T, _ = u.shape
    L0 = buf.shape[1]
    xcat = jnp.concatenate([buf.astype(u.dtype), u], axis=1)
    xf = xcat.astype(jnp.float32)
    cs = jnp.pad(jnp.cumsum(xf, axis=1), ((0, 0), (1, 0), (0, 0)))
    end = L0 + 1 + jnp.arange(T)
    diffs = []
    for gi, w in enumerate(POOL_WINDOWS):
        c0, c1 = gi * POOL_GROUP, (gi + 1) * POOL_GROUP
        total = cs[:, end, c0:c1] - cs[:, end - w, c0:c1]
        cnt = jnp.minimum(w, pos + 1).astype(jnp.float32)[None, :, None]
        diffs.append(total / cnt - xf[:, L0:, c0:c1])
    d = jnp.stack(diffs, axis=2)
    y = jnp.einsum('ntgc,gce->ntge', d, w_pool.astype(jnp.float32)).reshape(N, T, D_POOL)
    y = y * pool_scale.astype(jnp.float32)
    return y.astype(u.dtype), xcat[:, -POOL_STATE:]


def compress(k, pe, w1, w2):
    N, L, G, D = k.shape
    r = CMP_BLOCK // CMP_STRIDE
    n_pc = L // CMP_STRIDE
    n_blk = n_pc - r + 1
    pieces = k.reshape(N, n_pc, CMP_STRIDE, G, D)
    blocks = jnp.concatenate([pieces[:, j:j + n_blk] for j in range(r)], axis=2)
    blocks = blocks + pe[None, None, :, None, :]
    flat = jnp.swapaxes(blocks, 2, 3).reshape(N, n_blk, G, CMP_BLOCK * D)
    return jax.nn.gelu(flat @ w1) @ w2


def cmp_slc_chunk(q, pos, kc, vc, cmp_end, ksb, vsb, overlap):
    N, T, H, D = q.shape
    G = N_KV_HEADS
    R = H // G
    scale = HEAD_DIM ** -0.5
    qg = q.reshape(N, T, G, R, D)
    s = jnp.einsum('ntgrd,ncgd->ngrtc', qg, kc).astype(jnp.float32) * scale
    p_c = masked_probs(s, cmp_end[None, :] <= pos[:, None])
    o_c = jnp.einsum('ngrtc,ncgd->ntgrd', p_c.astype(vc.dtype), vc)
    imp = jnp.einsum('ngrtc,cs->ngts', p_c, overlap)
    NS = overlap.shape[1]
    j = jnp.arange(NS)[None, :]
    cur = (pos // SLC_BLOCK)[:, None]
    forced = (j == 0) | (j == cur) | (j == cur - 1)
    score = jnp.where(j <= cur, jnp.where(forced, BIG, imp), -BIG)
    _, idx = lax.top_k(score, min(N_SEL, NS))
    K = idx.shape[-1]
    n_i = jnp.arange(N)[:, None, None, None]
    g_i = jnp.arange(G)[None, :, None, None]
    kg = ksb[n_i, g_i, idx]
    vg = vsb[n_i, g_i, idx]
    kpos = idx[..., None] * SLC_BLOCK + jnp.arange(SLC_BLOCK)
    mask = (kpos <= pos[None, None, :, None, None]).reshape(N, G, 1, T, K * SLC_BLOCK)
    s2 = jnp.einsum('ntgrd,ngtksd->ngrtks', qg, kg).astype(jnp.float32).reshape(N, G, R, T, K * SLC_BLOCK) * scale
    p_s = masked_probs(s2, mask)
    o_s = jnp.einsum('ngrtm,ngtmd->ntgrd', p_s.astype(vg.dtype), vg.reshape(N, G, T, K * SLC_BLOCK, D))
    return o_c.reshape(N, T, H, D), o_s.reshape(N, T, H, D)


def sparse_branches(q, qpos, cmp_rows, slc_rows, pe, w1, w2, n_chunks):
    N, L = cmp_rows.shape[:2]
    Lp = -(-L // SLC_BLOCK) * SLC_BLOCK
    pad = ((0, 0), (0, Lp - L), (0, 0), (0, 0), (0, 0))
    cmp_rows = jnp.pad(cmp_rows, pad)
    slc_rows = jnp.pad(slc_rows, pad)
    kcb = compress(cmp_rows[:, :, 0], pe[0], w1[0], w2[0])
    vcb = compress(cmp_rows[:, :, 1], pe[1], w1[1], w2[1])
    n_cmp = kcb.shape[1]
    n_slc = Lp // SLC_BLOCK
    cmp_start = jnp.arange(n_cmp) * CMP_STRIDE
    cmp_end = cmp_start + CMP_BLOCK - 1
    slc_start = jnp.arange(n_slc) * SLC_BLOCK
    overlap = ((cmp_end[:, None] >= slc_start[None, :]) &
               (cmp_start[:, None] < slc_start[None, :] + SLC_BLOCK)).astype(jnp.float32)
    def to_blocks(a):
        return jnp.transpose(a.reshape(N, n_slc, SLC_BLOCK, N_KV_HEADS, HEAD_DIM), (0, 3, 1, 2, 4))
    ksb = to_blocks(slc_rows[:, :, 0])
    vsb = to_blocks(slc_rows[:, :, 1])
    def chunk(qc, pc):
        return cmp_slc_chunk(qc, pc, kcb, vcb, cmp_end, ksb, vsb, overlap)
    if n_chunks == 1:
        return chunk(q, qpos)
    T = q.shape[1]
    Tc = T // n_chunks
    qs = jnp.moveaxis(q.reshape(N, n_chunks, Tc, N_HEADS, HEAD_DIM), 1, 0)
    ps = qpos.reshape(n_chunks, Tc)
    oc, osl = lax.map(lambda a: chunk(a[0], a[1]), (qs, ps))
    oc = jnp.moveaxis(oc, 0, 1).reshape(N, T, N_HEADS, HEAD_DIM)
    osl = jnp.moveaxis(osl, 0, 1).reshape(N, T, N_HEADS, HEAD_DIM)
    return oc, osl


def window_attn_banded(q, k, v):
    N, S, H, D = q.shape
    G = N_KV_HEADS
    R = H // G
    nb = S // Q_BLOCK
    n_prev = -(-(WINDOW - 1) // Q_BLOCK)
    KW = (n_prev + 1) * Q_BLOCK
    def band(a):
        ap = jnp.pad(a, ((0, 0), (n_prev * Q_BLOCK, 0), (0, 0), (0, 0))).reshape(N, nb + n_prev, Q_BLOCK, G, D)
        return jnp.concatenate([ap[:, j:j + nb] for j in range(n_prev + 1)], axis=2)
    kb, vb = band(k), band(v)
    qb = q.reshape(N, nb, Q_BLOCK, G, R, D)
    qpos = jnp.arange(nb)[:, None] * Q_BLOCK + jnp.arange(Q_BLOCK)[None, :]
    kpos = (jnp.arange(nb)[:, None] - n_prev) * Q_BLOCK + jnp.arange(KW)[None, :]
    kp, qp = kpos[:, None, :], qpos[:, :, None]
    mask = (kp <= qp) & (kp > qp - WINDOW) & (kp >= 0)
    s = jnp.einsum('nbqgrd,nbkgd->nbgrqk', qb, kb).astype(jnp.float32) * (HEAD_DIM ** -0.5)
    p = masked_probs(s, mask[None, :, None, None])
    o = jnp.einsum('nbgrqk,nbkgd->nbqgrd', p.astype(vb.dtype), vb)
    return o.reshape(N, S, H, D)


def window_attn_dense(q, qpos, k, v, kpos):
    N, T, H, D = q.shape
    G = N_KV_HEADS
    R = H // G
    qg = q.reshape(N, T, G, R, D)
    s = jnp.einsum('ntgrd,nsgd->ngrts', qg, k).astype(jnp.float32) * (HEAD_DIM ** -0.5)
    mask = (kpos[None, :] <= qpos[:, None]) & (kpos[None, :] > qpos[:, None] - WINDOW)
    p = masked_probs(s, mask)
    o = jnp.einsum('ngrts,nsgd->ntgrd', p.astype(v.dtype), v)
    return o.reshape(N, T, H, D)


def mixer_inputs(x, pos, w_norm, w_in, g_q, g_k):
    N, T, _ = x.shape
    h = rmsnorm(x, w_norm)
    sizes = (D_CONV, D_CONV, D_CONV, D_POOL, D_POOL, D_ATTN, D_KV, N_BRANCH * N_HEADS, D_ATTN)
    offs = [int(v) for v in np.cumsum(sizes)[:-1]]
    a_val, a_gate, z_a, b_in, z_b, q, kv, gate, z_c = jnp.split(h @ w_in, offs, axis=-1)
    q = rope(rmsnorm(q.reshape(N, T, N_HEADS, HEAD_DIM), g_q), pos)
    kv = kv.reshape(N, T, N_BRANCH, 2, N_KV_HEADS, HEAD_DIM)
    k = rope(rmsnorm(kv[:, :, :, 0], g_k[:, None, :]), pos)
    rows = jnp.stack([k, kv[:, :, :, 1]], axis=3)
    glu = a_val * jax.nn.sigmoid(a_gate)
    gates = jax.nn.sigmoid(gate).reshape(N, T, N_BRANCH, N_HEADS)
    return glu, z_a, b_in, z_b, q, rows, gates, z_c


def mixer_output(x, y_a, z_a, y_b, z_b, o_c, o_s, o_w, gates, z_c, w_out):
    N, T, _ = x.shape
    o = gates[..., 0, :, None] * o_c + gates[..., 1, :, None] * o_s + gates[..., 2, :, None] * o_w
    y = jnp.concatenate([y_a * jax.nn.silu(z_a), y_b * jax.nn.silu(z_b),
                         o.reshape(N, T, D_ATTN) * jax.nn.silu(z_c)], axis=-1)
    return x + (y @ w_out).astype(x.dtype)


def setup_inputs(seed: int = 0) -> dict:
    key = jax.random.key(seed)
    ks = jax.random.split(key, 32)
    f = jnp.float32
    def nrm(k, shape, s=1.0):
        return jax.random.normal(k, shape, f) * s
    n_pages = PAST_LEN // PAGE_SIZE
    n_used = DEC_BATCH * n_pages
    n_phys = n_used + max(1, n_used // 4)
    win_len = min(WINDOW, PAST_LEN)
    page_table = jax.random.permutation(ks[7], n_phys)[:n_used].reshape(DEC_BATCH, n_pages).astype(jnp.int32)
    return {
        'x_prompt': nrm(ks[0], (BATCH, SEQ, D_MODEL)),
        'x_sample': nrm(ks[1], (DEC_BATCH, DEC_SEQ, D_MODEL)),
        'state_conv': nrm(ks[2], (DEPTH, DEC_BATCH, CONV_STATE, D_CONV), 0.5),
        'state_pool': nrm(ks[3], (DEPTH, DEC_BATCH, POOL_STATE, D_POOL)),
        'cache_win_kv': nrm(ks[4], (DEPTH, DEC_BATCH, win_len, 2, N_KV_HEADS, HEAD_DIM)),
        'cache_cmp_kv': nrm(ks[5], (DEPTH, n_phys, PAGE_SIZE, 2, N_KV_HEADS, HEAD_DIM)),
        'cache_slc_kv': nrm(ks[6], (DEPTH, n_phys, PAGE_SIZE, 2, N_KV_HEADS, HEAD_DIM)),
        'page_table': page_table,
        'w_norm': 1.0 + nrm(ks[8], (DEPTH, D_MODEL), 0.02),
        'w_in': nrm(ks[9], (DEPTH, D_MODEL, D_IN), D_MODEL ** -0.5),
        'w_out': nrm(ks[10], (DEPTH, D_MIX, D_MODEL), D_MIX ** -0.5),
        'w_dw': nrm(ks[11], (DEPTH, CONV_WIDTH, D_CONV), CONV_WIDTH ** -0.5),
        'b_dw': nrm(ks[12], (DEPTH, D_CONV), 0.01),
        'ln_g': 1.0 + nrm(ks[13], (DEPTH, D_CONV), 0.02),
        'ln_b': nrm(ks[14], (DEPTH, D_CONV), 0.01),
        'w_pw': nrm(ks[15], (DEPTH, D_CONV, D_CONV), D_CONV ** -0.5),
        'b_pw': nrm(ks[16], (DEPTH, D_CONV), 0.01),
        'w_pool': nrm(ks[17], (DEPTH, N_POOL_GROUPS, POOL_GROUP, POOL_GROUP), POOL_GROUP ** -0.5),
        'pool_scale': 1.0 + nrm(ks[18], (DEPTH, D_POOL), 0.1),
        'g_q': 1.0 + nrm(ks[19], (DEPTH, HEAD_DIM), 0.02),
        'g_k': 1.0 + nrm(ks[20], (DEPTH, N_BRANCH, HEAD_DIM), 0.02),
        'cmp_pe': nrm(ks[21], (DEPTH, 2, CMP_BLOCK, HEAD_DIM), 0.1),
        'cmp_w1': nrm(ks[22], (DEPTH, 2, CMP_BLOCK * HEAD_DIM, HEAD_DIM), (CMP_BLOCK * HEAD_DIM) ** -0.5),
        'cmp_w2': nrm(ks[23], (DEPTH, 2, HEAD_DIM, HEAD_DIM), HEAD_DIM ** -0.5),
    }


def reference(x_prompt, x_sample, state_conv, state_pool, cache_win_kv, cache_cmp_kv, cache_slc_kv, page_table,
              w_norm, w_in, w_out, w_dw, b_dw, ln_g, ln_b, w_pw, b_pw, w_pool, pool_scale,
              g_q, g_k, cmp_pe, cmp_w1, cmp_w2):
    Bp, S, _ = x_prompt.shape
    Bs, T, _ = x_sample.shape
    P = page_table.shape[1] * PAGE_SIZE
    pos_p = jnp.arange(S)
    pos_s = P + jnp.arange(T)
    xp, xs = x_prompt, x_sample
    conv_p, pool_p, win_p, cmp_p, slc_p = [], [], [], [], []
    conv_s, pool_s, win_s, cmp_s, slc_s = [], [], [], [], []
    for l in range(DEPTH):
        glu, z_a, b_in, z_b, q, rows, gates, z_c = mixer_inputs(xp, pos_p, w_norm[l], w_in[l], g_q[l], g_k[l])
        y_a, cv = conv_mixer(glu, jnp.zeros((Bp, CONV_STATE, D_CONV), glu.dtype),
                             w_dw[l], b_dw[l], ln_g[l], ln_b[l], w_pw[l], b_pw[l])
        y_b, pl = pool_mixer(b_in, jnp.zeros((Bp, POOL_STATE, D_POOL), b_in.dtype), pos_p, w_pool[l], pool_scale[l])
        cmp_rows, slc_rows, win_rows = rows[:, :, 0], rows[:, :, 1], rows[:, :, 2]
        o_c, o_s = sparse_branches(q, pos_p, cmp_rows, slc_rows, cmp_pe[l], cmp_w1[l], cmp_w2[l], S // Q_BLOCK)
        o_w = window_attn_banded(q, win_rows[:, :, 0], win_rows[:, :, 1])
        xp = mixer_output(xp, y_a, z_a, y_b, z_b, o_c, o_s, o_w, gates, z_c, w_out[l])
        conv_p.append(cv)
        pool_p.append(pl)
        win_p.append(win_rows[:, S - min(WINDOW, S):])
        cmp_p.append(cmp_rows)
        slc_p.append(slc_rows)
        glu, z_a, b_in, z_b, q, rows, gates, z_c = mixer_inputs(xs, pos_s, w_norm[l], w_in[l], g_q[l], g_k[l])
        y_a, cv = conv_mixer(glu, state_conv[l], w_dw[l], b_dw[l], ln_g[l], ln_b[l], w_pw[l], b_pw[l])
        y_b, pl = pool_mixer(b_in, state_pool[l], pos_s, w_pool[l], pool_scale[l])
        cmp_rows, slc_rows, win_rows = rows[:, :, 0], rows[:, :, 1], rows[:, :, 2]
        past_c = cache_cmp_kv[l][page_table].reshape(Bs, P, 2, N_KV_HEADS, HEAD_DIM)
        past_s = cache_slc_kv[l][page_table].reshape(Bs, P, 2, N_KV_HEADS, HEAD_DIM)
        full_c = jnp.concatenate([past_c.astype(cmp_rows.dtype), cmp_rows], axis=1)
        full_s = jnp.concatenate([past_s.astype(slc_rows.dtype), slc_rows], axis=1)
        o_c, o_s = sparse_branches(q, pos_s, full_c, full_s, cmp_pe[l], cmp_w1[l], cmp_w2[l], 1)
        buf = cache_win_kv[l].astype(win_rows.dtype)
        Lb = buf.shape[1]
        full_w = jnp.concatenate([buf, win_rows], axis=1)
        kpos = P - Lb + jnp.arange(Lb + T)
        o_w = window_attn_dense(q, pos_s, full_w[:, :, 0], full_w[:, :, 1], kpos)
        xs = mixer_output(xs, y_a, z_a, y_b, z_b, o_c, o_s, o_w, gates, z_c, w_out[l])
        conv_s.append(cv)
        pool_s.append(pl)
        win_s.append(full_w[:, T:])
        cmp_s.append(cmp_rows)
        slc_s.append(slc_rows)
    return (xp, xs,
            jnp.stack(conv_p), jnp.stack(pool_p), jnp.stack(win_p), jnp.stack(cmp_p), jnp.stack(slc_p),
            jnp.stack(conv_s), jnp.stack(pool_s), jnp.stack(win_s), jnp.stack(cmp_s), jnp.stack(slc_s))
```

```python
import numpy as np
from contextlib import ExitStack
import concourse.bass as bass
import concourse.mybir as mybir
from concourse.bass_utils import run_bass_kernel_spmd

F32 = mybir.dt.float32
BF16 = mybir.dt.bfloat16
I32 = mybir.dt.int32
AF = mybir.ActivationFunctionType
ALU = mybir.AluOpType
AX = mybir.AxisListType

NCORES = 8
T = 2048
DM = 1024
DIN = 3096
NT = T // 128
NSMP = 16
EPS = 1e-6
NEGV = -30000.0
import os as _os
STOP = _os.environ.get('KSTOP', '')


class StopBuild(Exception):
    pass


def stop_at(tag):
    if STOP == tag:
        raise StopBuild(tag)
SEM_EPOCH = 8000
N_DMA_SEMS = 12


class Res:
    def __init__(self, name, t=None):
        self.name = name
        self.t = t
        self.last_writer = None
        self.readers = {}
        self.excl = False

    def __getitem__(self, idx):
        return View(self.t[idx], self)

    def ap(self, dtype=None):
        a = self.t[:]
        if dtype is not None and dtype != F32:
            a = a.bitcast(dtype)
        return View(a, self)


class View:
    def __init__(self, ap, res):
        self.ap = ap
        self.res = res

    def __getitem__(self, idx):
        return View(self.ap[idx], self.res)

    def rr(self, s, **kw):
        return View(self.ap.rearrange(s, **kw), self.res)

    def bc(self, shape):
        return View(self.ap.to_broadcast(list(shape)), self.res)

    def us(self, axis):
        return View(self.ap.unsqueeze(axis), self.res)

    def bitcast(self, dt):
        return View(self.ap.bitcast(dt), self.res)


class Sched:
    ENG = ('pe', 'act', 'dve', 'pool', 'sp')

    def __init__(self, nc, es):
        self.nc = nc
        self.es = es
        self.streams = {e: [] for e in self.ENG}
        self.count = {e: 0 for e in self.ENG}
        self.waited = {e: {} for e in self.ENG}
        self.pending = {e: [] for e in self.ENG}
        self.sems = {}
        self.dma_sems = {e: [] for e in self.ENG}
        self.dma_rr = {e: 0 for e in self.ENG}
        self.out_events = []
        self.last_ev = {}
        self.nops = 0
        self.ps_rr = 0

    def sbuf(self, name, shape, dtype):
        t = self.es.enter_context(self.nc.sbuf_tensor(name, list(shape), dtype))
        return Res(name, t)

    def psum(self, name):
        t = self.es.enter_context(self.nc.psum_tensor(name, [128, 512], F32))
        r = Res(name, t)
        r.excl = True
        return r

    def _sem(self, key):
        if key not in self.sems:
            nm = "s_" + "_".join(str(k) for k in key)
            self.sems[key] = self.es.enter_context(self.nc.semaphore(nm))
        return self.sems[key]

    def _deps(self, reads, writes):
        deps = []
        for r in reads:
            if r.last_writer is not None:
                deps.append(r.last_writer)
            if r.excl:
                deps.extend(r.readers.items())
        for w in writes:
            if w.last_writer is not None:
                deps.append(w.last_writer)
            deps.extend(w.readers.items())
        return deps

    def _waits_for(self, eng, deps):
        need = {}
        wd = self.waited[eng]
        for k, v in deps:
            if wd.get(k, 0) >= v:
                continue
            if need.get(k, 0) < v:
                need[k] = v
        for k, v in need.items():
            wd[k] = v
        return list(need.items())

    def _commit(self, ev, reads, writes):
        k, v = ev
        for r in reads:
            if r.readers.get(k, 0) < v:
                r.readers[k] = v
        for w in writes:
            w.last_writer = ev
            w.readers = {}
        self.last_ev[k] = max(self.last_ev.get(k, 0), v)

    def op(self, eng, fn, reads=(), writes=()):
        deps = self._deps(reads, writes) + self.pending[eng]
        self.pending[eng] = []
        if eng == 'pe':
            deps = [d for d in deps if d[0][0] != 'pe']
        waits = self._waits_for(eng, deps)
        self.count[eng] += 1
        c = self.count[eng]
        key = (eng, (c - 1) // SEM_EPOCH)
        val = (c - 1) % SEM_EPOCH + 1
        self._sem(key)
        ev = (key, val)
        self.streams[eng].append((waits, fn, key, 1))
        self._commit(ev, reads, writes)
        self.nops += 1
        return ev

    def dma(self, q, out_ap, in_ap, reads=(), writes=(), out=False, fn=None):
        deps = self._deps(reads, writes) + self.pending[q]
        self.pending[q] = []
        pool = self.dma_sems[q]
        if len(pool) < N_DMA_SEMS:
            ent = [('d' + q, len(pool)), 0]
            pool.append(ent)
        else:
            ent = pool[self.dma_rr[q] % N_DMA_SEMS]
            self.dma_rr[q] += 1
            deps.append((ent[0], ent[1]))
        waits = self._waits_for(q, deps)
        ent[1] += 16
        key = ent[0]
        self._sem(key)
        ev = (key, ent[1])
        if fn is None:
            fn = lambda e: e.dma_start(out=out_ap, in_=in_ap)
        self.streams[q].append((waits, fn, key, 16))
        self._commit(ev, reads, writes)
        if out:
            self.out_events.append(ev)
        self.nops += 1
        return ev

    def barrier(self):
        evs = list(self.last_ev.items())
        for e in self.ENG:
            self.pending[e] = list(evs)

    def finish(self):
        nc = self.nc
        fin = self._waits_for('sp', self.out_events)
        streams = self.streams
        sems = self.sems

        def emit(engobj, name, final_waits=()):
            for waits, fn, key, inc in streams[name]:
                for k, v in waits:
                    engobj.wait_ge(sems[k], v)
                fn(engobj).then_inc(sems[key], inc)
            for k, v in final_waits:
                engobj.wait_ge(sems[k], v)

        with nc.Block() as block:
            @block.sync
            def _(e):
                emit(e, 'sp', fin)

            @block.tensor
            def _(e):
                emit(e, 'pe')

            @block.scalar
            def _(e):
                emit(e, 'act')

            @block.vector
            def _(e):
                emit(e, 'dve')

            @block.gpsimd
            def _(e):
                emit(e, 'pool')


def _rs(views):
    out = []
    for v in views:
        if v is None or isinstance(v, (int, float)):
            continue
        r = v.res if isinstance(v, View) else v
        if r not in out:
            out.append(r)
    return out


class K:
    def __init__(self, S):
        self.S = S

    def tt(self, eng, out, a, b, op):
        self.S.op(eng, lambda e: e.tensor_tensor(out.ap, a.ap, b.ap, op), reads=_rs([a, b]), writes=_rs([out]))

    def ts(self, eng, out, a, s1, s2, op0, op1=None):
        s1a = s1.ap if isinstance(s1, View) else s1
        s2a = s2.ap if isinstance(s2, View) else s2
        if op1 is None:
            self.S.op(eng, lambda e: e.tensor_scalar(out.ap, a.ap, s1a, s2a, op0), reads=_rs([a, s1, s2]), writes=_rs([out]))
        else:
            self.S.op(eng, lambda e: e.tensor_scalar(out.ap, a.ap, s1a, s2a, op0, op1), reads=_rs([a, s1, s2]), writes=_rs([out]))

    def stt(self, eng, out, a, s, b, op0, op1):
        sa = s.ap if isinstance(s, View) else s
        self.S.op(eng, lambda e: e.scalar_tensor_tensor(out.ap, a.ap, sa, b.ap, op0, op1), reads=_rs([a, s, b]), writes=_rs([out]))

    def copy(self, eng, out, a):
        if eng == 'act':
            self.S.op(eng, lambda e: e.copy(out.ap, a.ap), reads=_rs([a]), writes=_rs([out]))
        else:
            self.S.op(eng, lambda e: e.tensor_copy(out.ap, a.ap), reads=_rs([a]), writes=_rs([out]))

    def memset(self, eng, out, val):
        self.S.op(eng, lambda e: e.memset(out.ap, val), writes=_rs([out]))

    def act(self, out, a, func, bias=None, scale=None, accum=None):
        kw = {}
        if bias is not None:
            kw['bias'] = bias.ap if isinstance(bias, View) else bias
        if scale is not None:
            kw['scale'] = scale.ap if isinstance(scale, View) else scale
        if accum is not None:
            kw['accum_out'] = accum.ap
        self.S.op('act', lambda e: e.activation(out.ap, a.ap, func, **kw), reads=_rs([a, bias, scale]), writes=_rs([out, accum]))

    def red(self, eng, out, a, op=ALU.add):
        self.S.op(eng, lambda e: e.tensor_reduce(out.ap, a.ap, AX.X, op), reads=_rs([a]), writes=_rs([out]))

    def recip(self, out, a):
        self.S.op('dve', lambda e: e.reciprocal(out.ap, a.ap), reads=_rs([a]), writes=_rs([out]))

    def max8(self, out, a):
        self.S.op('dve', lambda e: e.max(out=out.ap, in_=a.ap), reads=_rs([a]), writes=_rs([out]))

    def mm(self, out, lhsT, rhs, start, stop):
        self.S.op('pe', lambda e: e.matmul(out.ap, lhsT.ap, rhs.ap, start=start, stop=stop), reads=_rs([lhsT, rhs]), writes=_rs([out]))

    def tr(self, out, a, ident):
        self.S.op('pe', lambda e: e.transpose(out.ap, a.ap, ident.ap), reads=_rs([a, ident]), writes=_rs([out]))

    def dma(self, q, out, a, outp=False):
        self.S.dma(q, out.ap, a.ap, reads=_rs([a]), writes=_rs([out]), out=outp)

    def rsqrt(self, out, a, scale, eps):
        self.act(out, a, AF.Sqrt, bias=eps, scale=scale)
        self.recip(out, out)


def _consts():
    c = {}
    half = 8
    inv = 500000.0 ** (-np.arange(half, dtype=np.float32) * 2.0 / 16.0)
    pos = np.arange(T + 1, dtype=np.float32)
    ang = pos[:, None] * inv[None, :].astype(np.float32)
    cs = np.concatenate([np.cos(ang), np.sin(ang)], axis=1).astype(np.float32)
    csp = cs[:T].reshape(NT, 128, 16).transpose(1, 0, 2)
    c['cs_p'] = np.ascontiguousarray(csp)
    c['cs_s'] = np.ascontiguousarray(np.broadcast_to(cs[T][None, :], (NSMP, 16)))
    p = np.arange(128)[:, None]
    i = np.arange(128)[None, :]
    tri = np.where(p <= i, 0.0, NEGV).astype(np.float32)
    triu = np.where(p > i, 0.0, NEGV).astype(np.float32)
    c['tri4'] = np.ascontiguousarray(np.tile(tri, (1, 4)))
    c['triu4'] = np.ascontiguousarray(np.tile(triu, (1, 4)))
    cm = np.zeros((128, 128), np.float32)
    for cp in range(8):
        cm[cp] = np.where(np.arange(128) >= 16 * cp + 15, 0.0, NEGV)
    c['cmrel4'] = np.ascontiguousarray(np.tile(cm, (1, 4)))
    sh = np.zeros((128, NT, 127), np.float32)
    for qt in range(NT):
        for cp in range(8):
            cc = 8 * qt - 1 + cp
            if 0 <= cc < 127:
                sh[cp, qt, cc] = 1.0
    c['shift'] = sh
    nf = np.zeros((128, NT, 32), np.float32)
    addc = np.zeros((128, NT, 32), np.float32)
    for qt in range(NT):
        for ii in range(128):
            cur = (qt * 128 + ii) // 64
            for j in range(32):
                if j > cur:
                    addc[ii, qt, j] = -1e4
                elif j == 0 or j == cur or j == cur - 1:
                    addc[ii, qt, j] = 1e4
                else:
                    nf[ii, qt, j] = 1.0
    c['nf'] = nf
    c['addc'] = addc
    ec = np.zeros((32, T), np.float32)
    for j in range(32):
        ec[j, j * 64:(j + 1) * 64] = 1.0
    c['ec'] = ec
    def overlap(n_cmp, n_slc):
        cs_ = np.arange(n_cmp) * 16
        ce = cs_ + 31
        ss = np.arange(n_slc) * 64
        return ((ce[:, None] >= ss[None, :]) & (cs_[:, None] < ss[None, :] + 64)).astype(np.float32)
    c['ov_p'] = overlap(127, 32)
    c['ov_s'] = overlap(127, 33)
    invc = np.zeros((128, 2, 16), np.float32)
    for ch in range(2):
        for pp in range(128):
            w = (2, 4, 8, 16)[ch * 2 + pp // 64]
            invc[pp, ch] = 1.0 / np.minimum(w, np.arange(16) + 1)
    c['invc'] = invc
    grp = np.zeros((128, 32), np.float32)
    for n in range(16):
        for h in range(8):
            grp[n * 8 + h, n * 2 + h // 4] = 1.0
    c['grp'] = grp
    nfs = np.ones((32, 33), np.float32)
    adds = np.zeros((32, 33), np.float32)
    for j in (0, 31, 32):
        nfs[:, j] = 0.0
        adds[:, j] = 1e4
    c['nfs'] = nfs
    c['adds'] = adds
    e33 = np.zeros((33, 128), np.float32)
    for pc in range(128):
        e33[pc // 4, pc] = 1.0
    c['e33'] = e33
    sel8 = np.zeros((16, 128), np.float32)
    for pc in range(128):
        sel8[pc // 8, pc] = 1.0
    c['sel8'] = sel8
    c['pmod8'] = (np.arange(128) % 8).astype(np.float32).reshape(128, 1)
    return c


CONST_SHAPES = None


def _layout_params(inp):
    d = {}
    d['wnorm'] = np.ascontiguousarray(inp['w_norm'].reshape(2, 8, 128).transpose(0, 2, 1))
    d['wdw'] = np.ascontiguousarray(inp['w_dw'].reshape(2, 31, 2, 128).transpose(0, 3, 2, 1))
    def pc(a):
        return a.reshape(2, 2, 128).transpose(0, 2, 1)
    d['cvec'] = np.ascontiguousarray(np.stack([pc(inp['b_dw']), pc(inp['ln_g']), pc(inp['ln_b']),
                                               pc(inp['b_pw']), pc(inp['pool_scale'])], axis=-1))
    wp = inp['w_pool']
    bd = np.zeros((2, 2, 128, 128), np.float32)
    for g in range(4):
        ch, o = g // 2, (g % 2) * 64
        bd[:, ch, o:o + 64, o:o + 64] = wp[:, g]
    d['wpool_bd'] = np.ascontiguousarray(bd.transpose(0, 2, 1, 3))
    d['pe_t'] = np.ascontiguousarray(inp['cmp_pe'].transpose(0, 3, 1, 2))
    w1 = inp['cmp_w1'].reshape(2, 2, 32, 64, 64)
    d['w1h'] = np.ascontiguousarray(w1.transpose(0, 3, 1, 2, 4))
    w1p = inp['cmp_w1'].reshape(2, 2, 16, 128, 64)
    d['w1p'] = np.ascontiguousarray(w1p.transpose(0, 3, 1, 2, 4))
    d['w2'] = np.ascontiguousarray(inp['cmp_w2'].transpose(0, 2, 1, 3))
    d['rows_s'] = np.ascontiguousarray(np.stack([inp['b_dw'], inp['ln_g'], inp['ln_b'], inp['b_pw'],
                                                 inp['pool_scale']], axis=1))
    return d


def build_nc(n_phys, do_prompt=True, do_sample=True, depth=2):
    nc = bass.Bass("TRN2", target_bir_lowering=False)
    consts = _consts()
    D = {}

    def din(name, shape, dt=F32):
        D[name] = nc.dram_tensor(name, list(shape), dt, kind="ExternalInput").ap()
        return D[name]

    def dout(name, shape):
        D[name] = nc.dram_tensor(name, list(shape), F32, kind="ExternalOutput").ap()
        return D[name]

    def dscr(name, shape, dt=F32):
        D[name] = nc.dram_tensor(name, list(shape), dt, kind="Internal").ap()
        return D[name]

    din('xp', [T, DM]); din('xs', [NSMP, DM])
    din('w_in', [2, DM, DIN]); din('w_out', [2, DM, DM])
    din('wnorm', [2, 128, 8]); din('wdw', [2, 128, 2, 31]); din('cvec', [2, 128, 2, 5])
    din('w_pw', [2, 256, 256]); din('wpool_bd', [2, 128, 2, 128])
    din('g_q', [2, 64]); din('g_k', [2, 3, 64])
    din('pe_t', [2, 64, 2, 32]); din('w1h', [2, 64, 2, 32, 64]); din('w1p', [2, 128, 2, 16, 64]); din('w2', [2, 64, 2, 64])
    din('rows_s', [2, 5, 256]); din('w_dw', [2, 31, 256])
    din('state_conv', [2, NSMP, 30, 256]); din('state_pool', [2, NSMP, 15, 256])
    din('cache_win', [2, NSMP, 256, 256])
    din('cache_cmp', [2 * n_phys * 8, 4096]); din('cache_slc', [2 * n_phys * 8, 4096])
    din('ptab_t', [16, NSMP], I32)
    for k, v in consts.items():
        din('c_' + k, v.shape)
    dout('y_p', [T, DM]); dout('y_s', [NSMP, DM])
    dout('conv_p', [2, 30, 256]); dout('pool_p', [2, 15, 256]); dout('win_p', [2, 256, 256])
    dout('cmp_p', [2, T, 256]); dout('slc_p', [2, T, 256])
    dout('conv_s', [2, NSMP, 30, 256]); dout('pool_s', [2, NSMP, 15, 256]); dout('win_s', [2, NSMP, 256, 256])
    dout('cmp_s', [2, NSMP, 256]); dout('slc_s', [2, NSMP, 256])
    dscr('x1', [T, DM]); dscr('xs1', [NSMP, DM]); dscr('pjs_d', [NSMP, DIN]); dscr('nh_d', [NSMP, 8 * 387]); dscr('mixnh_d', [128, 64])
    dscr('qb_d', [T, 512], BF16); dscr('szc_d', [128, 4, T], BF16); dscr('mix_d', [128, 4, T], BF16)

    with ExitStack() as es:
        S = Sched(nc, es)
        k = K(S)
        RDICT = {}
        def RD(name, idx):
            if (name, idx) not in RDICT:
                RDICT[(name, idx)] = Res('%s_%s' % (name, idx))
            return RDICT[(name, idx)]
        def DV(name, res=None):
            return View(D[name], res if res is not None else Res('dram_' + name))
        PS = [S.psum("ps%d" % i) for i in range(8)]

        def ps_next():
            b = PS[S.ps_rr % 8]
            S.ps_rr += 1
            return b

        try:
          identf = S.sbuf("identf", [128, 128], F32)
          identb = S.sbuf("identb", [128, 128], BF16)
          onesb = S.sbuf("onesb", [128, 128], BF16)
          k.memset('pool', identf[:], 0.0)
          S.op('pool', lambda e: e.affine_select(identf.t[:], identf.t[:], pattern=[[-1, 128]], compare_op=ALU.not_equal,
                                                 fill=1.0, base=0, channel_multiplier=1), reads=[identf], writes=[identf])
          k.copy('dve', identb[:], identf[:])
          k.memset('pool', onesb[:], 1.0)
          stop_at('s1')
          stage = [S.sbuf("stage%d" % i, [128, 1024], F32) for i in range(2)]
          st_rr = [0]

          def load_const_bf(name, shape_free):
              src = D['c_' + name]
              npart = src.shape[0]
              t = S.sbuf("k_" + name, [128] + list(shape_free), BF16)
              nfree = int(np.prod(shape_free))
              sb = stage[st_rr[0] % 2]; st_rr[0] += 1
              flat_src = src if len(src.shape) == 2 else src.rearrange("p a b -> p (a b)")
              k.dma('sp', sb[0:npart, 0:nfree], View(flat_src, Res('c')))
              dst = t[0:npart]
              if len(shape_free) == 2:
                  dst = dst.rr("p a b -> p (a b)")
              k.copy('dve', dst, sb[0:npart, 0:nfree])
              return t

          def load_const_f(name, shape_free, npart=128):
              src = D['c_' + name]
              t = S.sbuf("k_" + name, [128] + list(shape_free), F32)
              k.dma('sp', t[0:npart], View(src, Res('c')))
              return t

          if do_prompt:
              CS = load_const_f('cs_p', [NT, 16])
              TRI4 = load_const_bf('tri4', [512])
              TRIU4 = load_const_bf('triu4', [512])
              CMREL4 = load_const_bf('cmrel4', [512])
              SHIFT = S.sbuf("k_shift", [128, NT, 127], BF16)
              for h0 in range(0, NT, 8):
                  sb = stage[st_rr[0] % 2]; st_rr[0] += 1
                  k.dma('sp', sb[:, 0:8 * 127], View(D['c_shift'][:, h0:h0 + 8, :].rearrange("p a b -> p (a b)"), Res('c')))
                  k.copy('dve', SHIFT[:, h0:h0 + 8, :].rr("p a b -> p (a b)"), sb[:, 0:8 * 127])
              NF = load_const_f('nf', [NT, 32])
              ADDC = load_const_f('addc', [NT, 32])
              INVC = load_const_f('invc', [2, 16])
              stop_at('s2')
              KT2 = S.sbuf("KT2", [128, 3, T], BF16)
              KA = S.sbuf("KA", [128, 2, T], BF16)
              VX = S.sbuf("VX", [128, NT, 2, 2, 65], BF16)
              GT = S.sbuf("GT", [128, NT, 24], F32)
              VCX = S.sbuf("VCX", [128, 2, 97], BF16)
              KC2 = S.sbuf("KC2", [128, 127], BF16)
              k.memset('pool', KA[:], 0.0)
              k.memset('pool', VX[:], 1.0)
              k.memset('pool', VCX[:], 1.0)
              for h0 in range(0, T, 1024):
                  sb = stage[st_rr[0] % 2]; st_rr[0] += 1
                  k.dma('sp', sb[64:96, 0:1024], View(D['c_ec'][:, h0:h0 + 1024], Res('c')))
                  k.dma('sp', sb[0:32, 0:1024], View(D['c_ec'][:, h0:h0 + 1024], Res('c')))
                  k.copy('dve', KA[64:96, 0, h0:h0 + 1024], sb[64:96, 0:1024])
                  k.copy('dve', KA[0:32, 1, h0:h0 + 1024], sb[0:32, 0:1024])
              sb = stage[st_rr[0] % 2]; st_rr[0] += 1
              k.dma('sp', sb[0:127, 0:32], View(D['c_ov_p'], Res('c')))
              for g in range(2):
                  k.copy('dve', VCX[0:127, g, 65:97], sb[0:127, 0:32])
              QZ = [S.sbuf("QZ%d" % g, [128, 4, 128], BF16) for g in range(2)]
              QA = [S.sbuf("QA%d" % g, [128, 4, 128], BF16) for g in range(2)]
              for g in range(2):
                  k.memset('pool', QZ[g][:], 0.0)
                  k.memset('pool', QA[g][:], 0.0)

          stop_at('s3')
          ARENA_F32 = 37000
          arena = S.sbuf("arena", [128, ARENA_F32], F32)
          ar_off = [0]

          def ar_reset():
              ar_off[0] = 0

          def ar(name, shape, dt):
              nel = int(np.prod(shape))
              nwords = (nel * (4 if dt == F32 or dt == I32 else 2) + 3) // 4
              nwords = (nwords + 7) // 8 * 8
              a = arena.t[:, ar_off[0]:ar_off[0] + nwords]
              ar_off[0] += nwords
              assert ar_off[0] <= ARENA_F32, (name, ar_off[0])
              if dt != F32:
                  a = a.bitcast(dt)
              a = a[:, 0:nel]
              if len(shape) == 2:
                  a = a.rearrange("p (a b) -> p a b", a=shape[0])
              elif len(shape) == 3:
                  a = a.rearrange("p (a b c) -> p a b c", a=shape[0], b=shape[1])
              elif len(shape) == 4:
                  a = a.rearrange("p (a b c d) -> p a b c d", a=shape[0], b=shape[1], c=shape[2])
              return Res(name, a)

          for l in range(depth):
              def xsrc_t(ti, l=l):
                  rows = slice(ti * 128, (ti + 1) * 128)
                  return View(D['xp'][rows, :], Res('c')) if l == 0 else View(D['x1'][rows, :], RD('x1', ti))
              def xdst_t(ti, l=l):
                  rows = slice(ti * 128, (ti + 1) * 128)
                  return View(D['x1'][rows, :], RD('x1', ti)) if (l == 0 and depth > 1) else View(D['y_p'][rows, :], Res('c'))
              S.barrier(); ar_reset()
              WINB = ar("winb", [8, DIN], BF16)
              wn = ar("wn", [8], F32)
              k.dma('sp', wn[:], DV('wnorm')[l])
              cvec = ar("cvec", [2, 5], F32)
              k.dma('sp', cvec[:], DV('cvec')[l])
              wdw = ar("wdw", [2, 31], F32)
              k.dma('sp', wdw[:], DV('wdw')[l])
              gqb = ar("gqb", [64], F32)
              k.dma('sp', gqb[:], View(D['g_q'][l].partition_broadcast(128), Res('c')))
              gkb = ar("gkb", [3, 64], F32)
              k.dma('sp', gkb[:].rr("p a b -> p (a b)"), View(D['g_k'][l].rearrange("a b -> (a b)").partition_broadcast(128), Res('c')))
              stop_at('s4')
              ci = 0
              for kc in range(8):
                  for c0 in range(0, DIN, 1024):
                      cw = min(1024, DIN - c0)
                      if ci >= int(_os.environ.get('WMAX', '1000')):
                          continue
                      sb = stage[st_rr[0] % 2]; st_rr[0] += 1
                      k.dma('sp', sb[:, 0:cw], DV('w_in')[l, kc * 128:(kc + 1) * 128, c0:c0 + cw])
                      wvar = _os.environ.get('WVAR', 'mix')
                      if wvar == 'none':
                          pass
                      elif wvar == 'act' or (wvar == 'mix' and ci % 2 == 0):
                          k.act(WINB[:, kc, c0:c0 + cw], sb[:, 0:cw], AF.Identity, scale=wn[:, kc:kc + 1])
                      else:
                          k.ts('dve', WINB[:, kc, c0:c0 + cw], sb[:, 0:cw], wn[:, kc:kc + 1], None, ALU.mult)
                      ci += 1
              stop_at('w')
              mark = ar_off[0]
              if do_sample:
                  sample_s0(nc, S, k, D, DV, l, locals())
                  S.barrier(); ar_off[0] = mark
              stop_at('s0')
              if do_prompt:
                  prompt_phase_a(nc, S, k, D, DV, l, locals())
              S.barrier(); ar_reset()
              WOUTB = ar("woutb", [8, DM], BF16)
              mark_w = ar_off[0]
              for kc in range(8):
                  sb = stage[st_rr[0] % 2]; st_rr[0] += 1
                  k.dma('sp', sb[:], DV('w_out')[l, kc * 128:(kc + 1) * 128, :])
                  if kc % 2 == 0:
                      k.copy('act', WOUTB[:, kc, :], sb[:])
                  else:
                      k.copy('pool', WOUTB[:, kc, :], sb[:])
              if do_prompt:
                  prompt_phase_c(nc, S, k, D, DV, l, locals())
              stop_at('pc')
              if do_sample:
                  S.barrier(); ar_off[0] = mark_w
                  sample_phase_s(nc, S, k, D, DV, l, locals())
              stop_at('L0end')
        except StopBuild as ex:
            print('build stopped at', ex)
        S.finish()
    return nc


def prompt_phase_a(nc, S, k, D, DV, l, E):
    ar = E['ar']; WINB = E['WINB']; PS = E['PS']; ps_next = E['ps_next']
    identb = E['identb']; identf = E['identf']; onesb = E['onesb']
    cvec = E['cvec']; wdw = E['wdw']; gqb = E['gqb']; gkb = E['gkb']
    CS = E['CS']; KT2 = E['KT2']; KA = E['KA']; VX = E['VX']; GT = E['GT']; INVC = E['INVC']
    VCX = E['VCX']; KC2 = E['KC2']
    xsrc_t = E['xsrc_t']; stage = E['stage']; st_rr = E['st_rr']; RD = E['RD']
    BT = 4
    BW = BT * 128
    NB = NT // BT
    wpw = ar("wpw", [2, 256], BF16)
    wpl = ar("wpl", [2, 128], BF16)
    for kc in range(2):
        sb = stage[st_rr[0] % 2]; st_rr[0] += 1
        k.dma('sp', sb[:, 0:256], DV('w_pw')[l, kc * 128:(kc + 1) * 128, :])
        k.copy('dve', wpw[:, kc, :], sb[:, 0:256])
    sb = stage[st_rr[0] % 2]; st_rr[0] += 1
    k.dma('sp', sb[:, 0:256], View(D['wpool_bd'][l].rearrange("p a b -> p (a b)"), Res('c')))
    k.copy('dve', wpl[:].rr("p a b -> p (a b)"), sb[:, 0:256])
    XT = [ar("XT%d" % i, [DM], F32) for i in range(2)]
    junk = ar("junk", [DM], BF16)
    hb = ar("hb", [DM], BF16)
    ssq = ar("ssq", [8], F32)
    HT = ar("HT", [8, BW], BF16)
    PJ = ar("PJ", [1304], F32)
    SQ2 = ar("SQ2", [1280], F32)
    SS20 = ar("SS20", [20], F32)
    RT = ar("RT", [6, 8, 8], F32)
    KB = ar("KB", [4, 128], BF16)
    QBt = ar("QBt", [512], BF16)
    AV = ar("AV", [2, BW], F32)
    SG = ar("SG", [2, BW], F32)
    GLUB = ar("GLUB", [2, 32 + BW], BF16)
    BIN = ar("BIN", [2, 16 + BW], F32)
    SZA = ar("SZA", [2, BW], BF16)
    SZB = ar("SZB", [2, BW], BF16)
    SZC = ar("SZCb", [4, BW], BF16)
    MIXB = ar("MIXB", [4, BW], BF16)
    DG = ar("DG", [31, 128], BF16)
    CV = ar("CV", [2, BW], F32)
    CVB = ar("CVB", [2, BW], BF16)
    SQB = ar("SQB", [2, BW], BF16)
    MEAN = ar("MEAN", [BW], F32)
    VAR = ar("VAR", [BW], F32)
    ACTV = ar("ACTV", [2, BW], BF16)
    SA = ar("SA", [16 + BW], F32)
    SBb = ar("SBb", [16 + BW], F32)
    DB = ar("DB", [2, BW], BF16)
    TOUT = ar("TOUT", [256], F32)
    k.memset('pool', GLUB[:], 0.0)
    k.memset('pool', BIN[:], 0.0)

    for tb in range(NB):
        for tt in range(BT):
            ti = tb * BT + tt
            xt = XT[ti % 2]
            k.dma('sp', xt[:], xsrc_t(ti))
            stop_at('t0')
            k.memset('pool', ssq[:, 0:1], 0.0)
            k.act(junk[:], xt[:], AF.Square, accum=ssq[:, 0:1])
            k.rsqrt(ssq[:, 1:2], ssq[:, 0:1], 1.0 / DM, EPS)
            k.ts('dve', hb[:], xt[:], ssq[:, 1:2], None, ALU.mult)
            stop_at('t1')
            pst = ps_next()
            for kc in range(8):
                k.tr(pst.ap(BF16)[:, kc * 128:(kc + 1) * 128], hb[:, kc * 128:(kc + 1) * 128], identb[:])
            k.copy('act', HT[:, :, tt * 128:(tt + 1) * 128], pst.ap(BF16)[:, 0:1024].rr("p (a b) -> p a b", a=8))
            stop_at('t2')
            for (c0, cw, o0) in ((1280, 512, 0), (1792, 512, 512), (2304, 280, 1024)):
                pp = ps_next()
                for kc in range(8):
                    k.mm(pp.ap()[:, 0:cw], HT[:, kc, tt * 128:(tt + 1) * 128], WINB[:, kc, c0:c0 + cw], kc == 0, kc == 7)
                k.copy('act' if o0 != 512 else 'dve', PJ[:, o0:o0 + cw], pp.ap()[:, 0:cw])
            stop_at('t3')
            k.tt('pool', SQ2[:], PJ[:, 0:1280], PJ[:, 0:1280], ALU.mult)
            k.red('dve', SS20[:], SQ2[:].rr("p (h d) -> p h d", d=64))
            k.rsqrt(SS20[:], SS20[:], 1.0 / 64, EPS)
            Q3 = PJ[:, 0:512].rr("p (h d) -> p h d", d=64)
            k.tt('dve', Q3, Q3, SS20[:, 0:8].us(2).bc([128, 8, 64]), ALU.mult)
            k.tt('dve', Q3, Q3, gqb[:].us(1).bc([128, 8, 64]), ALU.mult)
            K3 = []
            for b in range(3):
                kk = PJ[:, 512 + b * 256:512 + b * 256 + 128].rr("p (g d) -> p g d", d=64)
                k.tt('dve', kk, kk, SS20[:, 8 + b * 4:8 + b * 4 + 2].us(2).bc([128, 2, 64]), ALU.mult)
                k.tt('dve', kk, kk, gkb[:, b, :].us(1).bc([128, 2, 64]), ALU.mult)
                K3.append(kk)
            stop_at('t4')
            cos8 = CS[:, ti, 0:8]
            sin8 = CS[:, ti, 8:16]
            def rope(X, nh, eng):
                x1 = X[:, :, 0:8]; x2 = X[:, :, 8:16]
                cb = cos8.us(1).bc([128, nh, 8]); sbv = sin8.us(1).bc([128, nh, 8])
                k.tt(eng, RT[:, 0, 0:nh, :], x1, cb, ALU.mult)
                k.tt(eng, RT[:, 1, 0:nh, :], x2, sbv, ALU.mult)
                k.tt(eng, RT[:, 2, 0:nh, :], x2, cb, ALU.mult)
                k.tt(eng, RT[:, 3, 0:nh, :], x1, sbv, ALU.mult)
                k.tt(eng, x1, RT[:, 0, 0:nh, :], RT[:, 1, 0:nh, :], ALU.subtract)
                k.tt(eng, x2, RT[:, 2, 0:nh, :], RT[:, 3, 0:nh, :], ALU.add)
            rope(Q3, 8, 'dve')
            for b in range(3):
                rope(K3[b], 2, 'pool')
            stop_at('t5')
            k.dma('sp', DV('cmp_p')[l, ti * 128:(ti + 1) * 128, :], PJ[:, 512:768], outp=True)
            k.dma('sp', DV('slc_p')[l, ti * 128:(ti + 1) * 128, :], PJ[:, 768:1024], outp=True)
            if ti >= NT - 2:
                k.dma('sp', DV('win_p')[l, (ti - NT + 2) * 128:(ti - NT + 3) * 128, :], PJ[:, 1024:1280], outp=True)
            k.copy('pool', QBt[:].rr("p (r g d) -> p r g d", r=4, g=2), PJ[:, 0:512].rr("p (g r d) -> p r g d", g=2, r=4))
            k.dma('sp', View(D['qb_d'][ti * 128:(ti + 1) * 128, :], RD('qb', ti)), QBt[:])
            k.act(GT[:, ti, :], PJ[:, 1280:1304], AF.Sigmoid)
            stop_at('t6')
            k.copy('pool', KB[:, 0:3, :], PJ[:, 512:1280].rr("p (b x) -> p b x", b=3)[:, :, 0:128])
            k.copy('pool', KB[:, 3, :], PJ[:, 640:768])
            stop_at('t6a')
            pk = ps_next()
            for j in range(4):
                k.tr(pk.ap(BF16)[:, j * 128:(j + 1) * 128], KB[:, j, :], identb[:])
            stop_at('t6b')
            pkv = pk.ap(BF16)
            cols = slice(ti * 128, (ti + 1) * 128)
            k.copy('act', KT2[:, 0, cols], pkv[:, 0:128])
            k.copy('act', KT2[:, 1, cols], pkv[:, 256:384])
            k.copy('act', KT2[:, 2, cols], pkv[:, 384:512])
            stop_at('t6c')
            k.copy('dve', KA[0:64, 0, cols], pkv[0:64, 128:256])
            stop_at('t6d')
            k.copy('dve', KA[64:128, 1, cols], pkv[64:128, 128:256])
            stop_at('t7')
            k.copy('pool', VX[:, ti, :, :, 0:64],
                   PJ[:, 768:1280].rr("p (b x) -> p b x", b=2)[:, :, 128:256].rr("p b (g d) -> p b g d", g=2))
        stop_at('a1')
        t0 = tb * BW
        def fm_proj(c0):
            pp = ps_next()
            for kc in range(8):
                k.mm(pp.ap()[:, 0:BW], WINB[:, kc, c0:c0 + 128], HT[:, kc, :], kc == 0, kc == 7)
            return pp
        for c in range(2):
            pp = fm_proj(c * 128)
            k.copy('dve', AV[:, c, :], pp.ap()[:, 0:BW])
            pp = fm_proj(256 + c * 128)
            k.act(SG[:, c, :], pp.ap()[:, 0:BW], AF.Sigmoid)
            k.tt('pool', AV[:, c, :], AV[:, c, :], SG[:, c, :], ALU.mult)
            k.copy('pool', GLUB[:, c, 32:32 + BW], AV[:, c, :])
            pp = fm_proj(512 + c * 128)
            k.act(SZA[:, c, :], pp.ap()[:, 0:BW], AF.Silu)
            pp = fm_proj(768 + c * 128)
            k.copy('dve', BIN[:, c, 16:16 + BW], pp.ap()[:, 0:BW])
            pp = fm_proj(1024 + c * 128)
            k.act(SZB[:, c, :], pp.ap()[:, 0:BW], AF.Silu)
        for c in range(4):
            pp = fm_proj(2584 + c * 128)
            k.act(SZC[:, c, :], pp.ap()[:, 0:BW], AF.Silu)
        k.dma('sp', View(D['szc_d'][:, :, t0:t0 + BW], RD('szc', tb)), SZC[:])
        if tb == NB - 1:
            pp = ps_next()
            for c in range(2):
                k.tr(pp.ap()[:, c * 128:(c + 1) * 128], AV[:, c, BW - 128:BW], identf[:])
            k.copy('dve', TOUT[:], pp.ap()[:, 0:256])
            k.dma('sp', DV('conv_p')[l], TOUT[98:128, :], outp=True)
            pp = ps_next()
            for c in range(2):
                k.tr(pp.ap()[:, c * 128:(c + 1) * 128], BIN[:, c, 16 + BW - 128:16 + BW], identf[:])
            k.copy('dve', TOUT[:], pp.ap()[:, 0:256])
            k.dma('sp', DV('pool_p')[l], TOUT[113:128, :], outp=True)
        stop_at('a2p')
        for c in range(2):
            for j in range(31):
                k.ts('pool', DG[:, j, :], identf[:], wdw[:, c, j:j + 1], None, ALU.mult)
            pp = ps_next()
            for j in range(31):
                k.mm(pp.ap()[:, 0:BW], DG[:, j, :], GLUB[:, c, 2 + j:2 + j + BW], j == 0, j == 30)
            k.act(CV[:, c, :], pp.ap()[:, 0:BW], AF.Identity, bias=cvec[:, c, 0:1])
            k.copy('pool', CVB[:, c, :], CV[:, c, :])
            k.tt('pool', SQB[:, c, :], CV[:, c, :], CV[:, c, :], ALU.mult)
        p1 = ps_next()
        for c in range(2):
            k.mm(p1.ap()[:, 0:BW], onesb[:], CVB[:, c, :], c == 0, c == 1)
        k.act(MEAN[:], p1.ap()[:, 0:BW], AF.Identity, scale=1.0 / 256)
        p2 = ps_next()
        for c in range(2):
            k.mm(p2.ap()[:, 0:BW], onesb[:], SQB[:, c, :], c == 0, c == 1)
        k.tt('dve', VAR[:], MEAN[:], MEAN[:], ALU.mult)
        k.stt('dve', VAR[:], p2.ap()[:, 0:BW], 1.0 / 256, VAR[:], ALU.mult, ALU.subtract)
        k.ts('dve', VAR[:], VAR[:], 0.0, None, ALU.max)
        k.rsqrt(VAR[:], VAR[:], 1.0, EPS)
        for c in range(2):
            k.tt('dve', CV[:, c, :], CV[:, c, :], MEAN[:], ALU.subtract)
            k.tt('dve', CV[:, c, :], CV[:, c, :], VAR[:], ALU.mult)
            k.act(ACTV[:, c, :], CV[:, c, :], AF.Silu, bias=cvec[:, c, 2:3], scale=cvec[:, c, 1:2])
        for co in range(2):
            pp = ps_next()
            for ci_ in range(2):
                k.mm(pp.ap()[:, 0:BW], wpw[:, ci_, co * 128:(co + 1) * 128], ACTV[:, ci_, :], ci_ == 0, ci_ == 1)
            k.stt('dve', MIXB[:, co, :], pp.ap()[:, 0:BW], cvec[:, co, 3:4], SZA[:, co, :], ALU.add, ALU.mult)
        k.copy('pool', GLUB[:, :, 2:32], GLUB[:, :, 2 + BW:32 + BW])
        stop_at('a2c')
        for c in range(2):
            X = BIN[:, c, :]
            n = BW + 15
            k.tt('pool', SA[:, 1:16 + BW], X[:, 1:16 + BW], X[:, 0:15 + BW], ALU.add)
            k.tt('pool', SBb[:, 3:16 + BW], SA[:, 3:16 + BW], SA[:, 1:14 + BW], ALU.add)
            if c == 0:
                tot = (SA, SBb)
                ws = (2, 4)
            else:
                k.tt('pool', SA[:, 7:16 + BW], SBb[:, 7:16 + BW], SBb[:, 3:12 + BW], ALU.add)
                k.tt('pool', SBb[:, 15:16 + BW], SA[:, 15:16 + BW], SA[:, 7:8 + BW], ALU.add)
                tot = (SA, SBb)
                ws = (8, 16)
            for hf in range(2):
                pr = slice(hf * 64, hf * 64 + 64)
                k.ts('pool', SG[pr, c, :], tot[hf][pr, 16:16 + BW], 1.0 / ws[hf], None, ALU.mult)
                k.tt('pool', DB[pr, c, :], SG[pr, c, :], X[pr, 16:16 + BW], ALU.subtract)
                if tb == 0:
                    k.tt('pool', SG[pr, c, 0:16], tot[hf][pr, 16:32], INVC[pr, c, :], ALU.mult)
                    k.tt('pool', DB[pr, c, 0:16], SG[pr, c, 0:16], X[pr, 16:32], ALU.subtract)
            pp = ps_next()
            k.mm(pp.ap()[:, 0:BW], wpl[:, c, :], DB[:, c, :], True, True)
            k.stt('dve', MIXB[:, 2 + c, :], pp.ap()[:, 0:BW], cvec[:, c, 4:5], SZB[:, c, :], ALU.mult, ALU.mult)
        k.copy('pool', BIN[:, :, 1:16], BIN[:, :, 1 + BW:16 + BW])
        k.dma('sp', View(D['mix_d'][:, :, t0:t0 + BW], RD('mix', tb)), MIXB[:])

    stop_at('a')
    S.barrier(); E['ar_reset']()
    w1h = ar("w1h", [2, 32, 64], BF16)
    for kv in range(2):
        for rh in range(2):
            sb = stage[st_rr[0] % 2]; st_rr[0] += 1
            srcv = View(D['w1h'][l, :, kv, rh * 16:(rh + 1) * 16, :].rearrange("p a b -> p (a b)"), Res('c'))
            k.dma('sp', sb[0:64, 0:1024], srcv)
            k.dma('sp', sb[64:128, 0:1024], srcv)
            k.copy('dve', w1h[:, kv, rh * 16:(rh + 1) * 16, :].rr("p a b -> p (a b)"), sb[:, 0:1024])
    pet = ar("pet", [2, 32], BF16)
    sb = stage[st_rr[0] % 2]; st_rr[0] += 1
    k.dma('sp', sb[0:64, 0:64], View(D['pe_t'][l].rearrange("p a b -> p (a b)"), Res('c')))
    k.copy('dve', pet[0:64].rr("p a b -> p (a b)"), sb[0:64, 0:64])
    w2p = ar("w2p", [2, 2, 128], BF16)
    k.memset('pool', w2p[:], 0.0)
    sb = stage[st_rr[0] % 2]; st_rr[0] += 1
    k.dma('sp', sb[0:64, 0:128], View(D['w2'][l].rearrange("p a b -> p (a b)"), Res('c')))
    for kv in range(2):
        k.copy('dve', w2p[0:64, kv, 0, 0:64], sb[0:64, kv * 64:(kv + 1) * 64])
        k.copy('dve', w2p[0:64, kv, 1, 64:128], sb[0:64, kv * 64:(kv + 1) * 64])
    PEB = ar("PEB", [2], F32)
    for kv in range(2):
        pp = ps_next()
        for r in range(32):
            k.mm(pp.ap()[0:64, 0:1], w1h[0:64, kv, r, :], pet[0:64, kv, r:r + 1], r == 0, r == 31)
        k.copy('dve', PEB[0:64, kv:kv + 1], pp.ap()[0:64, 0:1])
    U = ar("U", [254], F32)
    U2 = ar("U2", [254], F32)
    H = ar("H", [2, 254], BF16)
    for kv in range(2):
        slot = 0 if kv == 0 else 2
        for g in range(2):
            pp = ps_next()
            pr = slice(g * 64, g * 64 + 64)
            for r in range(32):
                k.mm(pp.ap()[0:64, 0:127], w1h[pr, kv, r, :], KT2[pr, slot, r:r + 16 * 126 + 1:16], r == 0, r == 31)
            k.act(U[0:64, g * 127:(g + 1) * 127], pp.ap()[0:64, 0:127], AF.Identity, bias=PEB[0:64, kv:kv + 1])
        k.tt('dve', U2[0:64], U[0:64], U[0:64], ALU.mult)
        k.ts('dve', U2[0:64], U2[0:64], 0.044715, 1.0, ALU.mult, ALU.add)
        k.tt('dve', U2[0:64], U2[0:64], U[0:64], ALU.mult)
        k.act(U2[0:64], U2[0:64], AF.Sigmoid, scale=1.5957691216057308)
        k.tt('dve', H[0:64, kv, :], U[0:64], U2[0:64], ALU.mult)
    pp = ps_next()
    for g in range(2):
        k.mm(pp.ap()[:, 0:127], w2p[0:64, 0, g, :], H[0:64, 0, g * 127:(g + 1) * 127], g == 0, g == 1)
    k.copy('dve', KC2[:], pp.ap()[:, 0:127])
    for g in range(2):
        pp = ps_next()
        k.mm(pp.ap()[0:127, 0:64], H[0:64, 1, g * 127:(g + 1) * 127], w2p[0:64, 1, 0, 0:64], True, True)
        k.copy('dve', VCX[0:127, g, 0:64], pp.ap()[0:127, 0:64])


def prompt_phase_c(nc, S, k, D, DV, l, E):
    stop_at('b')
    ar = E['ar']; PS = E['PS']
    identb = E['identb']
    KT2 = E['KT2']; KA = E['KA']; VX = E['VX']; GT = E['GT']; VCX = E['VCX']; KC2 = E['KC2']
    TRI4 = E['TRI4']; TRIU4 = E['TRIU4']; CMREL4 = E['CMREL4']; SHIFT = E['SHIFT']; NF = E['NF']; ADDC = E['ADDC']
    QZ = E['QZ']; QA = E['QA']
    xsrc_t = E['xsrc_t']; xdst_t = E['xdst_t']; stage = E['stage']; st_rr = E['st_rr']; RD = E['RD']
    BT = 4
    WOUTB = E['WOUTB']
    QBt = [ar("QBc%d" % i, [512], BF16) for i in range(2)]
    SZCt = [ar("SZCt%d" % i, [4, 128], BF16) for i in range(2)]
    MIXt = [ar("MIXt%d" % i, [8, 128], BF16) for i in range(2)]
    XTc = [ar("XTc%d" % i, [DM], F32) for i in range(2)]
    PT = [ar("PT%d" % i, [512], BF16) for i in range(3)]
    OACC = ar("OACC", [8, 64], F32)
    TMP = ar("TMP", [8, 64], F32)
    OB = ar("OB", [512], BF16)
    ZR = ar("ZR", [3, 8], F32)
    COEF = ar("COEF", [3, 8], F32)
    IMPH = ar("IMPH", [8, 32], F32)
    IMP = ar("IMP", [2, 32], F32)
    M8 = ar("M8", [2, 8], F32)
    NSELT = ar("NSELT", [128], BF16)
    k.memset('pool', NSELT[:], 0.0)
    pt_rr = [0]
    sc_rr = [0]
    def sc_bank():
        b = PS[sc_rr[0] % 2]; sc_rr[0] += 1
        return b

    for qt in range(NT):
        cols = slice(qt * 128, (qt + 1) * 128)
        qb = QBt[qt % 2]; szc = SZCt[qt % 2]; mixt = MIXt[qt % 2]; xt = XTc[qt % 2]
        k.dma('sp', qb[:], View(D['qb_d'][cols, :], RD('qb', qt)))
        k.dma('sp', szc[:], View(D['szc_d'][:, :, cols], RD('szc', qt // BT)))
        k.dma('sp', mixt[:, 0:4, :], View(D['mix_d'][:, :, cols], RD('mix', qt // BT)))
        k.dma('sp', xt[:], xsrc_t(qt))
        pq = sc_bank()
        for r in range(4):
            k.tr(pq.ap(BF16)[:, r * 128:(r + 1) * 128], qb[:, r * 128:(r + 1) * 128], identb[:])
        pqv = pq.ap(BF16)[:, 0:512].rr("p (r q) -> p r q", r=4)
        k.copy('act', QZ[0][0:64], pqv[0:64])
        k.copy('dve', QZ[1][64:128], pqv[64:128])
        k.copy('act', QA[0][0:64], pqv[0:64])
        k.copy('dve', QA[1][64:128], pqv[64:128])
        nv = min(127, 8 * qt + 7)
        for g in range(2):
            ps_s = sc_bank()
            k.mm(ps_s.ap()[0:nv, :], KC2[:, 0:nv], QZ[g][:].rr("p a b -> p (a b)"), True, False)
            k.mm(ps_s.ap()[0:nv, :], SHIFT[:, qt, 0:nv], CMREL4[:], False, True)
            pt = PT[pt_rr[0] % 3]; pt_rr[0] += 1
            k.act(pt[0:nv, :], ps_s.ap()[0:nv, :], AF.Exp, scale=0.125)
            for h4 in range(4):
                k.mm(PS[2 + g].ap()[:, h4 * 97:(h4 + 1) * 97], pt[0:nv, h4 * 128:(h4 + 1) * 128], VCX[0:nv, g, :], h4 == 0, h4 == 3)
        for g in range(2):
            oc = PS[2 + g].ap()[:, 0:388].rr("p (h x) -> p h x", h=4)
            hs = slice(g * 4, g * 4 + 4)
            k.ts('dve', ZR[:, 0, hs], oc[:, :, 64], 1e-30, None, ALU.max)
            k.recip(ZR[:, 0, hs], ZR[:, 0, hs])
            k.tt('dve', COEF[:, 0, hs], ZR[:, 0, hs], GT[:, qt, hs], ALU.mult)
            k.tt('dve', OACC[:, hs, :], oc[:, :, 0:64], COEF[:, 0, hs].us(2).bc([128, 4, 64]), ALU.mult)
            k.tt('dve', IMPH[:, hs, :], oc[:, :, 65:97], ZR[:, 0, hs].us(2).bc([128, 4, 32]), ALU.mult)
        k.red('dve', IMP[:], IMPH[:].rr("p (g h) j -> p g j h", g=2))
        k.tt('dve', IMP[:], IMP[:], NF[:, qt, :].us(1).bc([128, 2, 32]), ALU.mult)
        k.tt('dve', IMP[:], IMP[:], ADDC[:, qt, :].us(1).bc([128, 2, 32]), ALU.add)
        for g in range(2):
            k.max8(M8[:, g, :], IMP[:, g, :])
            dstc = NSELT[:, 64:96] if g == 0 else NSELT[:, 0:32]
            k.ts('dve', dstc, IMP[:, g, :], M8[:, g, 7:8], NEGV, ALU.is_lt, ALU.mult)
        pn = sc_bank()
        k.tr(pn.ap(BF16)[:, 0:128], NSELT[:], identb[:])
        k.copy('act', QA[0][64:96], pn.ap(BF16)[64:96, 0:128].us(1).bc([32, 4, 128]))
        k.copy('dve', QA[1][0:32], pn.ap(BF16)[0:32, 0:128].us(1).bc([32, 4, 128]))
        stop_at('c1')
        for g in range(2):
            kts = [kt for kt in (qt - 2, qt - 1, qt) if kt >= 0]
            for idx, kt in enumerate(kts):
                ps_s = sc_bank()
                masked = (kt == qt) or (kt == qt - 2)
                k.mm(ps_s.ap(), KT2[:, 1, kt * 128:(kt + 1) * 128], QZ[g][:].rr("p a b -> p (a b)"), True, not masked)
                if kt == qt:
                    k.mm(ps_s.ap(), identb[:], TRI4[:], False, True)
                elif kt == qt - 2:
                    k.mm(ps_s.ap(), identb[:], TRIU4[:], False, True)
                pt = PT[pt_rr[0] % 3]; pt_rr[0] += 1
                k.act(pt[:], ps_s.ap(), AF.Exp, scale=0.125)
                for h4 in range(4):
                    k.mm(PS[6 + g].ap()[:, h4 * 65:(h4 + 1) * 65], pt[:, h4 * 128:(h4 + 1) * 128], VX[:, kt, 1, g, :],
                         idx == 0 and h4 == 0, idx == len(kts) - 1 and h4 == 3)
        stop_at('c2')
        for g in range(2):
            for kt in range(qt + 1):
                ps_s = sc_bank()
                k.mm(ps_s.ap(), KA[:, g, kt * 128:(kt + 1) * 128], QA[g][:].rr("p a b -> p (a b)"), True, kt != qt)
                if kt == qt:
                    k.mm(ps_s.ap(), identb[:], TRI4[:], False, True)
                pt = PT[pt_rr[0] % 3]; pt_rr[0] += 1
                k.act(pt[:], ps_s.ap(), AF.Exp, scale=0.125)
                for h4 in range(4):
                    k.mm(PS[4 + g].ap()[:, h4 * 65:(h4 + 1) * 65], pt[:, h4 * 128:(h4 + 1) * 128], VX[:, kt, 0, g, :],
                         kt == 0 and h4 == 0, kt == qt and h4 == 3)
        stop_at('c3')
        for (br, pb) in ((1, 4), (2, 6)):
            for g in range(2):
                o = PS[pb + g].ap()[:, 0:260].rr("p (h x) -> p h x", h=4)
                hs = slice(g * 4, g * 4 + 4)
                k.recip(ZR[:, br, hs], o[:, :, 64])
                k.tt('dve', COEF[:, br, hs], ZR[:, br, hs], GT[:, qt, br * 8 + g * 4:br * 8 + g * 4 + 4], ALU.mult)
                k.tt('dve', TMP[:, hs, :], o[:, :, 0:64], COEF[:, br, hs].us(2).bc([128, 4, 64]), ALU.mult)
            k.tt('pool', OACC[:], OACC[:], TMP[:], ALU.add)
        k.copy('pool', OB[:], OACC[:].rr("p h d -> p (h d)"))
        po = sc_bank()
        for c in range(4):
            k.tr(po.ap(BF16)[:, c * 128:(c + 1) * 128], OB[:, c * 128:(c + 1) * 128], identb[:])
        k.tt('dve', mixt[:, 4:8, :], po.ap(BF16)[:, 0:512].rr("p (c q) -> p c q", c=4), szc[:], ALU.mult)
        for nb in range(2):
            for mc in range(8):
                k.mm(PS[2 + nb].ap(), mixt[:, mc, :], WOUTB[:, mc, nb * 512:(nb + 1) * 512], mc == 0, mc == 7)
            k.tt('dve', xt[:, nb * 512:(nb + 1) * 512], xt[:, nb * 512:(nb + 1) * 512], PS[2 + nb].ap(), ALU.add)
        k.dma('sp', xdst_t(qt), xt[:], outp=True)


def sample_s0(nc, S, k, D, DV, l, E):
    ar = E['ar']; WINB = E['WINB']; ps_next = E['ps_next']; identb = E['identb']
    gqb = E['gqb']; gkb = E['gkb']; RD = E['RD']; depth = E['depth']
    N = NSMP
    XS = ar("XS", [DM], F32)
    junk = ar("junk_s", [DM], BF16)
    hb = ar("hb_s", [DM], BF16)
    ssq = ar("ssq_s", [8], F32)
    HTs = ar("HTs", [8, N], BF16)
    PJS = ar("PJS", [DIN], F32)
    SQ2 = ar("SQ2s", [1280], F32)
    SS20 = ar("SS20s", [20], F32)
    RT = ar("RTs", [6, 8, 8], F32)
    CSs = ar("CSs", [16], F32)
    k.dma('sp', CSs[0:N], DV('c_cs_s'))
    xsrc = DV('xs') if l == 0 else View(D['xs1'], RD('xs1', 0))
    k.dma('sp', XS[0:N], xsrc)
    k.memset('pool', ssq[0:N, 0:1], 0.0)
    k.act(junk[0:N], XS[0:N], AF.Square, accum=ssq[0:N, 0:1])
    k.rsqrt(ssq[0:N, 1:2], ssq[0:N, 0:1], 1.0 / DM, EPS)
    k.ts('dve', hb[0:N], XS[0:N], ssq[0:N, 1:2], None, ALU.mult)
    pst = ps_next()
    for kc in range(8):
        k.tr(pst.ap(BF16)[:, kc * N:(kc + 1) * N], hb[0:N, kc * 128:(kc + 1) * 128], identb[0:N, 0:N])
    k.copy('act', HTs[:].rr("p a b -> p (a b)"), pst.ap(BF16)[:, 0:8 * N])
    for c0 in range(0, DIN, 512):
        cw = min(512, DIN - c0)
        pp = ps_next()
        for kc in range(8):
            k.mm(pp.ap()[0:N, 0:cw], HTs[:, kc, :], WINB[:, kc, c0:c0 + cw], kc == 0, kc == 7)
        k.copy('act' if (c0 // 512) % 2 == 0 else 'dve', PJS[0:N, c0:c0 + cw], pp.ap()[0:N, 0:cw])
    QO = 1280; KO = 1792
    k.tt('pool', SQ2[0:N], PJS[0:N, QO:QO + 1280], PJS[0:N, QO:QO + 1280], ALU.mult)
    k.red('dve', SS20[0:N], SQ2[0:N].rr("p (h d) -> p h d", d=64))
    k.rsqrt(SS20[0:N], SS20[0:N], 1.0 / 64, EPS)
    Q3 = PJS[0:N, QO:QO + 512].rr("p (h d) -> p h d", d=64)
    k.tt('dve', Q3, Q3, SS20[0:N, 0:8].us(2).bc([N, 8, 64]), ALU.mult)
    k.tt('dve', Q3, Q3, gqb[0:N].us(1).bc([N, 8, 64]), ALU.mult)
    K3 = []
    for b in range(3):
        kk = PJS[0:N, KO + b * 256:KO + b * 256 + 128].rr("p (g d) -> p g d", d=64)
        k.tt('dve', kk, kk, SS20[0:N, 8 + b * 4:8 + b * 4 + 2].us(2).bc([N, 2, 64]), ALU.mult)
        k.tt('dve', kk, kk, gkb[0:N, b, :].us(1).bc([N, 2, 64]), ALU.mult)
        K3.append(kk)
    cos8 = CSs[0:N, 0:8]; sin8 = CSs[0:N, 8:16]
    def rope(X, nh):
        x1 = X[:, :, 0:8]; x2 = X[:, :, 8:16]
        cb = cos8.us(1).bc([N, nh, 8]); sbv = sin8.us(1).bc([N, nh, 8])
        k.tt('dve', RT[0:N, 0, 0:nh, :], x1, cb, ALU.mult)
        k.tt('dve', RT[0:N, 1, 0:nh, :], x2, sbv, ALU.mult)
        k.tt('dve', RT[0:N, 2, 0:nh, :], x2, cb, ALU.mult)
        k.tt('dve', RT[0:N, 3, 0:nh, :], x1, sbv, ALU.mult)
        k.tt('dve', x1, RT[0:N, 0, 0:nh, :], RT[0:N, 1, 0:nh, :], ALU.subtract)
        k.tt('dve', x2, RT[0:N, 2, 0:nh, :], RT[0:N, 3, 0:nh, :], ALU.add)
    rope(Q3, 8)
    for b in range(3):
        rope(K3[b], 2)
    k.act(SQ2[0:N, 0:256], PJS[0:N, 256:512], AF.Sigmoid)
    k.tt('dve', PJS[0:N, 0:256], PJS[0:N, 0:256], SQ2[0:N, 0:256], ALU.mult)
    k.act(PJS[0:N, 512:768], PJS[0:N, 512:768], AF.Silu)
    k.act(PJS[0:N, 1024:1280], PJS[0:N, 1024:1280], AF.Silu)
    k.act(PJS[0:N, 2560:2584], PJS[0:N, 2560:2584], AF.Sigmoid)
    k.act(PJS[0:N, 2584:3096], PJS[0:N, 2584:3096], AF.Silu)
    k.dma('sp', View(D['pjs_d'], RD('pjs', l)), PJS[0:N])
    k.dma('sp', DV('cmp_s')[l], PJS[0:N, KO:KO + 256], outp=True)
    k.dma('sp', DV('slc_s')[l], PJS[0:N, KO + 256:KO + 512], outp=True)
    k.dma('sp', DV('win_s')[l, :, 255, :], PJS[0:N, KO + 512:KO + 768], outp=True)
    k.dma('sp', DV('conv_s')[l, :, 29, :], PJS[0:N, 0:256], outp=True)
    k.dma('sp', DV('pool_s')[l, :, 14, :], PJS[0:N, 768:1024], outp=True)
    k.dma('sp', DV('conv_s')[l, :, 0:29, :], DV('state_conv')[l, :, 1:30, :], outp=True)
    k.dma('sp', DV('pool_s')[l, :, 0:14, :], DV('state_pool')[l, :, 1:15, :], outp=True)
    k.dma('sp', DV('win_s')[l, :, 0:255, :], DV('cache_win')[l, :, 1:256, :], outp=True)


def sample_phase_s(nc, S, k, D, DV, l, E):
    ar = E['ar']; PS = E['PS']; identb = E['identb']; identf = E['identf']
    stage = E['stage']; st_rr = E['st_rr']; RD = E['RD']; WOUTB = E['WOUTB']; depth = E['depth']
    n_phys = E['n_phys']
    N = NSMP
    sc_rr = [0]
    def sc_bank():
        b = PS[sc_rr[0] % 2]; sc_rr[0] += 1
        return b
    def nstage():
        sb = stage[st_rr[0] % 2]; st_rr[0] += 1
        return sb
    w1h = ar("w1h_s", [2, 32, 64], BF16)
    for kv in range(2):
        for rh in range(2):
            sb = nstage()
            srcv = View(D['w1h'][l, :, kv, rh * 16:(rh + 1) * 16, :].rearrange("p a b -> p (a b)"), Res('c'))
            k.dma('sp', sb[0:64, 0:1024], srcv)
            k.dma('sp', sb[64:128, 0:1024], srcv)
            k.copy('dve', w1h[:, kv, rh * 16:(rh + 1) * 16, :].rr("p a b -> p (a b)"), sb[:, 0:1024])
    pet = ar("pet_s", [2, 32], BF16)
    sb = nstage()
    k.dma('sp', sb[0:64, 0:64], View(D['pe_t'][l].rearrange("p a b -> p (a b)"), Res('c')))
    k.copy('dve', pet[0:64].rr("p a b -> p (a b)"), sb[0:64, 0:64])
    w2p = ar("w2p_s", [2, 2, 128], BF16)
    k.memset('pool', w2p[:], 0.0)
    sb = nstage()
    k.dma('sp', sb[0:64, 0:128], View(D['w2'][l].rearrange("p a b -> p (a b)"), Res('c')))
    for kv in range(2):
        k.copy('dve', w2p[0:64, kv, 0, 0:64], sb[0:64, kv * 64:(kv + 1) * 64])
        k.copy('dve', w2p[0:64, kv, 1, 64:128], sb[0:64, kv * 64:(kv + 1) * 64])
    PEB = ar("PEB_s", [2], F32)
    for kv in range(2):
        pp = sc_bank()
        for r in range(32):
            k.mm(pp.ap()[0:64, 0:1], w1h[0:64, kv, r, :], pet[0:64, kv, r:r + 1], r == 0, r == 31)
        k.copy('dve', PEB[0:64, kv:kv + 1], pp.ap()[0:64, 0:1])
    VCXs = ar("VCXs", [2, 98], BF16)
    k.memset('pool', VCXs[:], 1.0)
    sb = nstage()
    k.dma('sp', sb[0:127, 0:33], DV('c_ov_s'))
    for g in range(2):
        k.copy('dve', VCXs[0:127, g, 65:98], sb[0:127, 0:33])
    GRP = ar("GRP", [32], F32); k.dma('sp', GRP[:], DV('c_grp'))
    NFs = ar("NFs", [33], F32); k.dma('sp', NFs[0:32], DV('c_nfs'))
    ADDs = ar("ADDs", [33], F32); k.dma('sp', ADDs[0:32], DV('c_adds'))
    E33 = ar("E33", [128], F32); k.dma('sp', E33[0:33], DV('c_e33'))
    SEL8 = ar("SEL8", [128], F32); k.dma('sp', SEL8[0:16], DV('c_sel8'))
    PMOD = ar("PMOD", [8], F32); k.dma('sp', PMOD[:, 0:1], DV('c_pmod8'))
    PTI = ar("PTI", [16], I32)
    PTF = ar("PTF", [16], F32)
    IDXF = ar("IDXF", [16], F32)
    IDX = ar("IDX", [16], I32)
    k.dma('sp', PTI[0:16], DV('ptab_t'))
    k.copy('dve', PTF[0:16], PTI[0:16])
    pp = sc_bank()
    k.mm(pp.ap()[:, 0:16], SEL8[0:16, :], PTF[0:16, :], True, True)
    k.ts('dve', IDXF[:], pp.ap()[:, 0:16], 8.0, float(l * n_phys * 8), ALU.mult, ALU.add)
    k.ts('dve', IDXF[:], IDXF[:], PMOD[:, 0:1], None, ALU.add)
    k.copy('dve', IDX[:], IDXF[:])
    stop_at('q0')
    PJ2 = ar("PJ2", [DIN], F32)
    k.dma('sp', PJ2[0:N], View(D['pjs_d'], RD('pjs', l)))
    MIXs = ar("MIXs", [DM], F32)
    KO = 1792
    ar_off = E['ar_off']
    mark2 = ar_off[0]
    wpw = ar("wpw_s", [2, 256], BF16)
    wpl = ar("wpl_s", [2, 128], BF16)
    for kc in range(2):
        sb = nstage()
        k.dma('sp', sb[:, 0:256], DV('w_pw')[l, kc * 128:(kc + 1) * 128, :])
        k.copy('dve', wpw[:, kc, :], sb[:, 0:256])
    sb = nstage()
    k.dma('sp', sb[:, 0:256], View(D['wpool_bd'][l].rearrange("p a b -> p (a b)"), Res('c')))
    k.copy('dve', wpl[:].rr("p a b -> p (a b)"), sb[:, 0:256])
    ROWS = ar("ROWS", [5, 256], F32)
    k.dma('sp', ROWS[0:N].rr("p a b -> p (a b)"), View(D['rows_s'][l].rearrange("a b -> (a b)").partition_broadcast(N), Res('c')))
    XC = ar("XCs", [31, 128], F32)
    WD = ar("WDs", [31, 128], F32)
    CVs = ar("CVs", [256], F32)
    T1 = ar("T1s", [256], F32)
    ST = ar("STs", [8], F32)
    for ch in range(2):
        cs_ = slice(ch * 128, (ch + 1) * 128)
        k.dma('sp', XC[0:N, 0:30, :], DV('state_conv')[l, :, :, cs_])
        k.copy('pool', XC[0:N, 30, :], PJ2[0:N, cs_])
        k.dma('sp', WD[0:N], View(D['w_dw'][l, :, cs_].partition_broadcast(N), Res('c')))
        k.tt('dve', XC[0:N], XC[0:N], WD[0:N], ALU.mult)
        k.red('dve', CVs[0:N, cs_], XC[0:N].rr("p j c -> p c j"))
    k.tt('dve', CVs[0:N], CVs[0:N], ROWS[0:N, 0, :], ALU.add)
    k.red('dve', ST[0:N, 0:1], CVs[0:N])
    k.ts('dve', ST[0:N, 0:1], ST[0:N, 0:1], 1.0 / 256, None, ALU.mult)
    k.ts('dve', CVs[0:N], CVs[0:N], ST[0:N, 0:1], None, ALU.subtract)
    k.tt('dve', T1[0:N], CVs[0:N], CVs[0:N], ALU.mult)
    k.red('dve', ST[0:N, 1:2], T1[0:N])
    k.rsqrt(ST[0:N, 2:3], ST[0:N, 1:2], 1.0 / 256, EPS)
    k.ts('dve', CVs[0:N], CVs[0:N], ST[0:N, 2:3], None, ALU.mult)
    k.tt('dve', CVs[0:N], CVs[0:N], ROWS[0:N, 1, :], ALU.mult)
    k.tt('dve', CVs[0:N], CVs[0:N], ROWS[0:N, 2, :], ALU.add)
    ACs = ar("ACs", [256], BF16)
    k.act(ACs[0:N], CVs[0:N], AF.Silu)
    ATs = ar("ATs", [2, N], BF16)
    pp = sc_bank()
    for c in range(2):
        k.tr(pp.ap(BF16)[:, c * N:(c + 1) * N], ACs[0:N, c * 128:(c + 1) * 128], identb[0:N, 0:N])
    k.copy('act', ATs[:].rr("p a b -> p (a b)"), pp.ap(BF16)[:, 0:2 * N])
    pp = sc_bank()
    for c in range(2):
        k.mm(pp.ap()[0:N, 0:256], ATs[:, c, :], wpw[:, c, :], c == 0, c == 1)
    k.tt('dve', T1[0:N], pp.ap()[0:N, 0:256], ROWS[0:N, 3, :], ALU.add)
    k.tt('dve', MIXs[0:N, 0:256], T1[0:N], PJ2[0:N, 512:768], ALU.mult)
    stop_at('q1')
    XP = ar("XPs", [16, 256], F32)
    k.dma('sp', XP[0:N, 0:15, :], DV('state_pool')[l])
    k.copy('pool', XP[0:N, 15, :], PJ2[0:N, 768:1024])
    Ds = ar("Ds", [256], F32)
    for gi, w in enumerate((2, 4, 8, 16)):
        gs = slice(gi * 64, (gi + 1) * 64)
        k.red('dve', Ds[0:N, gs], XP[0:N, 16 - w:16, gs].rr("p j c -> p c j"))
        k.stt('dve', Ds[0:N, gs], Ds[0:N, gs], 1.0 / w, PJ2[0:N, 768 + gi * 64:768 + (gi + 1) * 64], ALU.mult, ALU.subtract)
    Dsb = ar("Dsb", [256], BF16)
    k.copy('dve', Dsb[0:N], Ds[0:N])
    DTs = ar("DTs", [2, N], BF16)
    pp = sc_bank()
    for c in range(2):
        k.tr(pp.ap(BF16)[:, c * N:(c + 1) * N], Dsb[0:N, c * 128:(c + 1) * 128], identb[0:N, 0:N])
    k.copy('act', DTs[:].rr("p a b -> p (a b)"), pp.ap(BF16)[:, 0:2 * N])
    pp = sc_bank()
    for c in range(2):
        k.mm(pp.ap()[0:N, c * 128:(c + 1) * 128], DTs[:, c, :], wpl[:, c, :], c == 0, c == 1)
    k.tt('dve', T1[0:N], pp.ap()[0:N, 0:256], ROWS[0:N, 4, :], ALU.mult)
    k.tt('dve', MIXs[0:N, 256:512], T1[0:N], PJ2[0:N, 1024:1280], ALU.mult)
    stop_at('q2')
    S.barrier(); ar_off[0] = mark2
    NHS = ar("NHS", [8, 387], F32)
    k.copy('pool', NHS[0:N, :, 0:64], PJ2[0:N, 1280:1792].rr("p (h d) -> p h d", d=64))
    for j, off in enumerate((KO + 256, KO + 384, KO + 512, KO + 640)):
        k.copy('pool', NHS[0:N, :, 64 + j * 64:128 + j * 64].rr("p (g r) d -> p g r d", g=2),
               PJ2[0:N, off:off + 128].rr("p (g d) -> p g d", g=2).us(2).bc([N, 2, 4, 64]))
    k.copy('pool', NHS[0:N, :, 320:384], PJ2[0:N, 2584:3096].rr("p (h d) -> p h d", d=64))
    k.copy('pool', NHS[0:N, :, 384:387], PJ2[0:N, 2560:2584].rr("p (b h) -> p h b", b=3))
    k.dma('sp', View(D['nh_d'], RD('nh', l)), NHS[0:N].rr("p a b -> p (a b)"))
    NH = ar("NH", [387], F32)
    k.dma('sp', NH[:], View(D['nh_d'].rearrange("n (h x) -> (n h) x", h=8), RD('nh', l)))
    QPAD = ar("QPAD", [2, 8, 128], BF16)
    k.memset('pool', QPAD[0:N], 0.0)
    k.copy('pool', QPAD[0:N, 0, :, 0:64], PJ2[0:N, 1280:1792].rr("p (h d) -> p h d", d=64))
    k.copy('pool', QPAD[0:N, 1, :, 64:128], PJ2[0:N, 1280:1792].rr("p (h d) -> p h d", d=64))
    QTZ = ar("QTZ", [2, 8, N], BF16)
    pp = sc_bank()
    for par in range(2):
        for h in range(8):
            k.tr(pp.ap(BF16)[:, (par * 8 + h) * N:(par * 8 + h + 1) * N], QPAD[0:N, par, h, :], identb[0:N, 0:N])
    k.copy('act', QTZ[:].rr("p a b c -> p (a b c)"), pp.ap(BF16)[:, 0:16 * N])
    stop_at('q3')
    AC = ar("AC", [4096], F32)
    XTc = ar("XTc", [2, 16, 128], BF16)
    U = ar("Us", [254], F32)
    U2 = ar("U2s", [254], F32)
    H = ar("Hs", [2, 254], BF16)
    KC2s = ar("KC2s", [127], BF16)
    PTs = ar("PTs", [8], BF16)
    OCT = PS[2]; OST = PS[3]; OWT = PS[4]
    cflat = View(D['cache_cmp'], Res('c'))
    sflat = View(D['cache_slc'], Res('c'))
    def gather(dst, src, n):
        S.dma('pool', None, None, reads=_rs([IDX[:]]), writes=_rs([dst[:]]),
              fn=lambda e: e.indirect_dma_start(out=dst.t[:, :], out_offset=None, in_=src.ap[:, :],
                                                in_offset=bass.IndirectOffsetOnAxis(ap=IDX.t[:, n:n + 1], axis=0)))
    for n in range(N):
        gather(AC, cflat, n)
        stop_at('q4')
        for kv in range(2):
            for r0 in range(0, 16, 4):
                pp = PS[5 + ((kv * 4 + r0 // 4) % 2)]
                for rr in range(4):
                    r = r0 + rr
                    k.tr(pp.ap()[:, rr * 128:(rr + 1) * 128], AC[:, r * 256 + kv * 128:r * 256 + kv * 128 + 128], identf[:])
                k.copy('act' if (r0 // 4) % 2 == 0 else 'dve', XTc[:, kv, r0:r0 + 4, :].rr("p a b -> p (a b)"), pp.ap()[:, 0:512])
        stop_at('q5')
        for kv in range(2):
            for g in range(2):
                pp = sc_bank()
                pr = slice(g * 64, g * 64 + 64)
                i = 0
                for jp in range(2):
                    for r in range(16):
                        k.mm(pp.ap()[0:64, 0:127], w1h[pr, kv, jp * 16 + r, :], XTc[pr, kv, r, jp:jp + 127], i == 0, i == 31)
                        i += 1
                k.act(U[0:64, g * 127:(g + 1) * 127], pp.ap()[0:64, 0:127], AF.Identity, bias=PEB[0:64, kv:kv + 1])
            k.tt('dve', U2[0:64], U[0:64], U[0:64], ALU.mult)
            k.ts('dve', U2[0:64], U2[0:64], 0.044715, 1.0, ALU.mult, ALU.add)
            k.tt('dve', U2[0:64], U2[0:64], U[0:64], ALU.mult)
            k.act(U2[0:64], U2[0:64], AF.Sigmoid, scale=1.5957691216057308)
            k.tt('dve', H[0:64, kv, :], U[0:64], U2[0:64], ALU.mult)
        stop_at('q6')
        pp = sc_bank()
        for g in range(2):
            k.mm(pp.ap()[:, 0:127], w2p[0:64, 0, g, :], H[0:64, 0, g * 127:(g + 1) * 127], g == 0, g == 1)
        k.copy('dve', KC2s[:], pp.ap()[:, 0:127])
        for g in range(2):
            pp = sc_bank()
            k.mm(pp.ap()[0:127, 0:64], H[0:64, 1, g * 127:(g + 1) * 127], w2p[0:64, 1, 0, 0:64], True, True)
            k.copy('dve', VCXs[0:127, g, 0:64], pp.ap()[0:127, 0:64])
        pp = sc_bank()
        for g in range(2):
            k.mm(pp.ap()[0:127, g * 4:(g + 1) * 4], KC2s[:, 0:127], QTZ[:, g, 4 * g:4 * g + 4, n], True, True)
        k.act(PTs[0:127, :], pp.ap()[0:127, 0:8], AF.Exp, scale=0.125)
        for g in range(2):
            k.mm(OCT.ap()[0:98, n * 8 + 4 * g:n * 8 + 4 * g + 4], VCXs[0:127, g, :], PTs[0:127, 4 * g:4 * g + 4], True, True)
    stop_at('q7')
    OT = ar("OT", [128], F32)
    OCN = ar("OCN", [98], F32)
    k.copy('dve', OT[0:98], OCT.ap()[0:98, 0:128])
    pp = sc_bank()
    k.tr(pp.ap()[:, 0:98], OT[0:98, :], identf[0:98, 0:98])
    k.copy('dve', OCN[:], pp.ap()[:, 0:98])
    RZ = ar("RZs", [8], F32)
    k.ts('dve', RZ[:, 0:1], OCN[:, 64:65], 1e-30, None, ALU.max)
    k.recip(RZ[:, 0:1], RZ[:, 0:1])
    IMPN = ar("IMPN", [33], F32)
    k.ts('dve', IMPN[:], OCN[:, 65:98], RZ[:, 0:1], None, ALU.mult)
    pp = sc_bank()
    k.mm(pp.ap()[0:32, 0:33], GRP[:, :], IMPN[:, :], True, True)
    SC = ar("SCs", [33], F32)
    k.tt('dve', SC[0:32], pp.ap()[0:32, 0:33], NFs[0:32], ALU.mult)
    k.tt('dve', SC[0:32], SC[0:32], ADDs[0:32], ALU.add)
    M8 = ar("M8s", [8], F32)
    k.max8(M8[0:32], SC[0:32])
    NEG = ar("NEGs", [33], F32)
    k.ts('dve', NEG[0:32], SC[0:32], M8[0:32, 7:8], NEGV, ALU.is_lt, ALU.mult)
    pp = sc_bank()
    k.tr(pp.ap()[0:33, 0:32], NEG[0:32, :], identf[0:32, 0:32])
    NEGT = ar("NEGT", [32], F32)
    k.copy('dve', NEGT[0:33], pp.ap()[0:33, 0:32])
    pp = sc_bank()
    k.mm(pp.ap()[:, 0:32], E33[0:33, :], NEGT[0:33, :], True, True)
    NEGPC = ar("NEGPC", [32], F32)
    k.copy('dve', NEGPC[:], pp.ap()[:, 0:32])
    stop_at('q8')
    XTs = ar("XTs", [16, 128], BF16)
    Vs = ar("Vs", [16, 2, 65], BF16)
    k.memset('pool', Vs[:], 1.0)
    PT2 = ar("PT2", [2, 64], BF16)
    Ws = ar("Ws", [2, 256], F32)
    KTw = ar("KTw", [2, 128], BF16)
    Vw = ar("Vw", [2, 2, 65], BF16)
    k.memset('pool', Vw[:], 1.0)
    PTw = ar("PTw", [16], BF16)
    for n in range(N):
        gather(AC, sflat, n)
        for r0 in range(0, 16, 4):
            pp = PS[5 + (r0 // 4) % 2]
            for rr in range(4):
                r = r0 + rr
                k.tr(pp.ap()[:, rr * 128:(rr + 1) * 128], AC[:, r * 256:r * 256 + 128], identf[:])
            k.copy('act' if (r0 // 4) % 2 == 0 else 'dve', XTs[:, r0:r0 + 4, :].rr("p a b -> p (a b)"), pp.ap()[:, 0:512])
        k.copy('pool', Vs[:, :, :, 0:64], AC[:].rr("p (r x) -> p r x", r=16)[:, :, 128:256].rr("p r (g d) -> p r g d", g=2))
        pp = sc_bank()
        for g in range(2):
            for r in range(16):
                k.mm(pp.ap()[:, (g * 16 + r) * 4:(g * 16 + r) * 4 + 4], XTs[:, r, :], QTZ[:, g, 4 * g:4 * g + 4, n], True, True)
        for g in range(2):
            k.act(PT2[:, g, :], pp.ap()[:, g * 64:(g + 1) * 64], AF.Exp, scale=0.125, bias=NEGPC[:, 2 * n + g:2 * n + g + 1])
        for g in range(2):
            for r in range(16):
                k.mm(OST.ap()[0:65, n * 8 + 4 * g:n * 8 + 4 * g + 4], Vs[:, r, g, :], PT2[:, g, r * 4:(r + 1) * 4], r == 0, r == 15)
        stop_at('q9')
        k.dma('sp', Ws[:], DV('cache_win')[l, n].rr("(t p) f -> p t f", t=2))
        pp = PS[7]
        for t in range(2):
            k.tr(pp.ap()[:, t * 128:(t + 1) * 128], Ws[:, t, 0:128], identf[:])
        k.copy('act', KTw[:].rr("p a b -> p (a b)"), pp.ap()[:, 0:256])
        k.copy('pool', Vw[:, :, :, 0:64], Ws[:, :, 128:256].rr("p t (g d) -> p t g d", g=2))
        k.memset('pool', Vw[0:1, 0, :, :], 0.0)
        pp = sc_bank()
        for g in range(2):
            for t in range(2):
                k.mm(pp.ap()[:, (g * 2 + t) * 4:(g * 2 + t) * 4 + 4], KTw[:, t, :], QTZ[:, g, 4 * g:4 * g + 4, n], True, True)
        k.act(PTw[:], pp.ap()[:, 0:16], AF.Exp, scale=0.125)
        for g in range(2):
            for t in range(2):
                k.mm(OWT.ap()[0:65, n * 8 + 4 * g:n * 8 + 4 * g + 4], Vw[:, t, g, :], PTw[:, (g * 2 + t) * 4:(g * 2 + t) * 4 + 4], t == 0, t == 1)
    stop_at('q10')
    ONH = ar("ONH", [64], F32)
    OBR = ar("OBR", [65], F32)
    PN = ar("PN", [8], F32)
    T64 = ar("T64", [64], F32)
    k.tt('dve', PN[:, 0:1], RZ[:, 0:1], NH[:, 384:385], ALU.mult)
    k.ts('dve', ONH[:], OCN[:, 0:64], PN[:, 0:1], None, ALU.mult)
    for bi, (bank, ko, vo, go) in enumerate(((OST, 64, 128, 385), (OWT, 192, 256, 386))):
        k.copy('dve', OT[0:65], bank.ap()[0:65, 0:128])
        pp = sc_bank()
        k.tr(pp.ap()[:, 0:65], OT[0:65, :], identf[0:65, 0:65])
        k.copy('dve', OBR[:], pp.ap()[:, 0:65])
        k.tt('dve', T64[:], NH[:, 0:64], NH[:, ko:ko + 64], ALU.mult)
        k.red('dve', PN[:, 1:2], T64[:])
        k.act(PN[:, 2:3], PN[:, 1:2], AF.Exp, scale=0.125)
        k.stt('dve', OBR[:, 0:64], NH[:, vo:vo + 64], PN[:, 2:3], OBR[:, 0:64], ALU.mult, ALU.add)
        k.tt('dve', PN[:, 3:4], OBR[:, 64:65], PN[:, 2:3], ALU.add)
        k.recip(PN[:, 3:4], PN[:, 3:4])
        k.tt('dve', PN[:, 4:5], PN[:, 3:4], NH[:, go:go + 1], ALU.mult)
        k.stt('dve', ONH[:], OBR[:, 0:64], PN[:, 4:5], ONH[:], ALU.mult, ALU.add)
    k.tt('dve', ONH[:], ONH[:], NH[:, 320:384], ALU.mult)
    k.dma('sp', View(D['mixnh_d'], RD('mixnh', l)), ONH[:])
    k.dma('sp', MIXs[0:N, 512:1024], View(D['mixnh_d'].rearrange("(n h) d -> n (h d)", h=8), RD('mixnh', l)))
    stop_at('q11')
    MB = ar("MBs", [DM], BF16)
    k.copy('dve', MB[0:N], MIXs[0:N])
    MT = ar("MTs", [8, N], BF16)
    pp = sc_bank()
    for c in range(8):
        k.tr(pp.ap(BF16)[:, c * N:(c + 1) * N], MB[0:N, c * 128:(c + 1) * 128], identb[0:N, 0:N])
    k.copy('act', MT[:].rr("p a b -> p (a b)"), pp.ap(BF16)[:, 0:8 * N])
    XS2 = ar("XS2", [DM], F32)
    xsrc = DV('xs') if l == 0 else View(D['xs1'], RD('xs1', 0))
    k.dma('sp', XS2[0:N], xsrc)
    for nb in range(2):
        pp = sc_bank()
        for mc in range(8):
            k.mm(pp.ap()[0:N, :], MT[:, mc, :], WOUTB[:, mc, nb * 512:(nb + 1) * 512], mc == 0, mc == 7)
        k.tt('dve', XS2[0:N, nb * 512:(nb + 1) * 512], XS2[0:N, nb * 512:(nb + 1) * 512], pp.ap()[0:N, :], ALU.add)
    if l == depth - 1:
        k.dma('sp', DV('y_s'), XS2[0:N], outp=True)
    else:
        k.dma('sp', View(D['xs1'], RD('xs1', 0)), XS2[0:N])


def kernel(x_prompt, x_sample, state_conv, state_pool, cache_win_kv, cache_cmp_kv, cache_slc_kv, page_table,
           w_norm, w_in, w_out, w_dw, b_dw, ln_g, ln_b, w_pw, b_pw, w_pool, pool_scale,
           g_q, g_k, cmp_pe, cmp_w1, cmp_w2, _cores=None, _flags=None):
    f32 = np.float32
    inp = dict(w_norm=np.asarray(w_norm, f32), w_dw=np.asarray(w_dw, f32), b_dw=np.asarray(b_dw, f32),
               ln_g=np.asarray(ln_g, f32), ln_b=np.asarray(ln_b, f32), b_pw=np.asarray(b_pw, f32),
               w_pool=np.asarray(w_pool, f32), pool_scale=np.asarray(pool_scale, f32),
               cmp_pe=np.asarray(cmp_pe, f32), cmp_w1=np.asarray(cmp_w1, f32), cmp_w2=np.asarray(cmp_w2, f32))
    lay = _layout_params(inp)
    consts = _consts()
    flags = _flags or {}
    n_phys = cache_cmp_kv.shape[1]
    nc = build_nc(n_phys, **flags)
    cores = list(range(NCORES)) if _cores is None else _cores
    cc = np.ascontiguousarray(np.asarray(cache_cmp_kv, f32)).reshape(2 * n_phys * 8, 4096)
    csl = np.ascontiguousarray(np.asarray(cache_slc_kv, f32)).reshape(2 * n_phys * 8, 4096)
    shared = dict(w_in=np.asarray(w_in, f32), w_out=np.asarray(w_out, f32), w_pw=np.asarray(w_pw, f32),
                  g_q=np.asarray(g_q, f32), g_k=np.asarray(g_k, f32), w_dw=inp['w_dw'],
                  cache_cmp=cc, cache_slc=csl)
    for kk_, v in lay.items():
        shared[kk_] = v
    for kk_, v in consts.items():
        shared['c_' + kk_] = v
    in_maps = []
    for c in cores:
        m = dict(shared)
        m['xp'] = np.ascontiguousarray(np.asarray(x_prompt[c], f32))
        sl = slice(c * NSMP, (c + 1) * NSMP)
        m['xs'] = np.ascontiguousarray(np.asarray(x_sample[sl, 0], f32))
        m['state_conv'] = np.ascontiguousarray(np.asarray(state_conv[:, sl], f32))
        m['state_pool'] = np.ascontiguousarray(np.asarray(state_pool[:, sl], f32))
        m['cache_win'] = np.ascontiguousarray(np.asarray(cache_win_kv[:, sl], f32)).reshape(2, NSMP, 256, 256)
        m['ptab_t'] = np.ascontiguousarray(np.asarray(page_table[sl], np.int32).T)
        in_maps.append(m)
    res = run_bass_kernel_spmd(nc, in_maps, core_ids=list(range(len(cores))))
    R = res.results
    nb = len(cores)
    def gat(name):
        return [np.asarray(r[name]) for r in R]
    y_p = np.stack(gat('y_p'), 0)
    y_s = np.concatenate(gat('y_s'), 0).reshape(nb * NSMP, 1, DM)
    conv_p = np.stack(gat('conv_p'), 1)
    pool_p = np.stack(gat('pool_p'), 1)
    win_p = np.stack(gat('win_p'), 1).reshape(2, nb, 256, 2, 2, 64)
    cmp_p = np.stack(gat('cmp_p'), 1).reshape(2, nb, T, 2, 2, 64)
    slc_p = np.stack(gat('slc_p'), 1).reshape(2, nb, T, 2, 2, 64)
    conv_s = np.concatenate(gat('conv_s'), 1)
    pool_s = np.concatenate(gat('pool_s'), 1)
    win_s = np.concatenate(gat('win_s'), 1).reshape(2, nb * NSMP, 256, 2, 2, 64)
    cmp_s = np.concatenate(gat('cmp_s'), 1).reshape(2, nb * NSMP, 1, 2, 2, 64)
    slc_s = np.concatenate(gat('slc_s'), 1).reshape(2, nb * NSMP, 1, 2, 2, 64)
    return (y_p, y_s, conv_p, pool_p, win_p, cmp_p, slc_p, conv_s, pool_s, win_s, cmp_s, slc_s)
```

```python
import numpy as np
from contextlib import ExitStack
import concourse.bass as bass
import concourse.mybir as mybir
from concourse.bass_utils import run_bass_kernel_spmd

F32 = mybir.dt.float32
BF16 = mybir.dt.bfloat16
I32 = mybir.dt.int32
AF = mybir.ActivationFunctionType
ALU = mybir.AluOpType
AX = mybir.AxisListType

NCORES = 8
T = 2048
DM = 1024
DIN = 3096
NT = T // 128
NSMP = 16
EPS = 1e-6
NEGV = -30000.0
import os as _os
STOP = _os.environ.get('KSTOP', '')


class StopBuild(Exception):
    pass


def stop_at(tag):
    if STOP == tag:
        raise StopBuild(tag)
SEM_EPOCH = 8000
N_DMA_SEMS = 12


class Res:
    def __init__(self, name, t=None):
        self.name = name
        self.t = t
        self.last_writer = None
        self.readers = {}
        self.excl = False

    def __getitem__(self, idx):
        return View(self.t[idx], self)

    def ap(self, dtype=None):
        a = self.t[:]
        if dtype is not None and dtype != F32:
            a = a.bitcast(dtype)
        return View(a, self)


class View:
    def __init__(self, ap, res):
        self.ap = ap
        self.res = res

    def __getitem__(self, idx):
        return View(self.ap[idx], self.res)

    def rr(self, s, **kw):
        return View(self.ap.rearrange(s, **kw), self.res)

    def bc(self, shape):
        return View(self.ap.to_broadcast(list(shape)), self.res)

    def us(self, axis):
        return View(self.ap.unsqueeze(axis), self.res)

    def bitcast(self, dt):
        return View(self.ap.bitcast(dt), self.res)


class Sched:
    ENG = ('pe', 'act', 'dve', 'pool', 'sp')

    def __init__(self, nc, es):
        self.nc = nc
        self.es = es
        self.streams = {e: [] for e in self.ENG}
        self.count = {e: 0 for e in self.ENG}
        self.waited = {e: {} for e in self.ENG}
        self.pending = {e: [] for e in self.ENG}
        self.sems = {}
        self.dma_sems = {e: [] for e in self.ENG}
        self.dma_rr = {e: 0 for e in self.ENG}
        self.out_events = []
        self.last_ev = {}
        self.nops = 0
        self.ps_rr = 0

    def sbuf(self, name, shape, dtype):
        t = self.es.enter_context(self.nc.sbuf_tensor(name, list(shape), dtype))
        return Res(name, t)

    def psum(self, name):
        t = self.es.enter_context(self.nc.psum_tensor(name, [128, 512], F32))
        r = Res(name, t)
        r.excl = True
        return r

    def _sem(self, key):
        if key not in self.sems:
            nm = "s_" + "_".join(str(k) for k in key)
            self.sems[key] = self.es.enter_context(self.nc.semaphore(nm))
        return self.sems[key]

    def _deps(self, reads, writes):
        deps = []
        for r in reads:
            if r.last_writer is not None:
                deps.append(r.last_writer)
            if r.excl:
                deps.extend(r.readers.items())
        for w in writes:
            if w.last_writer is not None:
                deps.append(w.last_writer)
            deps.extend(w.readers.items())
        return deps

    def _waits_for(self, eng, deps):
        need = {}
        wd = self.waited[eng]
        for k, v in deps:
            if wd.get(k, 0) >= v:
                continue
            if need.get(k, 0) < v:
                need[k] = v
        for k, v in need.items():
            wd[k] = v
        return list(need.items())

    def _commit(self, ev, reads, writes):
        k, v = ev
        for r in reads:
            if r.readers.get(k, 0) < v:
                r.readers[k] = v
        for w in writes:
            w.last_writer = ev
            w.readers = {}
        self.last_ev[k] = max(self.last_ev.get(k, 0), v)

    def op(self, eng, fn, reads=(), writes=()):
        deps = self._deps(reads, writes) + self.pending[eng]
        self.pending[eng] = []
        if eng == 'pe':
            deps = [d for d in deps if d[0][0] != 'pe']
        waits = self._waits_for(eng, deps)
        self.count[eng] += 1
        c = self.count[eng]
        key = (eng, (c - 1) // SEM_EPOCH)
        val = (c - 1) % SEM_EPOCH + 1
        self._sem(key)
        ev = (key, val)
        self.streams[eng].append((waits, fn, key, 1))
        self._commit(ev, reads, writes)
        self.nops += 1
        return ev

    def dma(self, q, out_ap, in_ap, reads=(), writes=(), out=False, fn=None):
        deps = self._deps(reads, writes) + self.pending[q]
        self.pending[q] = []
        pool = self.dma_sems[q]
        if len(pool) < N_DMA_SEMS:
            ent = [('d' + q, len(pool)), 0]
            pool.append(ent)
        else:
            ent = pool[self.dma_rr[q] % N_DMA_SEMS]
            self.dma_rr[q] += 1
            deps.append((ent[0], ent[1]))
        waits = self._waits_for(q, deps)
        ent[1] += 16
        key = ent[0]
        self._sem(key)
        ev = (key, ent[1])
        if fn is None:
            fn = lambda e: e.dma_start(out=out_ap, in_=in_ap)
        self.streams[q].append((waits, fn, key, 16))
        self._commit(ev, reads, writes)
        if out:
            self.out_events.append(ev)
        self.nops += 1
        return ev

    def barrier(self):
        evs = list(self.last_ev.items())
        for e in self.ENG:
            self.pending[e] = list(evs)

    def finish(self):
        nc = self.nc
        fin = self._waits_for('sp', self.out_events)
        streams = self.streams
        sems = self.sems

        def emit(engobj, name, final_waits=()):
            for waits, fn, key, inc in streams[name]:
                for k, v in waits:
                    engobj.wait_ge(sems[k], v)
                fn(engobj).then_inc(sems[key], inc)
            for k, v in final_waits:
                engobj.wait_ge(sems[k], v)

        with nc.Block() as block:
            @block.sync
            def _(e):
                emit(e, 'sp', fin)

            @block.tensor
            def _(e):
                emit(e, 'pe')

            @block.scalar
            def _(e):
                emit(e, 'act')

            @block.vector
            def _(e):
                emit(e, 'dve')

            @block.gpsimd
            def _(e):
                emit(e, 'pool')


def _rs(views):
    out = []
    for v in views:
        if v is None or isinstance(v, (int, float)):
            continue
        r = v.res if isinstance(v, View) else v
        if r not in out:
            out.append(r)
    return out


class K:
    def __init__(self, S):
        self.S = S

    def tt(self, eng, out, a, b, op):
        self.S.op(eng, lambda e: e.tensor_tensor(out.ap, a.ap, b.ap, op), reads=_rs([a, b]), writes=_rs([out]))

    def ts(self, eng, out, a, s1, s2, op0, op1=None):
        s1a = s1.ap if isinstance(s1, View) else s1
        s2a = s2.ap if isinstance(s2, View) else s2
        if op1 is None:
            self.S.op(eng, lambda e: e.tensor_scalar(out.ap, a.ap, s1a, s2a, op0), reads=_rs([a, s1, s2]), writes=_rs([out]))
        else:
            self.S.op(eng, lambda e: e.tensor_scalar(out.ap, a.ap, s1a, s2a, op0, op1), reads=_rs([a, s1, s2]), writes=_rs([out]))

    def stt(self, eng, out, a, s, b, op0, op1):
        sa = s.ap if isinstance(s, View) else s
        self.S.op(eng, lambda e: e.scalar_tensor_tensor(out.ap, a.ap, sa, b.ap, op0, op1), reads=_rs([a, s, b]), writes=_rs([out]))

    def copy(self, eng, out, a):
        if eng == 'act':
            self.S.op(eng, lambda e: e.copy(out.ap, a.ap), reads=_rs([a]), writes=_rs([out]))
        else:
            self.S.op(eng, lambda e: e.tensor_copy(out.ap, a.ap), reads=_rs([a]), writes=_rs([out]))

    def memset(self, eng, out, val):
        self.S.op(eng, lambda e: e.memset(out.ap, val), writes=_rs([out]))

    def act(self, out, a, func, bias=None, scale=None, accum=None):
        kw = {}
        if bias is not None:
            kw['bias'] = bias.ap if isinstance(bias, View) else bias
        if scale is not None:
            kw['scale'] = scale.ap if isinstance(scale, View) else scale
        if accum is not None:
            kw['accum_out'] = accum.ap
        self.S.op('act', lambda e: e.activation(out.ap, a.ap, func, **kw), reads=_rs([a, bias, scale]), writes=_rs([out, accum]))

    def red(self, eng, out, a, op=ALU.add):
        self.S.op(eng, lambda e: e.tensor_reduce(out.ap, a.ap, AX.X, op), reads=_rs([a]), writes=_rs([out]))

    def recip(self, out, a):
        self.S.op('dve', lambda e: e.reciprocal(out.ap, a.ap), reads=_rs([a]), writes=_rs([out]))

    def max8(self, out, a):
        self.S.op('dve', lambda e: e.max(out=out.ap, in_=a.ap), reads=_rs([a]), writes=_rs([out]))

    def mm(self, out, lhsT, rhs, start, stop):
        self.S.op('pe', lambda e: e.matmul(out.ap, lhsT.ap, rhs.ap, start=start, stop=stop), reads=_rs([lhsT, rhs]), writes=_rs([out]))

    def tr(self, out, a, ident):
        self.S.op('pe', lambda e: e.transpose(out.ap, a.ap, ident.ap), reads=_rs([a, ident]), writes=_rs([out]))

    def dma(self, q, out, a, outp=False):
        self.S.dma(q, out.ap, a.ap, reads=_rs([a]), writes=_rs([out]), out=outp)

    def rsqrt(self, out, a, scale, eps):
        self.act(out, a, AF.Sqrt, bias=eps, scale=scale)
        self.recip(out, out)


def _consts():
    c = {}
    half = 8
    inv = 500000.0 ** (-np.arange(half, dtype=np.float32) * 2.0 / 16.0)
    pos = np.arange(T + 1, dtype=np.float32)
    ang = pos[:, None] * inv[None, :].astype(np.float32)
    cs = np.concatenate([np.cos(ang), np.sin(ang)], axis=1).astype(np.float32)
    csp = cs[:T].reshape(NT, 128, 16).transpose(1, 0, 2)
    c['cs_p'] = np.ascontiguousarray(csp)
    c['cs_s'] = np.ascontiguousarray(np.broadcast_to(cs[T][None, :], (NSMP, 16)))
    p = np.arange(128)[:, None]
    i = np.arange(128)[None, :]
    tri = np.where(p <= i, 0.0, NEGV).astype(np.float32)
    triu = np.where(p > i, 0.0, NEGV).astype(np.float32)
    c['tri4'] = np.ascontiguousarray(np.tile(tri, (1, 4)))
    c['triu4'] = np.ascontiguousarray(np.tile(triu, (1, 4)))
    cm = np.zeros((128, 128), np.float32)
    for cp in range(8):
        cm[cp] = np.where(np.arange(128) >= 16 * cp + 15, 0.0, NEGV)
    c['cmrel4'] = np.ascontiguousarray(np.tile(cm, (1, 4)))
    sh = np.zeros((128, NT, 127), np.float32)
    for qt in range(NT):
        for cp in range(8):
            cc = 8 * qt - 1 + cp
            if 0 <= cc < 127:
                sh[cp, qt, cc] = 1.0
    c['shift'] = sh
    nf = np.zeros((128, NT, 32), np.float32)
    addc = np.zeros((128, NT, 32), np.float32)
    for qt in range(NT):
        for ii in range(128):
            cur = (qt * 128 + ii) // 64
            for j in range(32):
                if j > cur:
                    addc[ii, qt, j] = -1e4
                elif j == 0 or j == cur or j == cur - 1:
                    addc[ii, qt, j] = 1e4
                else:
                    nf[ii, qt, j] = 1.0
    c['nf'] = nf
    c['addc'] = addc
    ec = np.zeros((32, T), np.float32)
    for j in range(32):
        ec[j, j * 64:(j + 1) * 64] = 1.0
    c['ec'] = ec
    def overlap(n_cmp, n_slc):
        cs_ = np.arange(n_cmp) * 16
        ce = cs_ + 31
        ss = np.arange(n_slc) * 64
        return ((ce[:, None] >= ss[None, :]) & (cs_[:, None] < ss[None, :] + 64)).astype(np.float32)
    c['ov_p'] = overlap(127, 32)
    c['ov_s'] = overlap(127, 33)
    invc = np.zeros((128, 2, 16), np.float32)
    for ch in range(2):
        for pp in range(128):
            w = (2, 4, 8, 16)[ch * 2 + pp // 64]
            invc[pp, ch] = 1.0 / np.minimum(w, np.arange(16) + 1)
    c['invc'] = invc
    grp = np.zeros((128, 32), np.float32)
    for n in range(16):
        for h in range(8):
            grp[n * 8 + h, n * 2 + h // 4] = 1.0
    c['grp'] = grp
    nfs = np.ones((32, 33), np.float32)
    adds = np.zeros((32, 33), np.float32)
    for j in (0, 31, 32):
        nfs[:, j] = 0.0
        adds[:, j] = 1e4
    c['nfs'] = nfs
    c['adds'] = adds
    e33 = np.zeros((33, 128), np.float32)
    for pc in range(128):
        e33[pc // 4, pc] = 1.0
    c['e33'] = e33
    sel8 = np.zeros((16, 128), np.float32)
    for pc in range(128):
        sel8[pc // 8, pc] = 1.0
    c['sel8'] = sel8
    c['pmod8'] = (np.arange(128) % 8).astype(np.float32).reshape(128, 1)
    return c


CONST_SHAPES = None


def _layout_params(inp):
    d = {}
    d['wnorm'] = np.ascontiguousarray(inp['w_norm'].reshape(2, 8, 128).transpose(0, 2, 1))
    d['wdw'] = np.ascontiguousarray(inp['w_dw'].reshape(2, 31, 2, 128).transpose(0, 3, 2, 1))
    def pc(a):
        return a.reshape(2, 2, 128).transpose(0, 2, 1)
    d['cvec'] = np.ascontiguousarray(np.stack([pc(inp['b_dw']), pc(inp['ln_g']), pc(inp['ln_b']),
                                               pc(inp['b_pw']), pc(inp['pool_scale'])], axis=-1))
    wp = inp['w_pool']
    bd = np.zeros((2, 2, 128, 128), np.float32)
    for g in range(4):
        ch, o = g // 2, (g % 2) * 64
        bd[:, ch, o:o + 64, o:o + 64] = wp[:, g]
    d['wpool_bd'] = np.ascontiguousarray(bd.transpose(0, 2, 1, 3))
    d['pe_t'] = np.ascontiguousarray(inp['cmp_pe'].transpose(0, 3, 1, 2))
    w1 = inp['cmp_w1'].reshape(2, 2, 32, 64, 64)
    d['w1h'] = np.ascontiguousarray(w1.transpose(0, 3, 1, 2, 4))
    w1p = inp['cmp_w1'].reshape(2, 2, 16, 128, 64)
    d['w1p'] = np.ascontiguousarray(w1p.transpose(0, 3, 1, 2, 4))
    d['w2'] = np.ascontiguousarray(inp['cmp_w2'].transpose(0, 2, 1, 3))
    d['rows_s'] = np.ascontiguousarray(np.stack([inp['b_dw'], inp['ln_g'], inp['ln_b'], inp['b_pw'],
                                                 inp['pool_scale']], axis=1))
    return d


def build_nc(n_phys, do_prompt=True, do_sample=True, depth=2):
    nc = bass.Bass("TRN2", target_bir_lowering=False)
    consts = _consts()
    D = {}

    def din(name, shape, dt=F32):
        D[name] = nc.dram_tensor(name, list(shape), dt, kind="ExternalInput").ap()
        return D[name]

    def dout(name, shape):
        D[name] = nc.dram_tensor(name, list(shape), F32, kind="ExternalOutput").ap()
        return D[name]

    def dscr(name, shape, dt=F32):
        D[name] = nc.dram_tensor(name, list(shape), dt, kind="Internal").ap()
        return D[name]

    din('xp', [T, DM]); din('xs', [NSMP, DM])
    din('w_in', [2, DM, DIN]); din('w_out', [2, DM, DM])
    din('wnorm', [2, 128, 8]); din('wdw', [2, 128, 2, 31]); din('cvec', [2, 128, 2, 5])
    din('w_pw', [2, 256, 256]); din('wpool_bd', [2, 128, 2, 128])
    din('g_q', [2, 64]); din('g_k', [2, 3, 64])
    din('pe_t', [2, 64, 2, 32]); din('w1h', [2, 64, 2, 32, 64]); din('w1p', [2, 128, 2, 16, 64]); din('w2', [2, 64, 2, 64])
    din('rows_s', [2, 5, 256]); din('w_dw', [2, 31, 256])
    din('state_conv', [2, NSMP, 30, 256]); din('state_pool', [2, NSMP, 15, 256])
    din('cache_win', [2, NSMP, 256, 256])
    din('cache_cmp', [2 * n_phys * 8, 4096]); din('cache_slc', [2 * n_phys * 8, 4096])
    din('ptab_t', [16, NSMP], I32)
    for k, v in consts.items():
        din('c_' + k, v.shape)
    dout('y_p', [T, DM]); dout('y_s', [NSMP, DM])
    dout('conv_p', [2, 30, 256]); dout('pool_p', [2, 15, 256]); dout('win_p', [2, 256, 256])
    dout('cmp_p', [2, T, 256]); dout('slc_p', [2, T, 256])
    dout('conv_s', [2, NSMP, 30, 256]); dout('pool_s', [2, NSMP, 15, 256]); dout('win_s', [2, NSMP, 256, 256])
    dout('cmp_s', [2, NSMP, 256]); dout('slc_s', [2, NSMP, 256])
    dscr('x1', [T, DM]); dscr('xs1', [NSMP, DM]); dscr('pjs_d', [NSMP, DIN]); dscr('nh_d', [NSMP, 8 * 387]); dscr('mixnh_d', [128, 64])
    dscr('qb_d', [T, 512], BF16); dscr('szc_d', [128, 4, T], BF16); dscr('mix_d', [128, 4, T], BF16)

    with ExitStack() as es:
        S = Sched(nc, es)
        k = K(S)
        RDICT = {}
        def RD(name, idx):
            if (name, idx) not in RDICT:
                RDICT[(name, idx)] = Res('%s_%s' % (name, idx))
            return RDICT[(name, idx)]
        def DV(name, res=None):
            return View(D[name], res if res is not None else Res('dram_' + name))
        PS = [S.psum("ps%d" % i) for i in range(8)]

        def ps_next():
            b = PS[S.ps_rr % 8]
            S.ps_rr += 1
            return b

        try:
          identf = S.sbuf("identf", [128, 128], F32)
          identb = S.sbuf("identb", [128, 128], BF16)
          onesb = S.sbuf("onesb", [128, 128], BF16)
          k.memset('pool', identf[:], 0.0)
          S.op('pool', lambda e: e.affine_select(identf.t[:], identf.t[:], pattern=[[-1, 128]], compare_op=ALU.not_equal,
                                                 fill=1.0, base=0, channel_multiplier=1), reads=[identf], writes=[identf])
          k.copy('dve', identb[:], identf[:])
          k.memset('pool', onesb[:], 1.0)
          stop_at('s1')
          stage = [S.sbuf("stage%d" % i, [128, 1024], F32) for i in range(2)]
          st_rr = [0]

          def load_const_bf(name, shape_free):
              src = D['c_' + name]
              npart = src.shape[0]
              t = S.sbuf("k_" + name, [128] + list(shape_free), BF16)
              nfree = int(np.prod(shape_free))
              sb = stage[st_rr[0] % 2]; st_rr[0] += 1
              flat_src = src if len(src.shape) == 2 else src.rearrange("p a b -> p (a b)")
              k.dma('sp', sb[0:npart, 0:nfree], View(flat_src, Res('c')))
              dst = t[0:npart]
              if len(shape_free) == 2:
                  dst = dst.rr("p a b -> p (a b)")
              k.copy('dve', dst, sb[0:npart, 0:nfree])
              return t

          def load_const_f(name, shape_free, npart=128):
              src = D['c_' + name]
              t = S.sbuf("k_" + name, [128] + list(shape_free), F32)
              k.dma('sp', t[0:npart], View(src, Res('c')))
              return t

          if do_prompt:
              CS = load_const_f('cs_p', [NT, 16])
              TRI4 = load_const_bf('tri4', [512])
              TRIU4 = load_const_bf('triu4', [512])
              CMREL4 = load_const_bf('cmrel4', [512])
              SHIFT = S.sbuf("k_shift", [128, NT, 127], BF16)
              for h0 in range(0, NT, 8):
                  sb = stage[st_rr[0] % 2]; st_rr[0] += 1
                  k.dma('sp', sb[:, 0:8 * 127], View(D['c_shift'][:, h0:h0 + 8, :].rearrange("p a b -> p (a b)"), Res('c')))
                  k.copy('dve', SHIFT[:, h0:h0 + 8, :].rr("p a b -> p (a b)"), sb[:, 0:8 * 127])
              NF = load_const_f('nf', [NT, 32])
              ADDC = load_const_f('addc', [NT, 32])
              INVC = load_const_f('invc', [2, 16])
              stop_at('s2')
              KT2 = S.sbuf("KT2", [128, 3, T], BF16)
              KA = S.sbuf("KA", [128, 2, T], BF16)
              VX = S.sbuf("VX", [128, NT, 2, 2, 65], BF16)
              GT = S.sbuf("GT", [128, NT, 24], F32)
              VCX = S.sbuf("VCX", [128, 2, 97], BF16)
              KC2 = S.sbuf("KC2", [128, 127], BF16)
              k.memset('pool', KA[:], 0.0)
              k.memset('pool', VX[:], 1.0)
              k.memset('pool', VCX[:], 1.0)
              for h0 in range(0, T, 1024):
                  sb = stage[st_rr[0] % 2]; st_rr[0] += 1
                  k.dma('sp', sb[64:96, 0:1024], View(D['c_ec'][:, h0:h0 + 1024], Res('c')))
                  k.dma('sp', sb[0:32, 0:1024], View(D['c_ec'][:, h0:h0 + 1024], Res('c')))
                  k.copy('dve', KA[64:96, 0, h0:h0 + 1024], sb[64:96, 0:1024])
                  k.copy('dve', KA[0:32, 1, h0:h0 + 1024], sb[0:32, 0:1024])
              sb = stage[st_rr[0] % 2]; st_rr[0] += 1
              k.dma('sp', sb[0:127, 0:32], View(D['c_ov_p'], Res('c')))
              for g in range(2):
                  k.copy('dve', VCX[0:127, g, 65:97], sb[0:127, 0:32])
              QZ = [S.sbuf("QZ%d" % g, [128, 4, 128], BF16) for g in range(2)]
              QA = [S.sbuf("QA%d" % g, [128, 4, 128], BF16) for g in range(2)]
              for g in range(2):
                  k.memset('pool', QZ[g][:], 0.0)
                  k.memset('pool', QA[g][:], 0.0)

          stop_at('s3')
          ARENA_F32 = 37000
          arena = S.sbuf("arena", [128, ARENA_F32], F32)
          ar_off = [0]

          def ar_reset():
              ar_off[0] = 0

          def ar(name, shape, dt):
              nel = int(np.prod(shape))
              nwords = (nel * (4 if dt == F32 or dt == I32 else 2) + 3) // 4
              nwords = (nwords + 7) // 8 * 8
              a = arena.t[:, ar_off[0]:ar_off[0] + nwords]
              ar_off[0] += nwords
              assert ar_off[0] <= ARENA_F32, (name, ar_off[0])
              if dt != F32:
                  a = a.bitcast(dt)
              a = a[:, 0:nel]
              if len(shape) == 2:
                  a = a.rearrange("p (a b) -> p a b", a=shape[0])
              elif len(shape) == 3:
                  a = a.rearrange("p (a b c) -> p a b c", a=shape[0], b=shape[1])
              elif len(shape) == 4:
                  a = a.rearrange("p (a b c d) -> p a b c d", a=shape[0], b=shape[1], c=shape[2])
              return Res(name, a)

          for l in range(depth):
              def xsrc_t(ti, l=l):
                  rows = slice(ti * 128, (ti + 1) * 128)
                  return View(D['xp'][rows, :], Res('c')) if l == 0 else View(D['x1'][rows, :], RD('x1', ti))
              def xdst_t(ti, l=l):
                  rows = slice(ti * 128, (ti + 1) * 128)
                  return View(D['x1'][rows, :], RD('x1', ti)) if (l == 0 and depth > 1) else View(D['y_p'][rows, :], Res('c'))
              S.barrier(); ar_reset()
              WINB = ar("winb", [8, DIN], BF16)
              wn = ar("wn", [8], F32)
              k.dma('sp', wn[:], DV('wnorm')[l])
              cvec = ar("cvec", [2, 5], F32)
              k.dma('sp', cvec[:], DV('cvec')[l])
              wdw = ar("wdw", [2, 31], F32)
              k.dma('sp', wdw[:], DV('wdw')[l])
              gqb = ar("gqb", [64], F32)
              k.dma('sp', gqb[:], View(D['g_q'][l].partition_broadcast(128), Res('c')))
              gkb = ar("gkb", [3, 64], F32)
              k.dma('sp', gkb[:].rr("p a b -> p (a b)"), View(D['g_k'][l].rearrange("a b -> (a b)").partition_broadcast(128), Res('c')))
              stop_at('s4')
              ci = 0
              for kc in range(8):
                  for c0 in range(0, DIN, 1024):
                      cw = min(1024, DIN - c0)
                      if ci >= int(_os.environ.get('WMAX', '1000')):
                          continue
                      sb = stage[st_rr[0] % 2]; st_rr[0] += 1
                      k.dma('sp', sb[:, 0:cw], DV('w_in')[l, kc * 128:(kc + 1) * 128, c0:c0 + cw])
                      wvar = _os.environ.get('WVAR', 'mix')
                      if wvar == 'none':
                          pass
                      elif wvar == 'act' or (wvar == 'mix' and ci % 2 == 0):
                          k.act(WINB[:, kc, c0:c0 + cw], sb[:, 0:cw], AF.Identity, scale=wn[:, kc:kc + 1])
                      else:
                          k.ts('dve', WINB[:, kc, c0:c0 + cw], sb[:, 0:cw], wn[:, kc:kc + 1], None, ALU.mult)
                      ci += 1
              stop_at('w')
              mark = ar_off[0]
              if do_sample:
                  sample_s0(nc, S, k, D, DV, l, locals())
                  S.barrier(); ar_off[0] = mark
              stop_at('s0')
              if do_prompt:
                  prompt_phase_a(nc, S, k, D, DV, l, locals())
              S.barrier(); ar_reset()
              WOUTB = ar("woutb", [8, DM], BF16)
              mark_w = ar_off[0]
              for kc in range(8):
                  sb = stage[st_rr[0] % 2]; st_rr[0] += 1
                  k.dma('sp', sb[:], DV('w_out')[l, kc * 128:(kc + 1) * 128, :])
                  if kc % 2 == 0:
                      k.copy('act', WOUTB[:, kc, :], sb[:])
                  else:
                      k.copy('dve', WOUTB[:, kc, :], sb[:])
              if do_prompt:
                  prompt_phase_c(nc, S, k, D, DV, l, locals())
              stop_at('pc')
              if do_sample:
                  S.barrier(); ar_off[0] = mark_w
                  sample_phase_s(nc, S, k, D, DV, l, locals())
              stop_at('L0end')
        except StopBuild as ex:
            print('build stopped at', ex)
        S.finish()
    return nc


def prompt_phase_a(nc, S, k, D, DV, l, E):
    ar = E['ar']; WINB = E['WINB']; PS = E['PS']; ps_next = E['ps_next']
    identb = E['identb']; identf = E['identf']; onesb = E['onesb']
    cvec = E['cvec']; wdw = E['wdw']; gqb = E['gqb']; gkb = E['gkb']
    CS = E['CS']; KT2 = E['KT2']; KA = E['KA']; VX = E['VX']; GT = E['GT']; INVC = E['INVC']
    VCX = E['VCX']; KC2 = E['KC2']
    xsrc_t = E['xsrc_t']; stage = E['stage']; st_rr = E['st_rr']; RD = E['RD']
    BT = 4
    BW = BT * 128
    NB = NT // BT
    wpw = ar("wpw", [2, 256], BF16)
    wpl = ar("wpl", [2, 128], BF16)
    for kc in range(2):
        sb = stage[st_rr[0] % 2]; st_rr[0] += 1
        k.dma('sp', sb[:, 0:256], DV('w_pw')[l, kc * 128:(kc + 1) * 128, :])
        k.copy('dve', wpw[:, kc, :], sb[:, 0:256])
    sb = stage[st_rr[0] % 2]; st_rr[0] += 1
    k.dma('sp', sb[:, 0:256], View(D['wpool_bd'][l].rearrange("p a b -> p (a b)"), Res('c')))
    k.copy('dve', wpl[:].rr("p a b -> p (a b)"), sb[:, 0:256])
    XT = [ar("XT%d" % i, [DM], F32) for i in range(2)]
    junk = ar("junk", [DM], BF16)
    hb = ar("hb", [DM], BF16)
    ssq = ar("ssq", [8], F32)
    HT = ar("HT", [8, BW], BF16)
    PJ = ar("PJ", [1304], F32)
    SQ2 = ar("SQ2", [1280], F32)
    SS20 = ar("SS20", [20], F32)
    RT = ar("RT", [6, 8, 8], F32)
    KB = ar("KB", [4, 128], BF16)
    QBt = ar("QBt", [512], BF16)
    AV = ar("AV", [2, BW], F32)
    SG = ar("SG", [2, BW], F32)
    GLUB = ar("GLUB", [2, 32 + BW], BF16)
    BIN = ar("BIN", [2, 16 + BW], F32)
    SZA = ar("SZA", [2, BW], BF16)
    SZB = ar("SZB", [2, BW], BF16)
    SZC = ar("SZCb", [4, BW], BF16)
    MIXB = ar("MIXB", [4, BW], BF16)
    DG = ar("DG", [31, 128], BF16)
    CV = ar("CV", [2, BW], F32)
    CVB = ar("CVB", [2, BW], BF16)
    SQB = ar("SQB", [2, BW], BF16)
    MEAN = ar("MEAN", [BW], F32)
    VAR = ar("VAR", [BW], F32)
    ACTV = ar("ACTV", [2, BW], BF16)
    SA = ar("SA", [16 + BW], F32)
    SBb = ar("SBb", [16 + BW], F32)
    DB = ar("DB", [2, BW], BF16)
    TOUT = ar("TOUT", [256], F32)
    k.memset('pool', GLUB[:], 0.0)
    k.memset('pool', BIN[:], 0.0)

    for tb in range(NB):
        for tt in range(BT):
            ti = tb * BT + tt
            xt = XT[ti % 2]
            k.dma('sp', xt[:], xsrc_t(ti))
            stop_at('t0')
            k.memset('dve', ssq[:, 0:1], 0.0)
            k.act(junk[:], xt[:], AF.Square, accum=ssq[:, 0:1])
            k.rsqrt(ssq[:, 1:2], ssq[:, 0:1], 1.0 / DM, EPS)
            k.ts('dve', hb[:], xt[:], ssq[:, 1:2], None, ALU.mult)
            stop_at('t1')
            pst = ps_next()
            for kc in range(8):
                k.tr(pst.ap(BF16)[:, kc * 128:(kc + 1) * 128], hb[:, kc * 128:(kc + 1) * 128], identb[:])
            k.copy('act', HT[:, :, tt * 128:(tt + 1) * 128], pst.ap(BF16)[:, 0:1024].rr("p (a b) -> p a b", a=8))
            stop_at('t2')
            for (c0, cw, o0) in ((1280, 512, 0), (1792, 512, 512), (2304, 280, 1024)):
                pp = ps_next()
                for kc in range(8):
                    k.mm(pp.ap()[:, 0:cw], HT[:, kc, tt * 128:(tt + 1) * 128], WINB[:, kc, c0:c0 + cw], kc == 0, kc == 7)
                k.copy('act' if o0 != 512 else 'dve', PJ[:, o0:o0 + cw], pp.ap()[:, 0:cw])
            stop_at('t3')
            k.act(SQ2[:], PJ[:, 0:1280], AF.Square)
            k.red('dve', SS20[:], SQ2[:].rr("p (h d) -> p h d", d=64))
            k.rsqrt(SS20[:], SS20[:], 1.0 / 64, EPS)
            Q3 = PJ[:, 0:512].rr("p (h d) -> p h d", d=64)
            k.tt('dve', Q3, Q3, SS20[:, 0:8].us(2).bc([128, 8, 64]), ALU.mult)
            k.tt('dve', Q3, Q3, gqb[:].us(1).bc([128, 8, 64]), ALU.mult)
            K3 = []
            for b in range(3):
                kk = PJ[:, 512 + b * 256:512 + b * 256 + 128].rr("p (g d) -> p g d", d=64)
                k.tt('dve', kk, kk, SS20[:, 8 + b * 4:8 + b * 4 + 2].us(2).bc([128, 2, 64]), ALU.mult)
                k.tt('dve', kk, kk, gkb[:, b, :].us(1).bc([128, 2, 64]), ALU.mult)
                K3.append(kk)
            stop_at('t4')
            cos8 = CS[:, ti, 0:8]
            sin8 = CS[:, ti, 8:16]
            def rope(X, nh, eng):
                x1 = X[:, :, 0:8]; x2 = X[:, :, 8:16]
                cb = cos8.us(1).bc([128, nh, 8]); sbv = sin8.us(1).bc([128, nh, 8])
                k.tt(eng, RT[:, 0, 0:nh, :], x1, cb, ALU.mult)
                k.tt(eng, RT[:, 1, 0:nh, :], x2, sbv, ALU.mult)
                k.tt(eng, RT[:, 2, 0:nh, :], x2, cb, ALU.mult)
                k.tt(eng, RT[:, 3, 0:nh, :], x1, sbv, ALU.mult)
                k.tt(eng, x1, RT[:, 0, 0:nh, :], RT[:, 1, 0:nh, :], ALU.subtract)
                k.tt(eng, x2, RT[:, 2, 0:nh, :], RT[:, 3, 0:nh, :], ALU.add)
            rope(Q3, 8, 'dve')
            for b in range(3):
                rope(K3[b], 2, 'dve')
            stop_at('t5')
            k.dma('sp', DV('cmp_p')[l, ti * 128:(ti + 1) * 128, :], PJ[:, 512:768], outp=True)
            k.dma('sp', DV('slc_p')[l, ti * 128:(ti + 1) * 128, :], PJ[:, 768:1024], outp=True)
            if ti >= NT - 2:
                k.dma('sp', DV('win_p')[l, (ti - NT + 2) * 128:(ti - NT + 3) * 128, :], PJ[:, 1024:1280], outp=True)
            k.copy('act', QBt[:].rr("p (r g d) -> p r g d", r=4, g=2), PJ[:, 0:512].rr("p (g r d) -> p r g d", g=2, r=4))
            k.dma('sp', View(D['qb_d'][ti * 128:(ti + 1) * 128, :], RD('qb', ti)), QBt[:])
            k.act(GT[:, ti, :], PJ[:, 1280:1304], AF.Sigmoid)
            stop_at('t6')
            k.copy('dve', KB[:, 0:3, :], PJ[:, 512:1280].rr("p (b x) -> p b x", b=3)[:, :, 0:128])
            k.copy('act', KB[:, 3, :], PJ[:, 640:768])
            stop_at('t6a')
            pk = ps_next()
            for j in range(4):
                k.tr(pk.ap(BF16)[:, j * 128:(j + 1) * 128], KB[:, j, :], identb[:])
            stop_at('t6b')
            pkv = pk.ap(BF16)
            cols = slice(ti * 128, (ti + 1) * 128)
            k.copy('act', KT2[:, 0, cols], pkv[:, 0:128])
            k.copy('act', KT2[:, 1, cols], pkv[:, 256:384])
            k.copy('act', KT2[:, 2, cols], pkv[:, 384:512])
            stop_at('t6c')
            k.copy('dve', KA[0:64, 0, cols], pkv[0:64, 128:256])
            stop_at('t6d')
            k.copy('dve', KA[64:128, 1, cols], pkv[64:128, 128:256])
            stop_at('t7')
            k.copy('act', VX[:, ti, :, :, 0:64],
                   PJ[:, 768:1280].rr("p (b x) -> p b x", b=2)[:, :, 128:256].rr("p b (g d) -> p b g d", g=2))
        stop_at('a1')
        t0 = tb * BW
        def fm_proj(c0):
            pp = ps_next()
            for kc in range(8):
                k.mm(pp.ap()[:, 0:BW], WINB[:, kc, c0:c0 + 128], HT[:, kc, :], kc == 0, kc == 7)
            return pp
        for c in range(2):
            pp = fm_proj(c * 128)
            k.copy('dve', AV[:, c, :], pp.ap()[:, 0:BW])
            pp = fm_proj(256 + c * 128)
            k.act(SG[:, c, :], pp.ap()[:, 0:BW], AF.Sigmoid)
            k.tt('dve', AV[:, c, :], AV[:, c, :], SG[:, c, :], ALU.mult)
            k.copy('act', GLUB[:, c, 32:32 + BW], AV[:, c, :])
            pp = fm_proj(512 + c * 128)
            k.act(SZA[:, c, :], pp.ap()[:, 0:BW], AF.Silu)
            pp = fm_proj(768 + c * 128)
            k.copy('dve', BIN[:, c, 16:16 + BW], pp.ap()[:, 0:BW])
            pp = fm_proj(1024 + c * 128)
            k.act(SZB[:, c, :], pp.ap()[:, 0:BW], AF.Silu)
        for c in range(4):
            pp = fm_proj(2584 + c * 128)
            k.act(SZC[:, c, :], pp.ap()[:, 0:BW], AF.Silu)
        k.dma('sp', View(D['szc_d'][:, :, t0:t0 + BW], RD('szc', tb)), SZC[:])
        if tb == NB - 1:
            pp = ps_next()
            for c in range(2):
                k.tr(pp.ap()[:, c * 128:(c + 1) * 128], AV[:, c, BW - 128:BW], identf[:])
            k.copy('dve', TOUT[:], pp.ap()[:, 0:256])
            k.dma('sp', DV('conv_p')[l], TOUT[98:128, :], outp=True)
            pp = ps_next()
            for c in range(2):
                k.tr(pp.ap()[:, c * 128:(c + 1) * 128], BIN[:, c, 16 + BW - 128:16 + BW], identf[:])
            k.copy('dve', TOUT[:], pp.ap()[:, 0:256])
            k.dma('sp', DV('pool_p')[l], TOUT[113:128, :], outp=True)
        stop_at('a2p')
        for c in range(2):
            for j in range(31):
                (k.act(DG[:, j, :], identf[:], AF.Identity, scale=wdw[:, c, j:j + 1]) if j % 2 == 0 else k.ts('dve', DG[:, j, :], identf[:], wdw[:, c, j:j + 1], None, ALU.mult))
            pp = ps_next()
            for j in range(31):
                k.mm(pp.ap()[:, 0:BW], DG[:, j, :], GLUB[:, c, 2 + j:2 + j + BW], j == 0, j == 30)
            k.act(CV[:, c, :], pp.ap()[:, 0:BW], AF.Identity, bias=cvec[:, c, 0:1])
            k.copy('dve', CVB[:, c, :], CV[:, c, :])
            k.act(SQB[:, c, :], CV[:, c, :], AF.Square)
        p1 = ps_next()
        for c in range(2):
            k.mm(p1.ap()[:, 0:BW], onesb[:], CVB[:, c, :], c == 0, c == 1)
        k.act(MEAN[:], p1.ap()[:, 0:BW], AF.Identity, scale=1.0 / 256)
        p2 = ps_next()
        for c in range(2):
            k.mm(p2.ap()[:, 0:BW], onesb[:], SQB[:, c, :], c == 0, c == 1)
        k.tt('dve', VAR[:], MEAN[:], MEAN[:], ALU.mult)
        k.stt('dve', VAR[:], p2.ap()[:, 0:BW], 1.0 / 256, VAR[:], ALU.mult, ALU.subtract)
        k.ts('dve', VAR[:], VAR[:], 0.0, None, ALU.max)
        k.rsqrt(VAR[:], VAR[:], 1.0, EPS)
        for c in range(2):
            k.tt('dve', CV[:, c, :], CV[:, c, :], MEAN[:], ALU.subtract)
            k.tt('dve', CV[:, c, :], CV[:, c, :], VAR[:], ALU.mult)
            k.act(ACTV[:, c, :], CV[:, c, :], AF.Silu, bias=cvec[:, c, 2:3], scale=cvec[:, c, 1:2])
        for co in range(2):
            pp = ps_next()
            for ci_ in range(2):
                k.mm(pp.ap()[:, 0:BW], wpw[:, ci_, co * 128:(co + 1) * 128], ACTV[:, ci_, :], ci_ == 0, ci_ == 1)
            k.stt('dve', MIXB[:, co, :], pp.ap()[:, 0:BW], cvec[:, co, 3:4], SZA[:, co, :], ALU.add, ALU.mult)
        k.copy('dve', GLUB[:, :, 2:32], GLUB[:, :, 2 + BW:32 + BW])
        stop_at('a2c')
        for c in range(2):
            X = BIN[:, c, :]
            n = BW + 15
            k.tt('dve', SA[:, 1:16 + BW], X[:, 1:16 + BW], X[:, 0:15 + BW], ALU.add)
            k.tt('dve', SBb[:, 3:16 + BW], SA[:, 3:16 + BW], SA[:, 1:14 + BW], ALU.add)
            if c == 0:
                tot = (SA, SBb)
                ws = (2, 4)
            else:
                k.tt('dve', SA[:, 7:16 + BW], SBb[:, 7:16 + BW], SBb[:, 3:12 + BW], ALU.add)
                k.tt('dve', SBb[:, 15:16 + BW], SA[:, 15:16 + BW], SA[:, 7:8 + BW], ALU.add)
                tot = (SA, SBb)
                ws = (8, 16)
            for hf in range(2):
                pr = slice(hf * 64, hf * 64 + 64)
                k.ts('dve', SG[pr, c, :], tot[hf][pr, 16:16 + BW], 1.0 / ws[hf], None, ALU.mult)
                k.tt('dve', DB[pr, c, :], SG[pr, c, :], X[pr, 16:16 + BW], ALU.subtract)
                if tb == 0:
                    k.tt('dve', SG[pr, c, 0:16], tot[hf][pr, 16:32], INVC[pr, c, :], ALU.mult)
                    k.tt('dve', DB[pr, c, 0:16], SG[pr, c, 0:16], X[pr, 16:32], ALU.subtract)
            pp = ps_next()
            k.mm(pp.ap()[:, 0:BW], wpl[:, c, :], DB[:, c, :], True, True)
            k.stt('dve', MIXB[:, 2 + c, :], pp.ap()[:, 0:BW], cvec[:, c, 4:5], SZB[:, c, :], ALU.mult, ALU.mult)
        k.copy('dve', BIN[:, :, 1:16], BIN[:, :, 1 + BW:16 + BW])
        k.dma('sp', View(D['mix_d'][:, :, t0:t0 + BW], RD('mix', tb)), MIXB[:])

    stop_at('a')
    S.barrier(); E['ar_reset']()
    w1h = ar("w1h", [2, 32, 64], BF16)
    for kv in range(2):
        for rh in range(2):
            sb = stage[st_rr[0] % 2]; st_rr[0] += 1
            srcv = View(D['w1h'][l, :, kv, rh * 16:(rh + 1) * 16, :].rearrange("p a b -> p (a b)"), Res('c'))
            k.dma('sp', sb[0:64, 0:1024], srcv)
            k.dma('sp', sb[64:128, 0:1024], srcv)
            k.copy('dve', w1h[:, kv, rh * 16:(rh + 1) * 16, :].rr("p a b -> p (a b)"), sb[:, 0:1024])
    pet = ar("pet", [2, 32], BF16)
    sb = stage[st_rr[0] % 2]; st_rr[0] += 1
    k.dma('sp', sb[0:64, 0:64], View(D['pe_t'][l].rearrange("p a b -> p (a b)"), Res('c')))
    k.copy('dve', pet[0:64].rr("p a b -> p (a b)"), sb[0:64, 0:64])
    w2p = ar("w2p", [2, 2, 128], BF16)
    k.memset('pool', w2p[:], 0.0)
    sb = stage[st_rr[0] % 2]; st_rr[0] += 1
    k.dma('sp', sb[0:64, 0:128], View(D['w2'][l].rearrange("p a b -> p (a b)"), Res('c')))
    for kv in range(2):
        k.copy('dve', w2p[0:64, kv, 0, 0:64], sb[0:64, kv * 64:(kv + 1) * 64])
        k.copy('dve', w2p[0:64, kv, 1, 64:128], sb[0:64, kv * 64:(kv + 1) * 64])
    PEB = ar("PEB", [2], F32)
    for kv in range(2):
        pp = ps_next()
        for r in range(32):
            k.mm(pp.ap()[0:64, 0:1], w1h[0:64, kv, r, :], pet[0:64, kv, r:r + 1], r == 0, r == 31)
        k.copy('dve', PEB[0:64, kv:kv + 1], pp.ap()[0:64, 0:1])
    U = ar("U", [254], F32)
    U2 = ar("U2", [254], F32)
    H = ar("H", [2, 254], BF16)
    for kv in range(2):
        slot = 0 if kv == 0 else 2
        for g in range(2):
            pp = ps_next()
            pr = slice(g * 64, g * 64 + 64)
            for r in range(32):
                k.mm(pp.ap()[0:64, 0:127], w1h[pr, kv, r, :], KT2[pr, slot, r:r + 16 * 126 + 1:16], r == 0, r == 31)
            k.act(U[0:64, g * 127:(g + 1) * 127], pp.ap()[0:64, 0:127], AF.Identity, bias=PEB[0:64, kv:kv + 1])
        k.tt('dve', U2[0:64], U[0:64], U[0:64], ALU.mult)
        k.ts('dve', U2[0:64], U2[0:64], 0.044715, 1.0, ALU.mult, ALU.add)
        k.tt('dve', U2[0:64], U2[0:64], U[0:64], ALU.mult)
        k.act(U2[0:64], U2[0:64], AF.Sigmoid, scale=1.5957691216057308)
        k.tt('dve', H[0:64, kv, :], U[0:64], U2[0:64], ALU.mult)
    pp = ps_next()
    for g in range(2):
        k.mm(pp.ap()[:, 0:127], w2p[0:64, 0, g, :], H[0:64, 0, g * 127:(g + 1) * 127], g == 0, g == 1)
    k.copy('dve', KC2[:], pp.ap()[:, 0:127])
    for g in range(2):
        pp = ps_next()
        k.mm(pp.ap()[0:127, 0:64], H[0:64, 1, g * 127:(g + 1) * 127], w2p[0:64, 1, 0, 0:64], True, True)
        k.copy('dve', VCX[0:127, g, 0:64], pp.ap()[0:127, 0:64])


def prompt_phase_c(nc, S, k, D, DV, l, E):
    stop_at('b')
    ar = E['ar']; PS = E['PS']
    identb = E['identb']
    KT2 = E['KT2']; KA = E['KA']; VX = E['VX']; GT = E['GT']; VCX = E['VCX']; KC2 = E['KC2']
    TRI4 = E['TRI4']; TRIU4 = E['TRIU4']; CMREL4 = E['CMREL4']; SHIFT = E['SHIFT']; NF = E['NF']; ADDC = E['ADDC']
    QZ = E['QZ']; QA = E['QA']
    xsrc_t = E['xsrc_t']; xdst_t = E['xdst_t']; stage = E['stage']; st_rr = E['st_rr']; RD = E['RD']
    BT = 4
    WOUTB = E['WOUTB']
    QBt = [ar("QBc%d" % i, [512], BF16) for i in range(2)]
    SZCt = [ar("SZCt%d" % i, [4, 128], BF16) for i in range(2)]
    MIXt = [ar("MIXt%d" % i, [8, 128], BF16) for i in range(2)]
    XTc = [ar("XTc%d" % i, [DM], F32) for i in range(2)]
    PT = [ar("PT%d" % i, [512], BF16) for i in range(3)]
    OACC = ar("OACC", [8, 64], F32)
    TMP = ar("TMP", [8, 64], F32)
    OB = ar("OB", [512], BF16)
    ZR = ar("ZR", [3, 8], F32)
    COEF = ar("COEF", [3, 8], F32)
    IMPH = ar("IMPH", [8, 32], F32)
    IMP = ar("IMP", [2, 32], F32)
    M8 = ar("M8", [2, 8], F32)
    NSELT = ar("NSELT", [128], BF16)
    k.memset('pool', NSELT[:], 0.0)
    pt_rr = [0]
    sc_rr = [0]
    def sc_bank():
        b = PS[sc_rr[0] % 2]; sc_rr[0] += 1
        return b

    for qt in range(NT):
        cols = slice(qt * 128, (qt + 1) * 128)
        qb = QBt[qt % 2]; szc = SZCt[qt % 2]; mixt = MIXt[qt % 2]; xt = XTc[qt % 2]
        k.dma('sp', qb[:], View(D['qb_d'][cols, :], RD('qb', qt)))
        k.dma('sp', szc[:], View(D['szc_d'][:, :, cols], RD('szc', qt // BT)))
        k.dma('sp', mixt[:, 0:4, :], View(D['mix_d'][:, :, cols], RD('mix', qt // BT)))
        k.dma('sp', xt[:], xsrc_t(qt))
        pq = sc_bank()
        for r in range(4):
            k.tr(pq.ap(BF16)[:, r * 128:(r + 1) * 128], qb[:, r * 128:(r + 1) * 128], identb[:])
        pqv = pq.ap(BF16)[:, 0:512].rr("p (r q) -> p r q", r=4)
        k.copy('act', QZ[0][0:64], pqv[0:64])
        k.copy('dve', QZ[1][64:128], pqv[64:128])
        k.copy('act', QA[0][0:64], pqv[0:64])
        k.copy('dve', QA[1][64:128], pqv[64:128])
        nv = min(127, 8 * qt + 7)
        for g in range(2):
            ps_s = sc_bank()
            k.mm(ps_s.ap()[0:nv, :], KC2[:, 0:nv], QZ[g][:].rr("p a b -> p (a b)"), True, False)
            k.mm(ps_s.ap()[0:nv, :], SHIFT[:, qt, 0:nv], CMREL4[:], False, True)
            pt = PT[pt_rr[0] % 3]; pt_rr[0] += 1
            k.act(pt[0:nv, :], ps_s.ap()[0:nv, :], AF.Exp, scale=0.125)
            for h4 in range(4):
                k.mm(PS[2 + g].ap()[:, h4 * 97:(h4 + 1) * 97], pt[0:nv, h4 * 128:(h4 + 1) * 128], VCX[0:nv, g, :], h4 == 0, h4 == 3)
        for g in range(2):
            oc = PS[2 + g].ap()[:, 0:388].rr("p (h x) -> p h x", h=4)
            hs = slice(g * 4, g * 4 + 4)
            k.ts('dve', ZR[:, 0, hs], oc[:, :, 64], 1e-30, None, ALU.max)
            k.recip(ZR[:, 0, hs], ZR[:, 0, hs])
            k.tt('dve', COEF[:, 0, hs], ZR[:, 0, hs], GT[:, qt, hs], ALU.mult)
            k.tt('dve', OACC[:, hs, :], oc[:, :, 0:64], COEF[:, 0, hs].us(2).bc([128, 4, 64]), ALU.mult)
            k.tt('dve', IMPH[:, hs, :], oc[:, :, 65:97], ZR[:, 0, hs].us(2).bc([128, 4, 32]), ALU.mult)
        k.red('dve', IMP[:], IMPH[:].rr("p (g h) j -> p g j h", g=2))
        k.tt('dve', IMP[:], IMP[:], NF[:, qt, :].us(1).bc([128, 2, 32]), ALU.mult)
        k.tt('dve', IMP[:], IMP[:], ADDC[:, qt, :].us(1).bc([128, 2, 32]), ALU.add)
        for g in range(2):
            k.max8(M8[:, g, :], IMP[:, g, :])
            dstc = NSELT[:, 64:96] if g == 0 else NSELT[:, 0:32]
            k.ts('dve', dstc, IMP[:, g, :], M8[:, g, 7:8], NEGV, ALU.is_lt, ALU.mult)
        pn = sc_bank()
        k.tr(pn.ap(BF16)[:, 0:128], NSELT[:], identb[:])
        k.copy('act', QA[0][64:96], pn.ap(BF16)[64:96, 0:128].us(1).bc([32, 4, 128]))
        k.copy('dve', QA[1][0:32], pn.ap(BF16)[0:32, 0:128].us(1).bc([32, 4, 128]))
        stop_at('c1')
        for g in range(2):
            kts = [kt for kt in (qt - 2, qt - 1, qt) if kt >= 0]
            for idx, kt in enumerate(kts):
                ps_s = sc_bank()
                masked = (kt == qt) or (kt == qt - 2)
                k.mm(ps_s.ap(), KT2[:, 1, kt * 128:(kt + 1) * 128], QZ[g][:].rr("p a b -> p (a b)"), True, not masked)
                if kt == qt:
                    k.mm(ps_s.ap(), identb[:], TRI4[:], False, True)
                elif kt == qt - 2:
                    k.mm(ps_s.ap(), identb[:], TRIU4[:], False, True)
                pt = PT[pt_rr[0] % 3]; pt_rr[0] += 1
                k.act(pt[:], ps_s.ap(), AF.Exp, scale=0.125)
                for h4 in range(4):
                    k.mm(PS[6 + g].ap()[:, h4 * 65:(h4 + 1) * 65], pt[:, h4 * 128:(h4 + 1) * 128], VX[:, kt, 1, g, :],
                         idx == 0 and h4 == 0, idx == len(kts) - 1 and h4 == 3)
        stop_at('c2')
        for g in range(2):
            for kt in range(qt + 1):
                ps_s = sc_bank()
                k.mm(ps_s.ap(), KA[:, g, kt * 128:(kt + 1) * 128], QA[g][:].rr("p a b -> p (a b)"), True, kt != qt)
                if kt == qt:
                    k.mm(ps_s.ap(), identb[:], TRI4[:], False, True)
                pt = PT[pt_rr[0] % 3]; pt_rr[0] += 1
                k.act(pt[:], ps_s.ap(), AF.Exp, scale=0.125)
                for h4 in range(4):
                    k.mm(PS[4 + g].ap()[:, h4 * 65:(h4 + 1) * 65], pt[:, h4 * 128:(h4 + 1) * 128], VX[:, kt, 0, g, :],
                         kt == 0 and h4 == 0, kt == qt and h4 == 3)
        stop_at('c3')
        for (br, pb) in ((1, 4), (2, 6)):
            for g in range(2):
                o = PS[pb + g].ap()[:, 0:260].rr("p (h x) -> p h x", h=4)
                hs = slice(g * 4, g * 4 + 4)
                k.recip(ZR[:, br, hs], o[:, :, 64])
                k.tt('dve', COEF[:, br, hs], ZR[:, br, hs], GT[:, qt, br * 8 + g * 4:br * 8 + g * 4 + 4], ALU.mult)
                k.tt('dve', TMP[:, hs, :], o[:, :, 0:64], COEF[:, br, hs].us(2).bc([128, 4, 64]), ALU.mult)
            k.tt('dve', OACC[:], OACC[:], TMP[:], ALU.add)
        k.copy('act', OB[:], OACC[:].rr("p h d -> p (h d)"))
        po = sc_bank()
        for c in range(4):
            k.tr(po.ap(BF16)[:, c * 128:(c + 1) * 128], OB[:, c * 128:(c + 1) * 128], identb[:])
        k.tt('dve', mixt[:, 4:8, :], po.ap(BF16)[:, 0:512].rr("p (c q) -> p c q", c=4), szc[:], ALU.mult)
        for nb in range(2):
            for mc in range(8):
                k.mm(PS[2 + nb].ap(), mixt[:, mc, :], WOUTB[:, mc, nb * 512:(nb + 1) * 512], mc == 0, mc == 7)
            k.tt('dve', xt[:, nb * 512:(nb + 1) * 512], xt[:, nb * 512:(nb + 1) * 512], PS[2 + nb].ap(), ALU.add)
        k.dma('sp', xdst_t(qt), xt[:], outp=True)


def sample_s0(nc, S, k, D, DV, l, E):
    ar = E['ar']; WINB = E['WINB']; ps_next = E['ps_next']; identb = E['identb']
    gqb = E['gqb']; gkb = E['gkb']; RD = E['RD']; depth = E['depth']
    N = NSMP
    XS = ar("XS", [DM], F32)
    junk = ar("junk_s", [DM], BF16)
    hb = ar("hb_s", [DM], BF16)
    ssq = ar("ssq_s", [8], F32)
    HTs = ar("HTs", [8, N], BF16)
    PJS = ar("PJS", [DIN], F32)
    SQ2 = ar("SQ2s", [1280], F32)
    SS20 = ar("SS20s", [20], F32)
    RT = ar("RTs", [6, 8, 8], F32)
    CSs = ar("CSs", [16], F32)
    k.dma('sp', CSs[0:N], DV('c_cs_s'))
    xsrc = DV('xs') if l == 0 else View(D['xs1'], RD('xs1', 0))
    k.dma('sp', XS[0:N], xsrc)
    k.memset('pool', ssq[0:N, 0:1], 0.0)
    k.act(junk[0:N], XS[0:N], AF.Square, accum=ssq[0:N, 0:1])
    k.rsqrt(ssq[0:N, 1:2], ssq[0:N, 0:1], 1.0 / DM, EPS)
    k.ts('dve', hb[0:N], XS[0:N], ssq[0:N, 1:2], None, ALU.mult)
    pst = ps_next()
    for kc in range(8):
        k.tr(pst.ap(BF16)[:, kc * N:(kc + 1) * N], hb[0:N, kc * 128:(kc + 1) * 128], identb[0:N, 0:N])
    k.copy('act', HTs[:].rr("p a b -> p (a b)"), pst.ap(BF16)[:, 0:8 * N])
    for c0 in range(0, DIN, 512):
        cw = min(512, DIN - c0)
        pp = ps_next()
        for kc in range(8):
            k.mm(pp.ap()[0:N, 0:cw], HTs[:, kc, :], WINB[:, kc, c0:c0 + cw], kc == 0, kc == 7)
        k.copy('act' if (c0 // 512) % 2 == 0 else 'dve', PJS[0:N, c0:c0 + cw], pp.ap()[0:N, 0:cw])
    QO = 1280; KO = 1792
    k.act(SQ2[0:N], PJS[0:N, QO:QO + 1280], AF.Square)
    k.red('dve', SS20[0:N], SQ2[0:N].rr("p (h d) -> p h d", d=64))
    k.rsqrt(SS20[0:N], SS20[0:N], 1.0 / 64, EPS)
    Q3 = PJS[0:N, QO:QO + 512].rr("p (h d) -> p h d", d=64)
    k.tt('dve', Q3, Q3, SS20[0:N, 0:8].us(2).bc([N, 8, 64]), ALU.mult)
    k.tt('dve', Q3, Q3, gqb[0:N].us(1).bc([N, 8, 64]), ALU.mult)
    K3 = []
    for b in range(3):
        kk = PJS[0:N, KO + b * 256:KO + b * 256 + 128].rr("p (g d) -> p g d", d=64)
        k.tt('dve', kk, kk, SS20[0:N, 8 + b * 4:8 + b * 4 + 2].us(2).bc([N, 2, 64]), ALU.mult)
        k.tt('dve', kk, kk, gkb[0:N, b, :].us(1).bc([N, 2, 64]), ALU.mult)
        K3.append(kk)
    cos8 = CSs[0:N, 0:8]; sin8 = CSs[0:N, 8:16]
    def rope(X, nh):
        x1 = X[:, :, 0:8]; x2 = X[:, :, 8:16]
        cb = cos8.us(1).bc([N, nh, 8]); sbv = sin8.us(1).bc([N, nh, 8])
        k.tt('dve', RT[0:N, 0, 0:nh, :], x1, cb, ALU.mult)
        k.tt('dve', RT[0:N, 1, 0:nh, :], x2, sbv, ALU.mult)
        k.tt('dve', RT[0:N, 2, 0:nh, :], x2, cb, ALU.mult)
        k.tt('dve', RT[0:N, 3, 0:nh, :], x1, sbv, ALU.mult)
        k.tt('dve', x1, RT[0:N, 0, 0:nh, :], RT[0:N, 1, 0:nh, :], ALU.subtract)
        k.tt('dve', x2, RT[0:N, 2, 0:nh, :], RT[0:N, 3, 0:nh, :], ALU.add)
    rope(Q3, 8)
    for b in range(3):
        rope(K3[b], 2)
    k.act(SQ2[0:N, 0:256], PJS[0:N, 256:512], AF.Sigmoid)
    k.tt('dve', PJS[0:N, 0:256], PJS[0:N, 0:256], SQ2[0:N, 0:256], ALU.mult)
    k.act(PJS[0:N, 512:768], PJS[0:N, 512:768], AF.Silu)
    k.act(PJS[0:N, 1024:1280], PJS[0:N, 1024:1280], AF.Silu)
    k.act(PJS[0:N, 2560:2584], PJS[0:N, 2560:2584], AF.Sigmoid)
    k.act(PJS[0:N, 2584:3096], PJS[0:N, 2584:3096], AF.Silu)
    k.dma('sp', View(D['pjs_d'], RD('pjs', l)), PJS[0:N])
    k.dma('sp', DV('cmp_s')[l], PJS[0:N, KO:KO + 256], outp=True)
    k.dma('sp', DV('slc_s')[l], PJS[0:N, KO + 256:KO + 512], outp=True)
    k.dma('sp', DV('win_s')[l, :, 255, :], PJS[0:N, KO + 512:KO + 768], outp=True)
    k.dma('sp', DV('conv_s')[l, :, 29, :], PJS[0:N, 0:256], outp=True)
    k.dma('sp', DV('pool_s')[l, :, 14, :], PJS[0:N, 768:1024], outp=True)
    k.dma('sp', DV('conv_s')[l, :, 0:29, :], DV('state_conv')[l, :, 1:30, :], outp=True)
    k.dma('sp', DV('pool_s')[l, :, 0:14, :], DV('state_pool')[l, :, 1:15, :], outp=True)
    k.dma('sp', DV('win_s')[l, :, 0:255, :], DV('cache_win')[l, :, 1:256, :], outp=True)


def sample_phase_s(nc, S, k, D, DV, l, E):
    ar = E['ar']; PS = E['PS']; identb = E['identb']; identf = E['identf']
    stage = E['stage']; st_rr = E['st_rr']; RD = E['RD']; WOUTB = E['WOUTB']; depth = E['depth']
    n_phys = E['n_phys']
    N = NSMP
    sc_rr = [0]
    def sc_bank():
        b = PS[sc_rr[0] % 2]; sc_rr[0] += 1
        return b
    def nstage():
        sb = stage[st_rr[0] % 2]; st_rr[0] += 1
        return sb
    w1h = ar("w1h_s", [2, 32, 64], BF16)
    for kv in range(2):
        for rh in range(2):
            sb = nstage()
            srcv = View(D['w1h'][l, :, kv, rh * 16:(rh + 1) * 16, :].rearrange("p a b -> p (a b)"), Res('c'))
            k.dma('sp', sb[0:64, 0:1024], srcv)
            k.dma('sp', sb[64:128, 0:1024], srcv)
            k.copy('dve', w1h[:, kv, rh * 16:(rh + 1) * 16, :].rr("p a b -> p (a b)"), sb[:, 0:1024])
    pet = ar("pet_s", [2, 32], BF16)
    sb = nstage()
    k.dma('sp', sb[0:64, 0:64], View(D['pe_t'][l].rearrange("p a b -> p (a b)"), Res('c')))
    k.copy('dve', pet[0:64].rr("p a b -> p (a b)"), sb[0:64, 0:64])
    w2p = ar("w2p_s", [2, 2, 128], BF16)
    k.memset('pool', w2p[:], 0.0)
    sb = nstage()
    k.dma('sp', sb[0:64, 0:128], View(D['w2'][l].rearrange("p a b -> p (a b)"), Res('c')))
    for kv in range(2):
        k.copy('dve', w2p[0:64, kv, 0, 0:64], sb[0:64, kv * 64:(kv + 1) * 64])
        k.copy('dve', w2p[0:64, kv, 1, 64:128], sb[0:64, kv * 64:(kv + 1) * 64])
    PEB = ar("PEB_s", [2], F32)
    for kv in range(2):
        pp = sc_bank()
        for r in range(32):
            k.mm(pp.ap()[0:64, 0:1], w1h[0:64, kv, r, :], pet[0:64, kv, r:r + 1], r == 0, r == 31)
        k.copy('dve', PEB[0:64, kv:kv + 1], pp.ap()[0:64, 0:1])
    VCXs = ar("VCXs", [2, 98], BF16)
    k.memset('pool', VCXs[:], 1.0)
    sb = nstage()
    k.dma('sp', sb[0:127, 0:33], DV('c_ov_s'))
    for g in range(2):
        k.copy('dve', VCXs[0:127, g, 65:98], sb[0:127, 0:33])
    GRP = ar("GRP", [32], F32); k.dma('sp', GRP[:], DV('c_grp'))
    NFs = ar("NFs", [33], F32); k.dma('sp', NFs[0:32], DV('c_nfs'))
    ADDs = ar("ADDs", [33], F32); k.dma('sp', ADDs[0:32], DV('c_adds'))
    E33 = ar("E33", [128], F32); k.dma('sp', E33[0:33], DV('c_e33'))
    SEL8 = ar("SEL8", [128], F32); k.dma('sp', SEL8[0:16], DV('c_sel8'))
    PMOD = ar("PMOD", [8], F32); k.dma('sp', PMOD[:, 0:1], DV('c_pmod8'))
    PTI = ar("PTI", [16], I32)
    PTF = ar("PTF", [16], F32)
    IDXF = ar("IDXF", [16], F32)
    IDX = ar("IDX", [16], I32)
    k.dma('sp', PTI[0:16], DV('ptab_t'))
    k.copy('dve', PTF[0:16], PTI[0:16])
    pp = sc_bank()
    k.mm(pp.ap()[:, 0:16], SEL8[0:16, :], PTF[0:16, :], True, True)
    k.ts('dve', IDXF[:], pp.ap()[:, 0:16], 8.0, float(l * n_phys * 8), ALU.mult, ALU.add)
    k.ts('dve', IDXF[:], IDXF[:], PMOD[:, 0:1], None, ALU.add)
    k.copy('dve', IDX[:], IDXF[:])
    stop_at('q0')
    PJ2 = ar("PJ2", [DIN], F32)
    k.dma('sp', PJ2[0:N], View(D['pjs_d'], RD('pjs', l)))
    MIXs = ar("MIXs", [DM], F32)
    KO = 1792
    ar_off = E['ar_off']
    mark2 = ar_off[0]
    wpw = ar("wpw_s", [2, 256], BF16)
    wpl = ar("wpl_s", [2, 128], BF16)
    for kc in range(2):
        sb = nstage()
        k.dma('sp', sb[:, 0:256], DV('w_pw')[l, kc * 128:(kc + 1) * 128, :])
        k.copy('dve', wpw[:, kc, :], sb[:, 0:256])
    sb = nstage()
    k.dma('sp', sb[:, 0:256], View(D['wpool_bd'][l].rearrange("p a b -> p (a b)"), Res('c')))
    k.copy('dve', wpl[:].rr("p a b -> p (a b)"), sb[:, 0:256])
    ROWS = ar("ROWS", [5, 256], F32)
    k.dma('sp', ROWS[0:N].rr("p a b -> p (a b)"), View(D['rows_s'][l].rearrange("a b -> (a b)").partition_broadcast(N), Res('c')))
    XC = ar("XCs", [31, 128], F32)
    WD = ar("WDs", [31, 128], F32)
    CVs = ar("CVs", [256], F32)
    T1 = ar("T1s", [256], F32)
    ST = ar("STs", [8], F32)
    for ch in range(2):
        cs_ = slice(ch * 128, (ch + 1) * 128)
        k.dma('sp', XC[0:N, 0:30, :], DV('state_conv')[l, :, :, cs_])
        k.copy('pool', XC[0:N, 30, :], PJ2[0:N, cs_])
        k.dma('sp', WD[0:N], View(D['w_dw'][l, :, cs_].partition_broadcast(N), Res('c')))
        k.tt('dve', XC[0:N], XC[0:N], WD[0:N], ALU.mult)
        k.red('dve', CVs[0:N, cs_], XC[0:N].rr("p j c -> p c j"))
    k.tt('dve', CVs[0:N], CVs[0:N], ROWS[0:N, 0, :], ALU.add)
    k.red('dve', ST[0:N, 0:1], CVs[0:N])
    k.ts('dve', ST[0:N, 0:1], ST[0:N, 0:1], 1.0 / 256, None, ALU.mult)
    k.ts('dve', CVs[0:N], CVs[0:N], ST[0:N, 0:1], None, ALU.subtract)
    k.tt('dve', T1[0:N], CVs[0:N], CVs[0:N], ALU.mult)
    k.red('dve', ST[0:N, 1:2], T1[0:N])
    k.rsqrt(ST[0:N, 2:3], ST[0:N, 1:2], 1.0 / 256, EPS)
    k.ts('dve', CVs[0:N], CVs[0:N], ST[0:N, 2:3], None, ALU.mult)
    k.tt('dve', CVs[0:N], CVs[0:N], ROWS[0:N, 1, :], ALU.mult)
    k.tt('dve', CVs[0:N], CVs[0:N], ROWS[0:N, 2, :], ALU.add)
    ACs = ar("ACs", [256], BF16)
    k.act(ACs[0:N], CVs[0:N], AF.Silu)
    ATs = ar("ATs", [2, N], BF16)
    pp = sc_bank()
    for c in range(2):
        k.tr(pp.ap(BF16)[:, c * N:(c + 1) * N], ACs[0:N, c * 128:(c + 1) * 128], identb[0:N, 0:N])
    k.copy('act', ATs[:].rr("p a b -> p (a b)"), pp.ap(BF16)[:, 0:2 * N])
    pp = sc_bank()
    for c in range(2):
        k.mm(pp.ap()[0:N, 0:256], ATs[:, c, :], wpw[:, c, :], c == 0, c == 1)
    k.tt('dve', T1[0:N], pp.ap()[0:N, 0:256], ROWS[0:N, 3, :], ALU.add)
    k.tt('dve', MIXs[0:N, 0:256], T1[0:N], PJ2[0:N, 512:768], ALU.mult)
    stop_at('q1')
    XP = ar("XPs", [16, 256], F32)
    k.dma('sp', XP[0:N, 0:15, :], DV('state_pool')[l])
    k.copy('pool', XP[0:N, 15, :], PJ2[0:N, 768:1024])
    Ds = ar("Ds", [256], F32)
    for gi, w in enumerate((2, 4, 8, 16)):
        gs = slice(gi * 64, (gi + 1) * 64)
        k.red('dve', Ds[0:N, gs], XP[0:N, 16 - w:16, gs].rr("p j c -> p c j"))
        k.stt('dve', Ds[0:N, gs], Ds[0:N, gs], 1.0 / w, PJ2[0:N, 768 + gi * 64:768 + (gi + 1) * 64], ALU.mult, ALU.subtract)
    Dsb = ar("Dsb", [256], BF16)
    k.copy('dve', Dsb[0:N], Ds[0:N])
    DTs = ar("DTs", [2, N], BF16)
    pp = sc_bank()
    for c in range(2):
        k.tr(pp.ap(BF16)[:, c * N:(c + 1) * N], Dsb[0:N, c * 128:(c + 1) * 128], identb[0:N, 0:N])
    k.copy('act', DTs[:].rr("p a b -> p (a b)"), pp.ap(BF16)[:, 0:2 * N])
    pp = sc_bank()
    for c in range(2):
        k.mm(pp.ap()[0:N, c * 128:(c + 1) * 128], DTs[:, c, :], wpl[:, c, :], c == 0, c == 1)
    k.tt('dve', T1[0:N], pp.ap()[0:N, 0:256], ROWS[0:N, 4, :], ALU.mult)
    k.tt('dve', MIXs[0:N, 256:512], T1[0:N], PJ2[0:N, 1024:1280], ALU.mult)
    stop_at('q2')
    S.barrier(); ar_off[0] = mark2
    NHS = ar("NHS", [8, 387], F32)
    k.copy('pool', NHS[0:N, :, 0:64], PJ2[0:N, 1280:1792].rr("p (h d) -> p h d", d=64))
    for j, off in enumerate((KO + 256, KO + 384, KO + 512, KO + 640)):
        k.copy('pool', NHS[0:N, :, 64 + j * 64:128 + j * 64].rr("p (g r) d -> p g r d", g=2),
               PJ2[0:N, off:off + 128].rr("p (g d) -> p g d", g=2).us(2).bc([N, 2, 4, 64]))
    k.copy('pool', NHS[0:N, :, 320:384], PJ2[0:N, 2584:3096].rr("p (h d) -> p h d", d=64))
    k.copy('pool', NHS[0:N, :, 384:387], PJ2[0:N, 2560:2584].rr("p (b h) -> p h b", b=3))
    k.dma('sp', View(D['nh_d'], RD('nh', l)), NHS[0:N].rr("p a b -> p (a b)"))
    NH = ar("NH", [387], F32)
    k.dma('sp', NH[:], View(D['nh_d'].rearrange("n (h x) -> (n h) x", h=8), RD('nh', l)))
    QPAD = ar("QPAD", [2, 8, 128], BF16)
    k.memset('pool', QPAD[0:N], 0.0)
    k.copy('pool', QPAD[0:N, 0, :, 0:64], PJ2[0:N, 1280:1792].rr("p (h d) -> p h d", d=64))
    k.copy('pool', QPAD[0:N, 1, :, 64:128], PJ2[0:N, 1280:1792].rr("p (h d) -> p h d", d=64))
    QTZ = ar("QTZ", [2, 8, N], BF16)
    pp = sc_bank()
    for par in range(2):
        for h in range(8):
            k.tr(pp.ap(BF16)[:, (par * 8 + h) * N:(par * 8 + h + 1) * N], QPAD[0:N, par, h, :], identb[0:N, 0:N])
    k.copy('act', QTZ[:].rr("p a b c -> p (a b c)"), pp.ap(BF16)[:, 0:16 * N])
    stop_at('q3')
    AC = ar("AC", [4096], F32)
    XTc = ar("XTc", [2, 16, 128], BF16)
    U = ar("Us", [254], F32)
    U2 = ar("U2s", [254], F32)
    H = ar("Hs", [2, 254], BF16)
    KC2s = ar("KC2s", [127], BF16)
    PTs = ar("PTs", [8], BF16)
    OCT = PS[2]; OST = PS[3]; OWT = PS[4]
    cflat = View(D['cache_cmp'], Res('c'))
    sflat = View(D['cache_slc'], Res('c'))
    def gather(dst, src, n):
        S.dma('pool', None, None, reads=_rs([IDX[:]]), writes=_rs([dst[:]]),
              fn=lambda e: e.indirect_dma_start(out=dst.t[:, :], out_offset=None, in_=src.ap[:, :],
                                                in_offset=bass.IndirectOffsetOnAxis(ap=IDX.t[:, n:n + 1], axis=0)))
    for n in range(N):
        gather(AC, cflat, n)
        stop_at('q4')
        for kv in range(2):
            for r0 in range(0, 16, 4):
                pp = PS[5 + ((kv * 4 + r0 // 4) % 2)]
                for rr in range(4):
                    r = r0 + rr
                    k.tr(pp.ap()[:, rr * 128:(rr + 1) * 128], AC[:, r * 256 + kv * 128:r * 256 + kv * 128 + 128], identf[:])
                k.copy('act' if (r0 // 4) % 2 == 0 else 'dve', XTc[:, kv, r0:r0 + 4, :].rr("p a b -> p (a b)"), pp.ap()[:, 0:512])
        stop_at('q5')
        for kv in range(2):
            for g in range(2):
                pp = sc_bank()
                pr = slice(g * 64, g * 64 + 64)
                i = 0
                for jp in range(2):
                    for r in range(16):
                        k.mm(pp.ap()[0:64, 0:127], w1h[pr, kv, jp * 16 + r, :], XTc[pr, kv, r, jp:jp + 127], i == 0, i == 31)
                        i += 1
                k.act(U[0:64, g * 127:(g + 1) * 127], pp.ap()[0:64, 0:127], AF.Identity, bias=PEB[0:64, kv:kv + 1])
            k.tt('dve', U2[0:64], U[0:64], U[0:64], ALU.mult)
            k.ts('dve', U2[0:64], U2[0:64], 0.044715, 1.0, ALU.mult, ALU.add)
            k.tt('dve', U2[0:64], U2[0:64], U[0:64], ALU.mult)
            k.act(U2[0:64], U2[0:64], AF.Sigmoid, scale=1.5957691216057308)
            k.tt('dve', H[0:64, kv, :], U[0:64], U2[0:64], ALU.mult)
        stop_at('q6')
        pp = sc_bank()
        for g in range(2):
            k.mm(pp.ap()[:, 0:127], w2p[0:64, 0, g, :], H[0:64, 0, g * 127:(g + 1) * 127], g == 0, g == 1)
        k.copy('dve', KC2s[:], pp.ap()[:, 0:127])
        for g in range(2):
            pp = sc_bank()
            k.mm(pp.ap()[0:127, 0:64], H[0:64, 1, g * 127:(g + 1) * 127], w2p[0:64, 1, 0, 0:64], True, True)
            k.copy('dve', VCXs[0:127, g, 0:64], pp.ap()[0:127, 0:64])
        pp = sc_bank()
        for g in range(2):
            k.mm(pp.ap()[0:127, g * 4:(g + 1) * 4], KC2s[:, 0:127], QTZ[:, g, 4 * g:4 * g + 4, n], True, True)
        k.act(PTs[0:127, :], pp.ap()[0:127, 0:8], AF.Exp, scale=0.125)
        for g in range(2):
            k.mm(OCT.ap()[0:98, n * 8 + 4 * g:n * 8 + 4 * g + 4], VCXs[0:127, g, :], PTs[0:127, 4 * g:4 * g + 4], True, True)
    stop_at('q7')
    OT = ar("OT", [128], F32)
    OCN = ar("OCN", [98], F32)
    k.copy('dve', OT[0:98], OCT.ap()[0:98, 0:128])
    pp = sc_bank()
    k.tr(pp.ap()[:, 0:98], OT[0:98, :], identf[0:98, 0:98])
    k.copy('dve', OCN[:], pp.ap()[:, 0:98])
    RZ = ar("RZs", [8], F32)
    k.ts('dve', RZ[:, 0:1], OCN[:, 64:65], 1e-30, None, ALU.max)
    k.recip(RZ[:, 0:1], RZ[:, 0:1])
    IMPN = ar("IMPN", [33], F32)
    k.ts('dve', IMPN[:], OCN[:, 65:98], RZ[:, 0:1], None, ALU.mult)
    pp = sc_bank()
    k.mm(pp.ap()[0:32, 0:33], GRP[:, :], IMPN[:, :], True, True)
    SC = ar("SCs", [33], F32)
    k.tt('dve', SC[0:32], pp.ap()[0:32, 0:33], NFs[0:32], ALU.mult)
    k.tt('dve', SC[0:32], SC[0:32], ADDs[0:32], ALU.add)
    M8 = ar("M8s", [8], F32)
    k.max8(M8[0:32], SC[0:32])
    NEG = ar("NEGs", [33], F32)
    k.ts('dve', NEG[0:32], SC[0:32], M8[0:32, 7:8], NEGV, ALU.is_lt, ALU.mult)
    pp = sc_bank()
    k.tr(pp.ap()[0:33, 0:32], NEG[0:32, :], identf[0:32, 0:32])
    NEGT = ar("NEGT", [32], F32)
    k.copy('dve', NEGT[0:33], pp.ap()[0:33, 0:32])
    pp = sc_bank()
    k.mm(pp.ap()[:, 0:32], E33[0:33, :], NEGT[0:33, :], True, True)
    NEGPC = ar("NEGPC", [32], F32)
    k.copy('dve', NEGPC[:], pp.ap()[:, 0:32])
    stop_at('q8')
    XTs = ar("XTs", [16, 128], BF16)
    Vs = ar("Vs", [16, 2, 65], BF16)
    k.memset('pool', Vs[:], 1.0)
    PT2 = ar("PT2", [2, 64], BF16)
    Ws = ar("Ws", [2, 256], F32)
    KTw = ar("KTw", [2, 128], BF16)
    Vw = ar("Vw", [2, 2, 65], BF16)
    k.memset('pool', Vw[:], 1.0)
    PTw = ar("PTw", [16], BF16)
    for n in range(N):
        gather(AC, sflat, n)
        for r0 in range(0, 16, 4):
            pp = PS[5 + (r0 // 4) % 2]
            for rr in range(4):
                r = r0 + rr
                k.tr(pp.ap()[:, rr * 128:(rr + 1) * 128], AC[:, r * 256:r * 256 + 128], identf[:])
            k.copy('act' if (r0 // 4) % 2 == 0 else 'dve', XTs[:, r0:r0 + 4, :].rr("p a b -> p (a b)"), pp.ap()[:, 0:512])
        k.copy('dve' if n % 2 == 0 else 'act', Vs[:, :, :, 0:64], AC[:].rr("p (r x) -> p r x", r=16)[:, :, 128:256].rr("p r (g d) -> p r g d", g=2))
        pp = sc_bank()
        for g in range(2):
            for r in range(16):
                k.mm(pp.ap()[:, (g * 16 + r) * 4:(g * 16 + r) * 4 + 4], XTs[:, r, :], QTZ[:, g, 4 * g:4 * g + 4, n], True, True)
        for g in range(2):
            k.act(PT2[:, g, :], pp.ap()[:, g * 64:(g + 1) * 64], AF.Exp, scale=0.125, bias=NEGPC[:, 2 * n + g:2 * n + g + 1])
        for g in range(2):
            for r in range(16):
                k.mm(OST.ap()[0:65, n * 8 + 4 * g:n * 8 + 4 * g + 4], Vs[:, r, g, :], PT2[:, g, r * 4:(r + 1) * 4], r == 0, r == 15)
        stop_at('q9')
        k.dma('sp', Ws[:], DV('cache_win')[l, n].rr("(t p) f -> p t f", t=2))
        pp = PS[7]
        for t in range(2):
            k.tr(pp.ap()[:, t * 128:(t + 1) * 128], Ws[:, t, 0:128], identf[:])
        k.copy('act', KTw[:].rr("p a b -> p (a b)"), pp.ap()[:, 0:256])
        k.copy('dve', Vw[:, :, :, 0:64], Ws[:, :, 128:256].rr("p t (g d) -> p t g d", g=2))
        k.memset('pool', Vw[0:1, 0, :, :], 0.0)
        pp = sc_bank()
        for g in range(2):
            for t in range(2):
                k.mm(pp.ap()[:, (g * 2 + t) * 4:(g * 2 + t) * 4 + 4], KTw[:, t, :], QTZ[:, g, 4 * g:4 * g + 4, n], True, True)
        k.act(PTw[:], pp.ap()[:, 0:16], AF.Exp, scale=0.125)
        for g in range(2):
            for t in range(2):
                k.mm(OWT.ap()[0:65, n * 8 + 4 * g:n * 8 + 4 * g + 4], Vw[:, t, g, :], PTw[:, (g * 2 + t) * 4:(g * 2 + t) * 4 + 4], t == 0, t == 1)
    stop_at('q10')
    ONH = ar("ONH", [64], F32)
    OBR = ar("OBR", [65], F32)
    PN = ar("PN", [8], F32)
    T64 = ar("T64", [64], F32)
    k.tt('dve', PN[:, 0:1], RZ[:, 0:1], NH[:, 384:385], ALU.mult)
    k.ts('dve', ONH[:], OCN[:, 0:64], PN[:, 0:1], None, ALU.mult)
    for bi, (bank, ko, vo, go) in enumerate(((OST, 64, 128, 385), (OWT, 192, 256, 386))):
        k.copy('dve', OT[0:65], bank.ap()[0:65, 0:128])
        pp = sc_bank()
        k.tr(pp.ap()[:, 0:65], OT[0:65, :], identf[0:65, 0:65])
        k.copy('dve', OBR[:], pp.ap()[:, 0:65])
        k.tt('dve', T64[:], NH[:, 0:64], NH[:, ko:ko + 64], ALU.mult)
        k.red('dve', PN[:, 1:2], T64[:])
        k.act(PN[:, 2:3], PN[:, 1:2], AF.Exp, scale=0.125)
        k.stt('dve', OBR[:, 0:64], NH[:, vo:vo + 64], PN[:, 2:3], OBR[:, 0:64], ALU.mult, ALU.add)
        k.tt('dve', PN[:, 3:4], OBR[:, 64:65], PN[:, 2:3], ALU.add)
        k.recip(PN[:, 3:4], PN[:, 3:4])
        k.tt('dve', PN[:, 4:5], PN[:, 3:4], NH[:, go:go + 1], ALU.mult)
        k.stt('dve', ONH[:], OBR[:, 0:64], PN[:, 4:5], ONH[:], ALU.mult, ALU.add)
    k.tt('dve', ONH[:], ONH[:], NH[:, 320:384], ALU.mult)
    k.dma('sp', View(D['mixnh_d'], RD('mixnh', l)), ONH[:])
    k.dma('sp', MIXs[0:N, 512:1024], View(D['mixnh_d'].rearrange("(n h) d -> n (h d)", h=8), RD('mixnh', l)))
    stop_at('q11')
    MB = ar("MBs", [DM], BF16)
    k.copy('dve', MB[0:N], MIXs[0:N])
    MT = ar("MTs", [8, N], BF16)
    pp = sc_bank()
    for c in range(8):
        k.tr(pp.ap(BF16)[:, c * N:(c + 1) * N], MB[0:N, c * 128:(c + 1) * 128], identb[0:N, 0:N])
    k.copy('act', MT[:].rr("p a b -> p (a b)"), pp.ap(BF16)[:, 0:8 * N])
    XS2 = ar("XS2", [DM], F32)
    xsrc = DV('xs') if l == 0 else View(D['xs1'], RD('xs1', 0))
    k.dma('sp', XS2[0:N], xsrc)
    for nb in range(2):
        pp = sc_bank()
        for mc in range(8):
            k.mm(pp.ap()[0:N, :], MT[:, mc, :], WOUTB[:, mc, nb * 512:(nb + 1) * 512], mc == 0, mc == 7)
        k.tt('dve', XS2[0:N, nb * 512:(nb + 1) * 512], XS2[0:N, nb * 512:(nb + 1) * 512], pp.ap()[0:N, :], ALU.add)
    if l == depth - 1:
        k.dma('sp', DV('y_s'), XS2[0:N], outp=True)
    else:
        k.dma('sp', View(D['xs1'], RD('xs1', 0)), XS2[0:N])


def kernel(x_prompt, x_sample, state_conv, state_pool, cache_win_kv, cache_cmp_kv, cache_slc_kv, page_table,
           w_norm, w_in, w_out, w_dw, b_dw, ln_g, ln_b, w_pw, b_pw, w_pool, pool_scale,
           g_q, g_k, cmp_pe, cmp_w1, cmp_w2, _cores=None, _flags=None):
    f32 = np.float32
    inp = dict(w_norm=np.asarray(w_norm, f32), w_dw=np.asarray(w_dw, f32), b_dw=np.asarray(b_dw, f32),
               ln_g=np.asarray(ln_g, f32), ln_b=np.asarray(ln_b, f32), b_pw=np.asarray(b_pw, f32),
               w_pool=np.asarray(w_pool, f32), pool_scale=np.asarray(pool_scale, f32),
               cmp_pe=np.asarray(cmp_pe, f32), cmp_w1=np.asarray(cmp_w1, f32), cmp_w2=np.asarray(cmp_w2, f32))
    lay = _layout_params(inp)
    consts = _consts()
    flags = _flags or {}
    n_phys = cache_cmp_kv.shape[1]
    nc = build_nc(n_phys, **flags)
    cores = list(range(NCORES)) if _cores is None else _cores
    cc = np.ascontiguousarray(np.asarray(cache_cmp_kv, f32)).reshape(2 * n_phys * 8, 4096)
    csl = np.ascontiguousarray(np.asarray(cache_slc_kv, f32)).reshape(2 * n_phys * 8, 4096)
    shared = dict(w_in=np.asarray(w_in, f32), w_out=np.asarray(w_out, f32), w_pw=np.asarray(w_pw, f32),
                  g_q=np.asarray(g_q, f32), g_k=np.asarray(g_k, f32), w_dw=inp['w_dw'],
                  cache_cmp=cc, cache_slc=csl)
    for kk_, v in lay.items():
        shared[kk_] = v
    for kk_, v in consts.items():
        shared['c_' + kk_] = v
    in_maps = []
    for c in cores:
        m = dict(shared)
        m['xp'] = np.ascontiguousarray(np.asarray(x_prompt[c], f32))
        sl = slice(c * NSMP, (c + 1) * NSMP)
        m['xs'] = np.ascontiguousarray(np.asarray(x_sample[sl, 0], f32))
        m['state_conv'] = np.ascontiguousarray(np.asarray(state_conv[:, sl], f32))
        m['state_pool'] = np.ascontiguousarray(np.asarray(state_pool[:, sl], f32))
        m['cache_win'] = np.ascontiguousarray(np.asarray(cache_win_kv[:, sl], f32)).reshape(2, NSMP, 256, 256)
        m['ptab_t'] = np.ascontiguousarray(np.asarray(page_table[sl], np.int32).T)
        in_maps.append(m)
    res = run_bass_kernel_spmd(nc, in_maps, core_ids=list(range(len(cores))))
    R = res.results
    nb = len(cores)
    def gat(name):
        return [np.asarray(r[name]) for r in R]
    y_p = np.stack(gat('y_p'), 0)
    y_s = np.concatenate(gat('y_s'), 0).reshape(nb * NSMP, 1, DM)
    conv_p = np.stack(gat('conv_p'), 1)
    pool_p = np.stack(gat('pool_p'), 1)
    win_p = np.stack(gat('win_p'), 1).reshape(2, nb, 256, 2, 2, 64)
    cmp_p = np.stack(gat('cmp_p'), 1).reshape(2, nb, T, 2, 2, 64)
    slc_p = np.stack(gat('slc_p'), 1).reshape(2, nb, T, 2, 2, 64)
    conv_s = np.concatenate(gat('conv_s'), 1)
    pool_s = np.concatenate(gat('pool_s'), 1)
    win_s = np.concatenate(gat('win_s'), 1).reshape(2, nb * NSMP, 256, 2, 2, 64)
    cmp_s = np.concatenate(gat('cmp_s'), 1).reshape(2, nb * NSMP, 1, 2, 2, 64)
    slc_s = np.concatenate(gat('slc_s'), 1).reshape(2, nb * NSMP, 1, 2, 2, 64)
    return (y_p, y_s, conv_p, pool_p, win_p, cmp_p, slc_p, conv_s, pool_s, win_s, cmp_s, slc_s)
```

```python
import numpy as np
from contextlib import ExitStack
import concourse.bass as bass
import concourse.mybir as mybir
from concourse.bass_utils import run_bass_kernel_spmd

F32 = mybir.dt.float32
BF16 = mybir.dt.bfloat16
I32 = mybir.dt.int32
AF = mybir.ActivationFunctionType
ALU = mybir.AluOpType
AX = mybir.AxisListType

NCORES = 8
T = 2048
DM = 1024
DIN = 3096
NT = T // 128
NSMP = 16
EPS = 1e-6
NEGV = -30000.0
import os as _os
STOP = _os.environ.get('KSTOP', '')


class StopBuild(Exception):
    pass


def stop_at(tag):
    if STOP == tag:
        raise StopBuild(tag)
SEM_EPOCH = 8000
N_DMA_SEMS = 12


class Res:
    def __init__(self, name, t=None):
        self.name = name
        self.t = t
        self.last_writer = None
        self.readers = {}
        self.excl = False

    def __getitem__(self, idx):
        return View(self.t[idx], self)

    def ap(self, dtype=None):
        a = self.t[:]
        if dtype is not None and dtype != F32:
            a = a.bitcast(dtype)
        return View(a, self)


class View:
    def __init__(self, ap, res):
        self.ap = ap
        self.res = res

    def __getitem__(self, idx):
        return View(self.ap[idx], self.res)

    def rr(self, s, **kw):
        return View(self.ap.rearrange(s, **kw), self.res)

    def bc(self, shape):
        return View(self.ap.to_broadcast(list(shape)), self.res)

    def us(self, axis):
        return View(self.ap.unsqueeze(axis), self.res)

    def bitcast(self, dt):
        return View(self.ap.bitcast(dt), self.res)


class Sched:
    ENG = ('pe', 'act', 'dve', 'pool', 'sp')

    def __init__(self, nc, es):
        self.nc = nc
        self.es = es
        self.streams = {e: [] for e in self.ENG}
        self.count = {e: 0 for e in self.ENG}
        self.waited = {e: {} for e in self.ENG}
        self.pending = {e: [] for e in self.ENG}
        self.sems = {}
        self.dma_sems = {e: [] for e in self.ENG}
        self.dma_rr = {e: 0 for e in self.ENG}
        self.out_events = []
        self.last_ev = {}
        self.nops = 0
        self.ps_rr = 0

    def sbuf(self, name, shape, dtype):
        t = self.es.enter_context(self.nc.sbuf_tensor(name, list(shape), dtype))
        return Res(name, t)

    def psum(self, name):
        t = self.es.enter_context(self.nc.psum_tensor(name, [128, 512], F32))
        r = Res(name, t)
        r.excl = True
        return r

    def _sem(self, key):
        if key not in self.sems:
            nm = "s_" + "_".join(str(k) for k in key)
            self.sems[key] = self.es.enter_context(self.nc.semaphore(nm))
        return self.sems[key]

    def _deps(self, reads, writes):
        deps = []
        for r in reads:
            if r.last_writer is not None:
                deps.append(r.last_writer)
            if r.excl:
                deps.extend(r.readers.items())
        for w in writes:
            if w.last_writer is not None:
                deps.append(w.last_writer)
            deps.extend(w.readers.items())
        return deps

    def _waits_for(self, eng, deps):
        need = {}
        wd = self.waited[eng]
        for k, v in deps:
            if wd.get(k, 0) >= v:
                continue
            if need.get(k, 0) < v:
                need[k] = v
        for k, v in need.items():
            wd[k] = v
        return list(need.items())

    def _commit(self, ev, reads, writes):
        k, v = ev
        for r in reads:
            if r.readers.get(k, 0) < v:
                r.readers[k] = v
        for w in writes:
            w.last_writer = ev
            w.readers = {}
        self.last_ev[k] = max(self.last_ev.get(k, 0), v)

    def op(self, eng, fn, reads=(), writes=()):
        deps = self._deps(reads, writes) + self.pending[eng]
        self.pending[eng] = []
        if eng == 'pe':
            deps = [d for d in deps if d[0][0] != 'pe']
        waits = self._waits_for(eng, deps)
        self.count[eng] += 1
        c = self.count[eng]
        key = (eng, (c - 1) // SEM_EPOCH)
        val = (c - 1) % SEM_EPOCH + 1
        self._sem(key)
        ev = (key, val)
        self.streams[eng].append((waits, fn, key, 1))
        self._commit(ev, reads, writes)
        self.nops += 1
        return ev

    def dma(self, q, out_ap, in_ap, reads=(), writes=(), out=False, fn=None):
        deps = self._deps(reads, writes) + self.pending[q]
        self.pending[q] = []
        pool = self.dma_sems[q]
        if len(pool) < N_DMA_SEMS:
            ent = [('d' + q, len(pool)), 0]
            pool.append(ent)
        else:
            ent = pool[self.dma_rr[q] % N_DMA_SEMS]
            self.dma_rr[q] += 1
            deps.append((ent[0], ent[1]))
        waits = self._waits_for(q, deps)
        ent[1] += 16
        key = ent[0]
        self._sem(key)
        ev = (key, ent[1])
        if fn is None:
            fn = lambda e: e.dma_start(out=out_ap, in_=in_ap)
        self.streams[q].append((waits, fn, key, 16))
        self._commit(ev, reads, writes)
        if out:
            self.out_events.append(ev)
        self.nops += 1
        return ev

    def barrier(self):
        evs = list(self.last_ev.items())
        for e in self.ENG:
            self.pending[e] = list(evs)

    def finish(self):
        nc = self.nc
        fin = self._waits_for('sp', self.out_events)
        streams = self.streams
        sems = self.sems

        def emit(engobj, name, final_waits=()):
            for waits, fn, key, inc in streams[name]:
                for k, v in waits:
                    engobj.wait_ge(sems[k], v)
                fn(engobj).then_inc(sems[key], inc)
            for k, v in final_waits:
                engobj.wait_ge(sems[k], v)

        with nc.Block() as block:
            @block.sync
            def _(e):
                emit(e, 'sp', fin)

            @block.tensor
            def _(e):
                emit(e, 'pe')

            @block.scalar
            def _(e):
                emit(e, 'act')

            @block.vector
            def _(e):
                emit(e, 'dve')

            @block.gpsimd
            def _(e):
                emit(e, 'pool')


def _rs(views):
    out = []
    for v in views:
        if v is None or isinstance(v, (int, float)):
            continue
        r = v.res if isinstance(v, View) else v
        if r not in out:
            out.append(r)
    return out


class K:
    def __init__(self, S):
        self.S = S

    def tt(self, eng, out, a, b, op):
        self.S.op(eng, lambda e: e.tensor_tensor(out.ap, a.ap, b.ap, op), reads=_rs([a, b]), writes=_rs([out]))

    def ts(self, eng, out, a, s1, s2, op0, op1=None):
        s1a = s1.ap if isinstance(s1, View) else s1
        s2a = s2.ap if isinstance(s2, View) else s2
        if op1 is None:
            self.S.op(eng, lambda e: e.tensor_scalar(out.ap, a.ap, s1a, s2a, op0), reads=_rs([a, s1, s2]), writes=_rs([out]))
        else:
            self.S.op(eng, lambda e: e.tensor_scalar(out.ap, a.ap, s1a, s2a, op0, op1), reads=_rs([a, s1, s2]), writes=_rs([out]))

    def stt(self, eng, out, a, s, b, op0, op1):
        sa = s.ap if isinstance(s, View) else s
        self.S.op(eng, lambda e: e.scalar_tensor_tensor(out.ap, a.ap, sa, b.ap, op0, op1), reads=_rs([a, s, b]), writes=_rs([out]))

    def copy(self, eng, out, a):
        if eng == 'act':
            self.S.op(eng, lambda e: e.copy(out.ap, a.ap), reads=_rs([a]), writes=_rs([out]))
        else:
            self.S.op(eng, lambda e: e.tensor_copy(out.ap, a.ap), reads=_rs([a]), writes=_rs([out]))

    def memset(self, eng, out, val):
        self.S.op(eng, lambda e: e.memset(out.ap, val), writes=_rs([out]))

    def act(self, out, a, func, bias=None, scale=None, accum=None):
        kw = {}
        if bias is not None:
            kw['bias'] = bias.ap if isinstance(bias, View) else bias
        if scale is not None:
            kw['scale'] = scale.ap if isinstance(scale, View) else scale
        if accum is not None:
            kw['accum_out'] = accum.ap
        self.S.op('act', lambda e: e.activation(out.ap, a.ap, func, **kw), reads=_rs([a, bias, scale]), writes=_rs([out, accum]))

    def red(self, eng, out, a, op=ALU.add):
        self.S.op(eng, lambda e: e.tensor_reduce(out.ap, a.ap, AX.X, op), reads=_rs([a]), writes=_rs([out]))

    def recip(self, out, a):
        self.S.op('dve', lambda e: e.reciprocal(out.ap, a.ap), reads=_rs([a]), writes=_rs([out]))

    def max8(self, out, a):
        self.S.op('dve', lambda e: e.max(out=out.ap, in_=a.ap), reads=_rs([a]), writes=_rs([out]))

    def mm(self, out, lhsT, rhs, start, stop):
        self.S.op('pe', lambda e: e.matmul(out.ap, lhsT.ap, rhs.ap, start=start, stop=stop), reads=_rs([lhsT, rhs]), writes=_rs([out]))

    def tr(self, out, a, ident):
        self.S.op('pe', lambda e: e.transpose(out.ap, a.ap, ident.ap), reads=_rs([a, ident]), writes=_rs([out]))

    def dma(self, q, out, a, outp=False):
        self.S.dma(q, out.ap, a.ap, reads=_rs([a]), writes=_rs([out]), out=outp)

    def rsqrt(self, out, a, scale, eps):
        self.act(out, a, AF.Sqrt, bias=eps, scale=scale)
        self.recip(out, out)


def _consts():
    c = {}
    half = 8
    inv = 500000.0 ** (-np.arange(half, dtype=np.float32) * 2.0 / 16.0)
    pos = np.arange(T + 1, dtype=np.float32)
    ang = pos[:, None] * inv[None, :].astype(np.float32)
    cs = np.concatenate([np.cos(ang), np.sin(ang)], axis=1).astype(np.float32)
    csp = cs[:T].reshape(NT, 128, 16).transpose(1, 0, 2)
    c['cs_p'] = np.ascontiguousarray(csp)
    c['cs_s'] = np.ascontiguousarray(np.broadcast_to(cs[T][None, :], (NSMP, 16)))
    p = np.arange(128)[:, None]
    i = np.arange(128)[None, :]
    tri = np.where(p <= i, 0.0, NEGV).astype(np.float32)
    triu = np.where(p > i, 0.0, NEGV).astype(np.float32)
    c['tri4'] = np.ascontiguousarray(np.tile(tri, (1, 4)))
    c['triu4'] = np.ascontiguousarray(np.tile(triu, (1, 4)))
    cm = np.zeros((128, 128), np.float32)
    for cp in range(8):
        cm[cp] = np.where(np.arange(128) >= 16 * cp + 15, 0.0, NEGV)
    c['cmrel4'] = np.ascontiguousarray(np.tile(cm, (1, 4)))
    sh = np.zeros((128, NT, 127), np.float32)
    for qt in range(NT):
        for cp in range(8):
            cc = 8 * qt - 1 + cp
            if 0 <= cc < 127:
                sh[cp, qt, cc] = 1.0
    c['shift'] = sh
    nf = np.zeros((128, NT, 32), np.float32)
    addc = np.zeros((128, NT, 32), np.float32)
    for qt in range(NT):
        for ii in range(128):
            cur = (qt * 128 + ii) // 64
            for j in range(32):
                if j > cur:
                    addc[ii, qt, j] = -1e4
                elif j == 0 or j == cur or j == cur - 1:
                    addc[ii, qt, j] = 1e4
                else:
                    nf[ii, qt, j] = 1.0
    c['nf'] = nf
    c['addc'] = addc
    ec = np.zeros((32, T), np.float32)
    for j in range(32):
        ec[j, j * 64:(j + 1) * 64] = 1.0
    c['ec'] = ec
    def overlap(n_cmp, n_slc):
        cs_ = np.arange(n_cmp) * 16
        ce = cs_ + 31
        ss = np.arange(n_slc) * 64
        return ((ce[:, None] >= ss[None, :]) & (cs_[:, None] < ss[None, :] + 64)).astype(np.float32)
    c['ov_p'] = overlap(127, 32)
    c['ov_s'] = overlap(127, 33)
    invc = np.zeros((128, 2, 16), np.float32)
    for ch in range(2):
        for pp in range(128):
            w = (2, 4, 8, 16)[ch * 2 + pp // 64]
            invc[pp, ch] = 1.0 / np.minimum(w, np.arange(16) + 1)
    c['invc'] = invc
    grp = np.zeros((128, 32), np.float32)
    for n in range(16):
        for h in range(8):
            grp[n * 8 + h, n * 2 + h // 4] = 1.0
    c['grp'] = grp
    nfs = np.ones((32, 33), np.float32)
    adds = np.zeros((32, 33), np.float32)
    for j in (0, 31, 32):
        nfs[:, j] = 0.0
        adds[:, j] = 1e4
    c['nfs'] = nfs
    c['adds'] = adds
    e33 = np.zeros((33, 128), np.float32)
    for pc in range(128):
        e33[pc // 4, pc] = 1.0
    c['e33'] = e33
    sel8 = np.zeros((16, 128), np.float32)
    for pc in range(128):
        sel8[pc // 8, pc] = 1.0
    c['sel8'] = sel8
    c['pmod8'] = (np.arange(128) % 8).astype(np.float32).reshape(128, 1)
    return c


CONST_SHAPES = None


def _layout_params(inp):
    d = {}
    d['wnorm'] = np.ascontiguousarray(inp['w_norm'].reshape(2, 8, 128).transpose(0, 2, 1))
    d['wdw'] = np.ascontiguousarray(inp['w_dw'].reshape(2, 31, 2, 128).transpose(0, 3, 2, 1))
    def pc(a):
        return a.reshape(2, 2, 128).transpose(0, 2, 1)
    d['cvec'] = np.ascontiguousarray(np.stack([pc(inp['b_dw']), pc(inp['ln_g']), pc(inp['ln_b']),
                                               pc(inp['b_pw']), pc(inp['pool_scale'])], axis=-1))
    wp = inp['w_pool']
    bd = np.zeros((2, 2, 128, 128), np.float32)
    for g in range(4):
        ch, o = g // 2, (g % 2) * 64
        bd[:, ch, o:o + 64, o:o + 64] = wp[:, g]
    d['wpool_bd'] = np.ascontiguousarray(bd.transpose(0, 2, 1, 3))
    d['pe_t'] = np.ascontiguousarray(inp['cmp_pe'].transpose(0, 3, 1, 2))
    w1 = inp['cmp_w1'].reshape(2, 2, 32, 64, 64)
    d['w1h'] = np.ascontiguousarray(w1.transpose(0, 3, 1, 2, 4))
    w1p = inp['cmp_w1'].reshape(2, 2, 16, 128, 64)
    d['w1p'] = np.ascontiguousarray(w1p.transpose(0, 3, 1, 2, 4))
    d['w2'] = np.ascontiguousarray(inp['cmp_w2'].transpose(0, 2, 1, 3))
    d['rows_s'] = np.ascontiguousarray(np.stack([inp['b_dw'], inp['ln_g'], inp['ln_b'], inp['b_pw'],
                                                 inp['pool_scale']], axis=1))
    return d


def build_nc(n_phys, do_prompt=True, do_sample=True, depth=2):
    nc = bass.Bass("TRN2", target_bir_lowering=False)
    consts = _consts()
    D = {}

    def din(name, shape, dt=F32):
        D[name] = nc.dram_tensor(name, list(shape), dt, kind="ExternalInput").ap()
        return D[name]

    def dout(name, shape):
        D[name] = nc.dram_tensor(name, list(shape), F32, kind="ExternalOutput").ap()
        return D[name]

    def dscr(name, shape, dt=F32):
        D[name] = nc.dram_tensor(name, list(shape), dt, kind="Internal").ap()
        return D[name]

    din('xp', [T, DM]); din('xs', [NSMP, DM])
    din('w_in', [2, DM, DIN]); din('w_out', [2, DM, DM])
    din('wnorm', [2, 128, 8]); din('wdw', [2, 128, 2, 31]); din('cvec', [2, 128, 2, 5])
    din('w_pw', [2, 256, 256]); din('wpool_bd', [2, 128, 2, 128])
    din('g_q', [2, 64]); din('g_k', [2, 3, 64])
    din('pe_t', [2, 64, 2, 32]); din('w1h', [2, 64, 2, 32, 64]); din('w1p', [2, 128, 2, 16, 64]); din('w2', [2, 64, 2, 64])
    din('rows_s', [2, 5, 256]); din('w_dw', [2, 31, 256])
    din('state_conv', [2, NSMP, 30, 256]); din('state_pool', [2, NSMP, 15, 256])
    din('cache_win', [2, NSMP, 256, 256])
    din('cache_cmp', [2 * n_phys * 8, 4096]); din('cache_slc', [2 * n_phys * 8, 4096])
    din('ptab_t', [16, NSMP], I32)
    for k, v in consts.items():
        din('c_' + k, v.shape)
    dout('y_p', [T, DM]); dout('y_s', [NSMP, DM])
    dout('conv_p', [2, 30, 256]); dout('pool_p', [2, 15, 256]); dout('win_p', [2, 256, 256])
    dout('cmp_p', [2, T, 256]); dout('slc_p', [2, T, 256])
    dout('conv_s', [2, NSMP, 30, 256]); dout('pool_s', [2, NSMP, 15, 256]); dout('win_s', [2, NSMP, 256, 256])
    dout('cmp_s', [2, NSMP, 256]); dout('slc_s', [2, NSMP, 256])
    dscr('x1', [T, DM]); dscr('xs1', [NSMP, DM]); dscr('pjs_d', [NSMP, DIN]); dscr('nh_d', [NSMP, 8 * 387]); dscr('mixnh_d', [128, 64])
    dscr('qb_d', [T, 512], BF16); dscr('szc_d', [128, 4, T], BF16); dscr('mix_d', [128, 4, T], BF16)

    with ExitStack() as es:
        S = Sched(nc, es)
        k = K(S)
        RDICT = {}
        def RD(name, idx):
            if (name, idx) not in RDICT:
                RDICT[(name, idx)] = Res('%s_%s' % (name, idx))
            return RDICT[(name, idx)]
        def DV(name, res=None):
            return View(D[name], res if res is not None else Res('dram_' + name))
        PS = [S.psum("ps%d" % i) for i in range(8)]

        def ps_next():
            b = PS[S.ps_rr % 8]
            S.ps_rr += 1
            return b

        try:
          identf = S.sbuf("identf", [128, 128], F32)
          identb = S.sbuf("identb", [128, 128], BF16)
          onesb = S.sbuf("onesb", [128, 128], BF16)
          k.memset('pool', identf[:], 0.0)
          S.op('pool', lambda e: e.affine_select(identf.t[:], identf.t[:], pattern=[[-1, 128]], compare_op=ALU.not_equal,
                                                 fill=1.0, base=0, channel_multiplier=1), reads=[identf], writes=[identf])
          k.copy('dve', identb[:], identf[:])
          k.memset('pool', onesb[:], 1.0)
          stop_at('s1')
          stage = [S.sbuf("stage%d" % i, [128, 1024], F32) for i in range(2)]
          st_rr = [0]

          def load_const_bf(name, shape_free):
              src = D['c_' + name]
              npart = src.shape[0]
              t = S.sbuf("k_" + name, [128] + list(shape_free), BF16)
              nfree = int(np.prod(shape_free))
              sb = stage[st_rr[0] % 2]; st_rr[0] += 1
              flat_src = src if len(src.shape) == 2 else src.rearrange("p a b -> p (a b)")
              k.dma('sp', sb[0:npart, 0:nfree], View(flat_src, Res('c')))
              dst = t[0:npart]
              if len(shape_free) == 2:
                  dst = dst.rr("p a b -> p (a b)")
              k.copy('dve', dst, sb[0:npart, 0:nfree])
              return t

          def load_const_f(name, shape_free, npart=128):
              src = D['c_' + name]
              t = S.sbuf("k_" + name, [128] + list(shape_free), F32)
              k.dma('sp', t[0:npart], View(src, Res('c')))
              return t

          if do_prompt:
              CS = load_const_f('cs_p', [NT, 16])
              TRI4 = load_const_bf('tri4', [512])
              TRIU4 = load_const_bf('triu4', [512])
              CMREL4 = load_const_bf('cmrel4', [512])
              SHIFT = S.sbuf("k_shift", [128, NT, 127], BF16)
              for h0 in range(0, NT, 8):
                  sb = stage[st_rr[0] % 2]; st_rr[0] += 1
                  k.dma('sp', sb[:, 0:8 * 127], View(D['c_shift'][:, h0:h0 + 8, :].rearrange("p a b -> p (a b)"), Res('c')))
                  k.copy('dve', SHIFT[:, h0:h0 + 8, :].rr("p a b -> p (a b)"), sb[:, 0:8 * 127])
              NF = load_const_f('nf', [NT, 32])
              ADDC = load_const_f('addc', [NT, 32])
              INVC = load_const_f('invc', [2, 16])
              stop_at('s2')
              KT2 = S.sbuf("KT2", [128, 3, T], BF16)
              KA = S.sbuf("KA", [128, 2, T], BF16)
              VX = S.sbuf("VX", [128, NT, 2, 2, 65], BF16)
              GT = S.sbuf("GT", [128, NT, 24], F32)
              VCX = S.sbuf("VCX", [128, 2, 97], BF16)
              KC2 = S.sbuf("KC2", [128, 127], BF16)
              k.memset('pool', KA[:], 0.0)
              k.memset('pool', VX[:], 1.0)
              k.memset('pool', VCX[:], 1.0)
              for h0 in range(0, T, 1024):
                  sb = stage[st_rr[0] % 2]; st_rr[0] += 1
                  k.dma('sp', sb[64:96, 0:1024], View(D['c_ec'][:, h0:h0 + 1024], Res('c')))
                  k.dma('sp', sb[0:32, 0:1024], View(D['c_ec'][:, h0:h0 + 1024], Res('c')))
                  k.copy('dve', KA[64:96, 0, h0:h0 + 1024], sb[64:96, 0:1024])
                  k.copy('dve', KA[0:32, 1, h0:h0 + 1024], sb[0:32, 0:1024])
              sb = stage[st_rr[0] % 2]; st_rr[0] += 1
              k.dma('sp', sb[0:127, 0:32], View(D['c_ov_p'], Res('c')))
              for g in range(2):
                  k.copy('dve', VCX[0:127, g, 65:97], sb[0:127, 0:32])
              QZ = [S.sbuf("QZ%d" % g, [128, 4, 128], BF16) for g in range(2)]
              QA = [S.sbuf("QA%d" % g, [128, 4, 128], BF16) for g in range(2)]
              for g in range(2):
                  k.memset('pool', QZ[g][:], 0.0)
                  k.memset('pool', QA[g][:], 0.0)

          stop_at('s3')
          ARENA_F32 = 37000
          arena = S.sbuf("arena", [128, ARENA_F32], F32)
          ar_off = [0]

          def ar_reset():
              ar_off[0] = 0

          def ar(name, shape, dt):
              nel = int(np.prod(shape))
              nwords = (nel * (4 if dt == F32 or dt == I32 else 2) + 3) // 4
              nwords = (nwords + 7) // 8 * 8
              a = arena.t[:, ar_off[0]:ar_off[0] + nwords]
              ar_off[0] += nwords
              assert ar_off[0] <= ARENA_F32, (name, ar_off[0])
              if dt != F32:
                  a = a.bitcast(dt)
              a = a[:, 0:nel]
              if len(shape) == 2:
                  a = a.rearrange("p (a b) -> p a b", a=shape[0])
              elif len(shape) == 3:
                  a = a.rearrange("p (a b c) -> p a b c", a=shape[0], b=shape[1])
              elif len(shape) == 4:
                  a = a.rearrange("p (a b c d) -> p a b c d", a=shape[0], b=shape[1], c=shape[2])
              return Res(name, a)

          for l in range(depth):
              def xsrc_t(ti, l=l):
                  rows = slice(ti * 128, (ti + 1) * 128)
                  return View(D['xp'][rows, :], Res('c')) if l == 0 else View(D['x1'][rows, :], RD('x1', ti))
              def xdst_t(ti, l=l):
                  rows = slice(ti * 128, (ti + 1) * 128)
                  return View(D['x1'][rows, :], RD('x1', ti)) if (l == 0 and depth > 1) else View(D['y_p'][rows, :], Res('c'))
              S.barrier(); ar_reset()
              WINB = ar("winb", [8, DIN], BF16)
              wn = ar("wn", [8], F32)
              k.dma('sp', wn[:], DV('wnorm')[l])
              cvec = ar("cvec", [2, 5], F32)
              k.dma('sp', cvec[:], DV('cvec')[l])
              wdw = ar("wdw", [2, 31], F32)
              k.dma('sp', wdw[:], DV('wdw')[l])
              gqb = ar("gqb", [64], F32)
              k.dma('sp', gqb[:], View(D['g_q'][l].partition_broadcast(128), Res('c')))
              gkb = ar("gkb", [3, 64], F32)
              k.dma('sp', gkb[:].rr("p a b -> p (a b)"), View(D['g_k'][l].rearrange("a b -> (a b)").partition_broadcast(128), Res('c')))
              stop_at('s4')
              ci = 0
              for kc in range(8):
                  for c0 in range(0, DIN, 1024):
                      cw = min(1024, DIN - c0)
                      if ci >= int(_os.environ.get('WMAX', '1000')):
                          continue
                      sb = stage[st_rr[0] % 2]; st_rr[0] += 1
                      k.dma('sp', sb[:, 0:cw], DV('w_in')[l, kc * 128:(kc + 1) * 128, c0:c0 + cw])
                      wvar = _os.environ.get('WVAR', 'mix')
                      if wvar == 'none':
                          pass
                      elif wvar == 'act' or (wvar == 'mix' and ci % 2 == 0):
                          k.act(WINB[:, kc, c0:c0 + cw], sb[:, 0:cw], AF.Identity, scale=wn[:, kc:kc + 1])
                      else:
                          k.ts('dve', WINB[:, kc, c0:c0 + cw], sb[:, 0:cw], wn[:, kc:kc + 1], None, ALU.mult)
                      ci += 1
              stop_at('w')
              mark = ar_off[0]
              if do_sample:
                  sample_s0(nc, S, k, D, DV, l, locals())
                  S.barrier(); ar_off[0] = mark
              stop_at('s0')
              if do_prompt:
                  prompt_phase_a(nc, S, k, D, DV, l, locals())
              S.barrier(); ar_reset()
              WOUTB = ar("woutb", [8, DM], BF16)
              mark_w = ar_off[0]
              for kc in range(8):
                  sb = stage[st_rr[0] % 2]; st_rr[0] += 1
                  k.dma('sp', sb[:], DV('w_out')[l, kc * 128:(kc + 1) * 128, :])
                  if kc % 2 == 0:
                      k.copy('act', WOUTB[:, kc, :], sb[:])
                  else:
                      k.copy('dve', WOUTB[:, kc, :], sb[:])
              if do_prompt:
                  prompt_phase_c(nc, S, k, D, DV, l, locals())
              stop_at('pc')
              if do_sample:
                  S.barrier(); ar_off[0] = mark_w
                  sample_phase_s(nc, S, k, D, DV, l, locals())
              stop_at('L0end')
        except StopBuild as ex:
            print('build stopped at', ex)
        S.finish()
    return nc


def prompt_phase_a(nc, S, k, D, DV, l, E):
    ar = E['ar']; WINB = E['WINB']; PS = E['PS']; ps_next = E['ps_next']
    identb = E['identb']; identf = E['identf']; onesb = E['onesb']
    cvec = E['cvec']; wdw = E['wdw']; gqb = E['gqb']; gkb = E['gkb']
    CS = E['CS']; KT2 = E['KT2']; KA = E['KA']; VX = E['VX']; GT = E['GT']; INVC = E['INVC']
    VCX = E['VCX']; KC2 = E['KC2']
    xsrc_t = E['xsrc_t']; stage = E['stage']; st_rr = E['st_rr']; RD = E['RD']
    BT = 2
    BW = BT * 128
    NB = NT // BT
    wpw = ar("wpw", [2, 256], BF16)
    wpl = ar("wpl", [2, 128], BF16)
    for kc in range(2):
        sb = stage[st_rr[0] % 2]; st_rr[0] += 1
        k.dma('sp', sb[:, 0:256], DV('w_pw')[l, kc * 128:(kc + 1) * 128, :])
        k.copy('dve', wpw[:, kc, :], sb[:, 0:256])
    sb = stage[st_rr[0] % 2]; st_rr[0] += 1
    k.dma('sp', sb[:, 0:256], View(D['wpool_bd'][l].rearrange("p a b -> p (a b)"), Res('c')))
    k.copy('dve', wpl[:].rr("p a b -> p (a b)"), sb[:, 0:256])
    XT = [ar("XT%d" % i, [DM], F32) for i in range(2)]
    junks = [ar("junk%d" % i, [DM], BF16) for i in range(2)]
    hbs = [ar("hb%d" % i, [DM], BF16) for i in range(2)]
    ssqs = [ar("ssq%d" % i, [8], F32) for i in range(2)]
    HT = ar("HT", [8, BW], BF16)
    PJs = [ar("PJ%d" % i, [1304], F32) for i in range(2)]
    SQ2s = [ar("SQ2%d" % i, [1280], F32) for i in range(2)]
    SS20s = [ar("SS20%d" % i, [20], F32) for i in range(2)]
    RTs_ = [ar("RT%d" % i, [6, 8, 8], F32) for i in range(2)]
    KBs = [ar("KB%d" % i, [4, 128], BF16) for i in range(2)]
    QBts = [ar("QBt%d" % i, [512], BF16) for i in range(2)]
    AV = ar("AV", [2, BW], F32)
    SG = ar("SG", [2, BW], F32)
    GLUB = ar("GLUB", [2, 32 + BW], BF16)
    BIN = ar("BIN", [2, 16 + BW], F32)
    SZA = ar("SZA", [2, BW], BF16)
    SZB = ar("SZB", [2, BW], BF16)
    SZC = ar("SZCb", [4, BW], BF16)
    MIXB = ar("MIXB", [4, BW], BF16)
    DG = ar("DG", [2, 31, 128], BF16)
    CV = ar("CV", [2, BW], F32)
    CVB = ar("CVB", [2, BW], BF16)
    SQB = ar("SQB", [2, BW], BF16)
    MEAN = ar("MEAN", [BW], F32)
    VAR = ar("VAR", [BW], F32)
    ACTV = ar("ACTV", [2, BW], BF16)
    SA = ar("SA", [16 + BW], F32)
    SBb = ar("SBb", [16 + BW], F32)
    DB = ar("DB", [2, BW], BF16)
    TOUT = ar("TOUT", [256], F32)
    k.memset('pool', GLUB[:], 0.0)
    k.memset('pool', BIN[:], 0.0)
    for c in range(2):
        for j in range(31):
            if j % 2 == 0:
                k.act(DG[:, c, j, :], identf[:], AF.Identity, scale=wdw[:, c, j:j + 1])
            else:
                k.ts('dve', DG[:, c, j, :], identf[:], wdw[:, c, j:j + 1], None, ALU.mult)

    for tb in range(NB):
        for tt in range(BT):
            ti = tb * BT + tt
            xt = XT[ti % 2]
            pb = ti % 2
            junk = junks[pb]; hb = hbs[pb]; ssq = ssqs[pb]; PJ = PJs[pb]; SQ2 = SQ2s[pb]; SS20 = SS20s[pb]; RT = RTs_[pb]; KB = KBs[pb]; QBt = QBts[pb]
            k.dma('sp', xt[:], xsrc_t(ti))
            stop_at('t0')
            k.memset('dve', ssq[:, 0:1], 0.0)
            k.act(junk[:], xt[:], AF.Square, accum=ssq[:, 0:1])
            k.rsqrt(ssq[:, 1:2], ssq[:, 0:1], 1.0 / DM, EPS)
            k.ts('dve', hb[:], xt[:], ssq[:, 1:2], None, ALU.mult)
            stop_at('t1')
            pst = ps_next()
            for kc in range(8):
                k.tr(pst.ap(BF16)[:, kc * 128:(kc + 1) * 128], hb[:, kc * 128:(kc + 1) * 128], identb[:])
            k.copy('act', HT[:, :, tt * 128:(tt + 1) * 128], pst.ap(BF16)[:, 0:1024].rr("p (a b) -> p a b", a=8))
            stop_at('t2')
            for (c0, cw, o0) in ((1280, 512, 0), (1792, 512, 512), (2304, 280, 1024)):
                pp = ps_next()
                for kc in range(8):
                    k.mm(pp.ap()[:, 0:cw], HT[:, kc, tt * 128:(tt + 1) * 128], WINB[:, kc, c0:c0 + cw], kc == 0, kc == 7)
                k.copy('act' if o0 != 512 else 'dve', PJ[:, o0:o0 + cw], pp.ap()[:, 0:cw])
            stop_at('t3')
            k.act(SQ2[:], PJ[:, 0:1280], AF.Square)
            k.red('dve', SS20[:], SQ2[:].rr("p (h d) -> p h d", d=64))
            k.rsqrt(SS20[:], SS20[:], 1.0 / 64, EPS)
            Q3 = PJ[:, 0:512].rr("p (h d) -> p h d", d=64)
            k.tt('dve', Q3, Q3, SS20[:, 0:8].us(2).bc([128, 8, 64]), ALU.mult)
            k.tt('dve', Q3, Q3, gqb[:].us(1).bc([128, 8, 64]), ALU.mult)
            K3 = []
            for b in range(3):
                kk = PJ[:, 512 + b * 256:512 + b * 256 + 128].rr("p (g d) -> p g d", d=64)
                k.tt('dve', kk, kk, SS20[:, 8 + b * 4:8 + b * 4 + 2].us(2).bc([128, 2, 64]), ALU.mult)
                k.tt('dve', kk, kk, gkb[:, b, :].us(1).bc([128, 2, 64]), ALU.mult)
                K3.append(kk)
            stop_at('t4')
            cos8 = CS[:, ti, 0:8]
            sin8 = CS[:, ti, 8:16]
            def rope(X, nh, eng):
                x1 = X[:, :, 0:8]; x2 = X[:, :, 8:16]
                cb = cos8.us(1).bc([128, nh, 8]); sbv = sin8.us(1).bc([128, nh, 8])
                k.tt(eng, RT[:, 0, 0:nh, :], x1, cb, ALU.mult)
                k.tt(eng, RT[:, 1, 0:nh, :], x2, sbv, ALU.mult)
                k.tt(eng, RT[:, 2, 0:nh, :], x2, cb, ALU.mult)
                k.tt(eng, RT[:, 3, 0:nh, :], x1, sbv, ALU.mult)
                k.tt(eng, x1, RT[:, 0, 0:nh, :], RT[:, 1, 0:nh, :], ALU.subtract)
                k.tt(eng, x2, RT[:, 2, 0:nh, :], RT[:, 3, 0:nh, :], ALU.add)
            rope(Q3, 8, 'dve')
            for b in range(3):
                rope(K3[b], 2, 'dve')
            stop_at('t5')
            k.dma('sp', DV('cmp_p')[l, ti * 128:(ti + 1) * 128, :], PJ[:, 512:768], outp=True)
            k.dma('sp', DV('slc_p')[l, ti * 128:(ti + 1) * 128, :], PJ[:, 768:1024], outp=True)
            if ti >= NT - 2:
                k.dma('sp', DV('win_p')[l, (ti - NT + 2) * 128:(ti - NT + 3) * 128, :], PJ[:, 1024:1280], outp=True)
            k.copy('act', QBt[:].rr("p (r g d) -> p r g d", r=4, g=2), PJ[:, 0:512].rr("p (g r d) -> p r g d", g=2, r=4))
            k.dma('sp', View(D['qb_d'][ti * 128:(ti + 1) * 128, :], RD('qb', ti)), QBt[:])
            k.act(GT[:, ti, :], PJ[:, 1280:1304], AF.Sigmoid)
            stop_at('t6')
            k.copy('dve', KB[:, 0:3, :], PJ[:, 512:1280].rr("p (b x) -> p b x", b=3)[:, :, 0:128])
            k.copy('act', KB[:, 3, :], PJ[:, 640:768])
            stop_at('t6a')
            pk = ps_next()
            for j in range(4):
                k.tr(pk.ap(BF16)[:, j * 128:(j + 1) * 128], KB[:, j, :], identb[:])
            stop_at('t6b')
            pkv = pk.ap(BF16)
            cols = slice(ti * 128, (ti + 1) * 128)
            k.copy('act', KT2[:, 0, cols], pkv[:, 0:128])
            k.copy('act', KT2[:, 1, cols], pkv[:, 256:384])
            k.copy('act', KT2[:, 2, cols], pkv[:, 384:512])
            stop_at('t6c')
            k.copy('dve', KA[0:64, 0, cols], pkv[0:64, 128:256])
            stop_at('t6d')
            k.copy('dve', KA[64:128, 1, cols], pkv[64:128, 128:256])
            stop_at('t7')
            k.copy('act', VX[:, ti, :, :, 0:64],
                   PJ[:, 768:1280].rr("p (b x) -> p b x", b=2)[:, :, 128:256].rr("p b (g d) -> p b g d", g=2))
        stop_at('a1')
        t0 = tb * BW
        def fm_proj(c0):
            pp = ps_next()
            for kc in range(8):
                k.mm(pp.ap()[:, 0:BW], WINB[:, kc, c0:c0 + 128], HT[:, kc, :], kc == 0, kc == 7)
            return pp
        for c in range(2):
            pp = fm_proj(c * 128)
            k.copy('dve', AV[:, c, :], pp.ap()[:, 0:BW])
            pp = fm_proj(256 + c * 128)
            k.act(SG[:, c, :], pp.ap()[:, 0:BW], AF.Sigmoid)
            k.tt('dve', AV[:, c, :], AV[:, c, :], SG[:, c, :], ALU.mult)
            k.copy('act', GLUB[:, c, 32:32 + BW], AV[:, c, :])
            pp = fm_proj(512 + c * 128)
            k.act(SZA[:, c, :], pp.ap()[:, 0:BW], AF.Silu)
            pp = fm_proj(768 + c * 128)
            k.copy('dve', BIN[:, c, 16:16 + BW], pp.ap()[:, 0:BW])
            pp = fm_proj(1024 + c * 128)
            k.act(SZB[:, c, :], pp.ap()[:, 0:BW], AF.Silu)
        for c in range(4):
            pp = fm_proj(2584 + c * 128)
            k.act(SZC[:, c, :], pp.ap()[:, 0:BW], AF.Silu)
        k.dma('sp', View(D['szc_d'][:, :, t0:t0 + BW], RD('szc', tb)), SZC[:])
        if tb == NB - 1:
            pp = ps_next()
            for c in range(2):
                k.tr(pp.ap()[:, c * 128:(c + 1) * 128], AV[:, c, BW - 128:BW], identf[:])
            k.copy('dve', TOUT[:], pp.ap()[:, 0:256])
            k.dma('sp', DV('conv_p')[l], TOUT[98:128, :], outp=True)
            pp = ps_next()
            for c in range(2):
                k.tr(pp.ap()[:, c * 128:(c + 1) * 128], BIN[:, c, 16 + BW - 128:16 + BW], identf[:])
            k.copy('dve', TOUT[:], pp.ap()[:, 0:256])
            k.dma('sp', DV('pool_p')[l], TOUT[113:128, :], outp=True)
        stop_at('a2p')
        for c in range(2):
            pp = ps_next()
            for j in range(31):
                k.mm(pp.ap()[:, 0:BW], DG[:, c, j, :], GLUB[:, c, 2 + j:2 + j + BW], j == 0, j == 30)
            k.act(CV[:, c, :], pp.ap()[:, 0:BW], AF.Identity, bias=cvec[:, c, 0:1])
            k.copy('dve', CVB[:, c, :], CV[:, c, :])
            k.act(SQB[:, c, :], CV[:, c, :], AF.Square)
        p1 = ps_next()
        for c in range(2):
            k.mm(p1.ap()[:, 0:BW], onesb[:], CVB[:, c, :], c == 0, c == 1)
        k.act(MEAN[:], p1.ap()[:, 0:BW], AF.Identity, scale=1.0 / 256)
        p2 = ps_next()
        for c in range(2):
            k.mm(p2.ap()[:, 0:BW], onesb[:], SQB[:, c, :], c == 0, c == 1)
        k.tt('dve', VAR[:], MEAN[:], MEAN[:], ALU.mult)
        k.stt('dve', VAR[:], p2.ap()[:, 0:BW], 1.0 / 256, VAR[:], ALU.mult, ALU.subtract)
        k.ts('dve', VAR[:], VAR[:], 0.0, None, ALU.max)
        k.rsqrt(VAR[:], VAR[:], 1.0, EPS)
        for c in range(2):
            k.tt('dve', CV[:, c, :], CV[:, c, :], MEAN[:], ALU.subtract)
            k.tt('dve', CV[:, c, :], CV[:, c, :], VAR[:], ALU.mult)
            k.act(ACTV[:, c, :], CV[:, c, :], AF.Silu, bias=cvec[:, c, 2:3], scale=cvec[:, c, 1:2])
        for co in range(2):
            pp = ps_next()
            for ci_ in range(2):
                k.mm(pp.ap()[:, 0:BW], wpw[:, ci_, co * 128:(co + 1) * 128], ACTV[:, ci_, :], ci_ == 0, ci_ == 1)
            k.stt('dve', MIXB[:, co, :], pp.ap()[:, 0:BW], cvec[:, co, 3:4], SZA[:, co, :], ALU.add, ALU.mult)
        k.copy('dve', GLUB[:, :, 2:32], GLUB[:, :, 2 + BW:32 + BW])
        stop_at('a2c')
        for c in range(2):
            X = BIN[:, c, :]
            n = BW + 15
            k.tt('dve', SA[:, 1:16 + BW], X[:, 1:16 + BW], X[:, 0:15 + BW], ALU.add)
            k.tt('dve', SBb[:, 3:16 + BW], SA[:, 3:16 + BW], SA[:, 1:14 + BW], ALU.add)
            if c == 0:
                tot = (SA, SBb)
                ws = (2, 4)
            else:
                k.tt('dve', SA[:, 7:16 + BW], SBb[:, 7:16 + BW], SBb[:, 3:12 + BW], ALU.add)
                k.tt('dve', SBb[:, 15:16 + BW], SA[:, 15:16 + BW], SA[:, 7:8 + BW], ALU.add)
                tot = (SA, SBb)
                ws = (8, 16)
            for hf in range(2):
                pr = slice(hf * 64, hf * 64 + 64)
                k.ts('dve', SG[pr, c, :], tot[hf][pr, 16:16 + BW], 1.0 / ws[hf], None, ALU.mult)
                k.tt('dve', DB[pr, c, :], SG[pr, c, :], X[pr, 16:16 + BW], ALU.subtract)
                if tb == 0:
                    k.tt('dve', SG[pr, c, 0:16], tot[hf][pr, 16:32], INVC[pr, c, :], ALU.mult)
                    k.tt('dve', DB[pr, c, 0:16], SG[pr, c, 0:16], X[pr, 16:32], ALU.subtract)
            pp = ps_next()
            k.mm(pp.ap()[:, 0:BW], wpl[:, c, :], DB[:, c, :], True, True)
            k.stt('dve', MIXB[:, 2 + c, :], pp.ap()[:, 0:BW], cvec[:, c, 4:5], SZB[:, c, :], ALU.mult, ALU.mult)
        k.copy('dve', BIN[:, :, 1:16], BIN[:, :, 1 + BW:16 + BW])
        k.dma('sp', View(D['mix_d'][:, :, t0:t0 + BW], RD('mix', tb)), MIXB[:])

    stop_at('a')
    S.barrier(); E['ar_reset']()
    w1h = ar("w1h", [2, 32, 64], BF16)
    for kv in range(2):
        for rh in range(2):
            sb = stage[st_rr[0] % 2]; st_rr[0] += 1
            srcv = View(D['w1h'][l, :, kv, rh * 16:(rh + 1) * 16, :].rearrange("p a b -> p (a b)"), Res('c'))
            k.dma('sp', sb[0:64, 0:1024], srcv)
            k.dma('sp', sb[64:128, 0:1024], srcv)
            k.copy('dve', w1h[:, kv, rh * 16:(rh + 1) * 16, :].rr("p a b -> p (a b)"), sb[:, 0:1024])
    pet = ar("pet", [2, 32], BF16)
    sb = stage[st_rr[0] % 2]; st_rr[0] += 1
    k.dma('sp', sb[0:64, 0:64], View(D['pe_t'][l].rearrange("p a b -> p (a b)"), Res('c')))
    k.copy('dve', pet[0:64].rr("p a b -> p (a b)"), sb[0:64, 0:64])
    w2p = ar("w2p", [2, 2, 128], BF16)
    k.memset('pool', w2p[:], 0.0)
    sb = stage[st_rr[0] % 2]; st_rr[0] += 1
    k.dma('sp', sb[0:64, 0:128], View(D['w2'][l].rearrange("p a b -> p (a b)"), Res('c')))
    for kv in range(2):
        k.copy('dve', w2p[0:64, kv, 0, 0:64], sb[0:64, kv * 64:(kv + 1) * 64])
        k.copy('dve', w2p[0:64, kv, 1, 64:128], sb[0:64, kv * 64:(kv + 1) * 64])
    PEB = ar("PEB", [2], F32)
    for kv in range(2):
        pp = ps_next()
        for r in range(32):
            k.mm(pp.ap()[0:64, 0:1], w1h[0:64, kv, r, :], pet[0:64, kv, r:r + 1], r == 0, r == 31)
        k.copy('dve', PEB[0:64, kv:kv + 1], pp.ap()[0:64, 0:1])
    U = ar("U", [254], F32)
    U2 = ar("U2", [254], F32)
    H = ar("H", [2, 254], BF16)
    for kv in range(2):
        slot = 0 if kv == 0 else 2
        for g in range(2):
            pp = ps_next()
            pr = slice(g * 64, g * 64 + 64)
            for r in range(32):
                k.mm(pp.ap()[0:64, 0:127], w1h[pr, kv, r, :], KT2[pr, slot, r:r + 16 * 126 + 1:16], r == 0, r == 31)
            k.act(U[0:64, g * 127:(g + 1) * 127], pp.ap()[0:64, 0:127], AF.Identity, bias=PEB[0:64, kv:kv + 1])
        k.tt('dve', U2[0:64], U[0:64], U[0:64], ALU.mult)
        k.ts('dve', U2[0:64], U2[0:64], 0.044715, 1.0, ALU.mult, ALU.add)
        k.tt('dve', U2[0:64], U2[0:64], U[0:64], ALU.mult)
        k.act(U2[0:64], U2[0:64], AF.Sigmoid, scale=1.5957691216057308)
        k.tt('dve', H[0:64, kv, :], U[0:64], U2[0:64], ALU.mult)
    pp = ps_next()
    for g in range(2):
        k.mm(pp.ap()[:, 0:127], w2p[0:64, 0, g, :], H[0:64, 0, g * 127:(g + 1) * 127], g == 0, g == 1)
    k.copy('dve', KC2[:], pp.ap()[:, 0:127])
    for g in range(2):
        pp = ps_next()
        k.mm(pp.ap()[0:127, 0:64], H[0:64, 1, g * 127:(g + 1) * 127], w2p[0:64, 1, 0, 0:64], True, True)
        k.copy('dve', VCX[0:127, g, 0:64], pp.ap()[0:127, 0:64])


def prompt_phase_c(nc, S, k, D, DV, l, E):
    stop_at('b')
    ar = E['ar']; PS = E['PS']
    identb = E['identb']
    KT2 = E['KT2']; KA = E['KA']; VX = E['VX']; GT = E['GT']; VCX = E['VCX']; KC2 = E['KC2']
    TRI4 = E['TRI4']; TRIU4 = E['TRIU4']; CMREL4 = E['CMREL4']; SHIFT = E['SHIFT']; NF = E['NF']; ADDC = E['ADDC']
    QZ = E['QZ']; QA = E['QA']
    xsrc_t = E['xsrc_t']; xdst_t = E['xdst_t']; stage = E['stage']; st_rr = E['st_rr']; RD = E['RD']
    BT = 2
    WOUTB = E['WOUTB']
    QBt = [ar("QBc%d" % i, [512], BF16) for i in range(2)]
    SZCt = [ar("SZCt%d" % i, [4, 128], BF16) for i in range(2)]
    MIXt = [ar("MIXt%d" % i, [8, 128], BF16) for i in range(2)]
    XTc = [ar("XTc%d" % i, [DM], F32) for i in range(2)]
    NPT = 4
    PT = [ar("PT%d" % i, [512], BF16) for i in range(NPT)]
    OACCs = [ar("OACC%d" % i, [8, 64], F32) for i in range(2)]
    TMPs = [ar("TMP%d" % i, [8, 64], F32) for i in range(2)]
    OBs = [ar("OB%d" % i, [512], BF16) for i in range(2)]
    ZRs = [ar("ZR%d" % i, [3, 8], F32) for i in range(2)]
    COEFs = [ar("COEF%d" % i, [3, 8], F32) for i in range(2)]
    IMPHs = [ar("IMPH%d" % i, [8, 32], F32) for i in range(2)]
    IMPs = [ar("IMP%d" % i, [2, 32], F32) for i in range(2)]
    M8s = [ar("M8%d" % i, [2, 8], F32) for i in range(2)]
    NSELTs = [ar("NSELT%d" % i, [128], BF16) for i in range(2)]
    for i in range(2):
        k.memset('pool', NSELTs[i][:], 0.0)
    pt_rr = [0]
    sc_rr = [0]
    def sc_bank():
        b = PS[sc_rr[0] % 2]; sc_rr[0] += 1
        return b

    for qt in range(NT):
        cols = slice(qt * 128, (qt + 1) * 128)
        qb = QBt[qt % 2]; szc = SZCt[qt % 2]; mixt = MIXt[qt % 2]; xt = XTc[qt % 2]
        pb = qt % 2
        OACC = OACCs[pb]; TMP = TMPs[pb]; OB = OBs[pb]; ZR = ZRs[pb]; COEF = COEFs[pb]; IMPH = IMPHs[pb]; IMP = IMPs[pb]; M8 = M8s[pb]; NSELT = NSELTs[pb]
        k.dma('sp', qb[:], View(D['qb_d'][cols, :], RD('qb', qt)))
        k.dma('sp', szc[:], View(D['szc_d'][:, :, cols], RD('szc', qt // BT)))
        k.dma('sp', mixt[:, 0:4, :], View(D['mix_d'][:, :, cols], RD('mix', qt // BT)))
        k.dma('sp', xt[:], xsrc_t(qt))
        pq = sc_bank()
        for r in range(4):
            k.tr(pq.ap(BF16)[:, r * 128:(r + 1) * 128], qb[:, r * 128:(r + 1) * 128], identb[:])
        pqv = pq.ap(BF16)[:, 0:512].rr("p (r q) -> p r q", r=4)
        k.copy('act', QZ[0][0:64], pqv[0:64])
        k.copy('dve', QZ[1][64:128], pqv[64:128])
        k.copy('act', QA[0][0:64], pqv[0:64])
        k.copy('dve', QA[1][64:128], pqv[64:128])
        nv = min(127, 8 * qt + 7)
        for g in range(2):
            ps_s = sc_bank()
            k.mm(ps_s.ap()[0:nv, :], KC2[:, 0:nv], QZ[g][:].rr("p a b -> p (a b)"), True, False)
            k.mm(ps_s.ap()[0:nv, :], SHIFT[:, qt, 0:nv], CMREL4[:], False, True)
            pt = PT[pt_rr[0] % NPT]; pt_rr[0] += 1
            k.act(pt[0:nv, :], ps_s.ap()[0:nv, :], AF.Exp, scale=0.125)
            for h4 in range(4):
                k.mm(PS[2 + g].ap()[:, h4 * 97:(h4 + 1) * 97], pt[0:nv, h4 * 128:(h4 + 1) * 128], VCX[0:nv, g, :], h4 == 0, h4 == 3)
        for g in range(2):
            oc = PS[2 + g].ap()[:, 0:388].rr("p (h x) -> p h x", h=4)
            hs = slice(g * 4, g * 4 + 4)
            k.ts('dve', ZR[:, 0, hs], oc[:, :, 64], 1e-30, None, ALU.max)
            k.recip(ZR[:, 0, hs], ZR[:, 0, hs])
            k.tt('dve', COEF[:, 0, hs], ZR[:, 0, hs], GT[:, qt, hs], ALU.mult)
            k.tt('dve', OACC[:, hs, :], oc[:, :, 0:64], COEF[:, 0, hs].us(2).bc([128, 4, 64]), ALU.mult)
            k.tt('dve', IMPH[:, hs, :], oc[:, :, 65:97], ZR[:, 0, hs].us(2).bc([128, 4, 32]), ALU.mult)
        k.red('dve', IMP[:], IMPH[:].rr("p (g h) j -> p g j h", g=2))
        k.tt('dve', IMP[:], IMP[:], NF[:, qt, :].us(1).bc([128, 2, 32]), ALU.mult)
        k.tt('dve', IMP[:], IMP[:], ADDC[:, qt, :].us(1).bc([128, 2, 32]), ALU.add)
        for g in range(2):
            k.max8(M8[:, g, :], IMP[:, g, :])
            dstc = NSELT[:, 64:96] if g == 0 else NSELT[:, 0:32]
            k.ts('dve', dstc, IMP[:, g, :], M8[:, g, 7:8], NEGV, ALU.is_lt, ALU.mult)
        stop_at('c1')
        for g in range(2):
            kts = [kt for kt in (qt - 2, qt - 1, qt) if kt >= 0]
            for idx, kt in enumerate(kts):
                ps_s = sc_bank()
                masked = (kt == qt) or (kt == qt - 2)
                k.mm(ps_s.ap(), KT2[:, 1, kt * 128:(kt + 1) * 128], QZ[g][:].rr("p a b -> p (a b)"), True, not masked)
                if kt == qt:
                    k.mm(ps_s.ap(), identb[:], TRI4[:], False, True)
                elif kt == qt - 2:
                    k.mm(ps_s.ap(), identb[:], TRIU4[:], False, True)
                pt = PT[pt_rr[0] % NPT]; pt_rr[0] += 1
                k.act(pt[:], ps_s.ap(), AF.Exp, scale=0.125)
                for h4 in range(4):
                    k.mm(PS[6 + g].ap()[:, h4 * 65:(h4 + 1) * 65], pt[:, h4 * 128:(h4 + 1) * 128], VX[:, kt, 1, g, :],
                         idx == 0 and h4 == 0, idx == len(kts) - 1 and h4 == 3)
        stop_at('c2')
        pn = sc_bank()
        k.tr(pn.ap(BF16)[:, 0:128], NSELT[:], identb[:])
        k.copy('act', QA[0][64:96], pn.ap(BF16)[64:96, 0:128].us(1).bc([32, 4, 128]))
        k.copy('dve', QA[1][0:32], pn.ap(BF16)[0:32, 0:128].us(1).bc([32, 4, 128]))
        for g in range(2):
            for kt in range(qt + 1):
                ps_s = sc_bank()
                k.mm(ps_s.ap(), KA[:, g, kt * 128:(kt + 1) * 128], QA[g][:].rr("p a b -> p (a b)"), True, kt != qt)
                if kt == qt:
                    k.mm(ps_s.ap(), identb[:], TRI4[:], False, True)
                pt = PT[pt_rr[0] % NPT]; pt_rr[0] += 1
                k.act(pt[:], ps_s.ap(), AF.Exp, scale=0.125)
                for h4 in range(4):
                    k.mm(PS[4 + g].ap()[:, h4 * 65:(h4 + 1) * 65], pt[:, h4 * 128:(h4 + 1) * 128], VX[:, kt, 0, g, :],
                         kt == 0 and h4 == 0, kt == qt and h4 == 3)
        stop_at('c3')
        for (br, pb) in ((1, 4), (2, 6)):
            for g in range(2):
                o = PS[pb + g].ap()[:, 0:260].rr("p (h x) -> p h x", h=4)
                hs = slice(g * 4, g * 4 + 4)
                k.recip(ZR[:, br, hs], o[:, :, 64])
                k.tt('dve', COEF[:, br, hs], ZR[:, br, hs], GT[:, qt, br * 8 + g * 4:br * 8 + g * 4 + 4], ALU.mult)
                k.tt('dve', TMP[:, hs, :], o[:, :, 0:64], COEF[:, br, hs].us(2).bc([128, 4, 64]), ALU.mult)
            k.tt('dve', OACC[:], OACC[:], TMP[:], ALU.add)
        k.copy('act', OB[:], OACC[:].rr("p h d -> p (h d)"))
        po = sc_bank()
        for c in range(4):
            k.tr(po.ap(BF16)[:, c * 128:(c + 1) * 128], OB[:, c * 128:(c + 1) * 128], identb[:])
        k.tt('dve', mixt[:, 4:8, :], po.ap(BF16)[:, 0:512].rr("p (c q) -> p c q", c=4), szc[:], ALU.mult)
        for nb in range(2):
            for mc in range(8):
                k.mm(PS[2 + nb].ap(), mixt[:, mc, :], WOUTB[:, mc, nb * 512:(nb + 1) * 512], mc == 0, mc == 7)
            k.tt('dve', xt[:, nb * 512:(nb + 1) * 512], xt[:, nb * 512:(nb + 1) * 512], PS[2 + nb].ap(), ALU.add)
        k.dma('sp', xdst_t(qt), xt[:], outp=True)


def sample_s0(nc, S, k, D, DV, l, E):
    ar = E['ar']; WINB = E['WINB']; ps_next = E['ps_next']; identb = E['identb']
    gqb = E['gqb']; gkb = E['gkb']; RD = E['RD']; depth = E['depth']
    N = NSMP
    XS = ar("XS", [DM], F32)
    junk = ar("junk_s", [DM], BF16)
    hb = ar("hb_s", [DM], BF16)
    ssq = ar("ssq_s", [8], F32)
    HTs = ar("HTs", [8, N], BF16)
    PJS = ar("PJS", [DIN], F32)
    SQ2 = ar("SQ2s", [1280], F32)
    SS20 = ar("SS20s", [20], F32)
    RT = ar("RTs", [6, 8, 8], F32)
    CSs = ar("CSs", [16], F32)
    k.dma('sp', CSs[0:N], DV('c_cs_s'))
    xsrc = DV('xs') if l == 0 else View(D['xs1'], RD('xs1', 0))
    k.dma('sp', XS[0:N], xsrc)
    k.memset('pool', ssq[0:N, 0:1], 0.0)
    k.act(junk[0:N], XS[0:N], AF.Square, accum=ssq[0:N, 0:1])
    k.rsqrt(ssq[0:N, 1:2], ssq[0:N, 0:1], 1.0 / DM, EPS)
    k.ts('dve', hb[0:N], XS[0:N], ssq[0:N, 1:2], None, ALU.mult)
    pst = ps_next()
    for kc in range(8):
        k.tr(pst.ap(BF16)[:, kc * N:(kc + 1) * N], hb[0:N, kc * 128:(kc + 1) * 128], identb[0:N, 0:N])
    k.copy('act', HTs[:].rr("p a b -> p (a b)"), pst.ap(BF16)[:, 0:8 * N])
    for c0 in range(0, DIN, 512):
        cw = min(512, DIN - c0)
        pp = ps_next()
        for kc in range(8):
            k.mm(pp.ap()[0:N, 0:cw], HTs[:, kc, :], WINB[:, kc, c0:c0 + cw], kc == 0, kc == 7)
        k.copy('act' if (c0 // 512) % 2 == 0 else 'dve', PJS[0:N, c0:c0 + cw], pp.ap()[0:N, 0:cw])
    QO = 1280; KO = 1792
    k.act(SQ2[0:N], PJS[0:N, QO:QO + 1280], AF.Square)
    k.red('dve', SS20[0:N], SQ2[0:N].rr("p (h d) -> p h d", d=64))
    k.rsqrt(SS20[0:N], SS20[0:N], 1.0 / 64, EPS)
    Q3 = PJS[0:N, QO:QO + 512].rr("p (h d) -> p h d", d=64)
    k.tt('dve', Q3, Q3, SS20[0:N, 0:8].us(2).bc([N, 8, 64]), ALU.mult)
    k.tt('dve', Q3, Q3, gqb[0:N].us(1).bc([N, 8, 64]), ALU.mult)
    K3 = []
    for b in range(3):
        kk = PJS[0:N, KO + b * 256:KO + b * 256 + 128].rr("p (g d) -> p g d", d=64)
        k.tt('dve', kk, kk, SS20[0:N, 8 + b * 4:8 + b * 4 + 2].us(2).bc([N, 2, 64]), ALU.mult)
        k.tt('dve', kk, kk, gkb[0:N, b, :].us(1).bc([N, 2, 64]), ALU.mult)
        K3.append(kk)
    cos8 = CSs[0:N, 0:8]; sin8 = CSs[0:N, 8:16]
    def rope(X, nh):
        x1 = X[:, :, 0:8]; x2 = X[:, :, 8:16]
        cb = cos8.us(1).bc([N, nh, 8]); sbv = sin8.us(1).bc([N, nh, 8])
        k.tt('dve', RT[0:N, 0, 0:nh, :], x1, cb, ALU.mult)
        k.tt('dve', RT[0:N, 1, 0:nh, :], x2, sbv, ALU.mult)
        k.tt('dve', RT[0:N, 2, 0:nh, :], x2, cb, ALU.mult)
        k.tt('dve', RT[0:N, 3, 0:nh, :], x1, sbv, ALU.mult)
        k.tt('dve', x1, RT[0:N, 0, 0:nh, :], RT[0:N, 1, 0:nh, :], ALU.subtract)
        k.tt('dve', x2, RT[0:N, 2, 0:nh, :], RT[0:N, 3, 0:nh, :], ALU.add)
    rope(Q3, 8)
    for b in range(3):
        rope(K3[b], 2)
    k.act(SQ2[0:N, 0:256], PJS[0:N, 256:512], AF.Sigmoid)
    k.tt('dve', PJS[0:N, 0:256], PJS[0:N, 0:256], SQ2[0:N, 0:256], ALU.mult)
    k.act(PJS[0:N, 512:768], PJS[0:N, 512:768], AF.Silu)
    k.act(PJS[0:N, 1024:1280], PJS[0:N, 1024:1280], AF.Silu)
    k.act(PJS[0:N, 2560:2584], PJS[0:N, 2560:2584], AF.Sigmoid)
    k.act(PJS[0:N, 2584:3096], PJS[0:N, 2584:3096], AF.Silu)
    k.dma('sp', View(D['pjs_d'], RD('pjs', l)), PJS[0:N])
    k.dma('sp', DV('cmp_s')[l], PJS[0:N, KO:KO + 256], outp=True)
    k.dma('sp', DV('slc_s')[l], PJS[0:N, KO + 256:KO + 512], outp=True)
    k.dma('sp', DV('win_s')[l, :, 255, :], PJS[0:N, KO + 512:KO + 768], outp=True)
    k.dma('sp', DV('conv_s')[l, :, 29, :], PJS[0:N, 0:256], outp=True)
    k.dma('sp', DV('pool_s')[l, :, 14, :], PJS[0:N, 768:1024], outp=True)
    k.dma('sp', DV('conv_s')[l, :, 0:29, :], DV('state_conv')[l, :, 1:30, :], outp=True)
    k.dma('sp', DV('pool_s')[l, :, 0:14, :], DV('state_pool')[l, :, 1:15, :], outp=True)
    k.dma('sp', DV('win_s')[l, :, 0:255, :], DV('cache_win')[l, :, 1:256, :], outp=True)


def sample_phase_s(nc, S, k, D, DV, l, E):
    ar = E['ar']; PS = E['PS']; identb = E['identb']; identf = E['identf']
    stage = E['stage']; st_rr = E['st_rr']; RD = E['RD']; WOUTB = E['WOUTB']; depth = E['depth']
    n_phys = E['n_phys']
    N = NSMP
    sc_rr = [0]
    def sc_bank():
        b = PS[sc_rr[0] % 2]; sc_rr[0] += 1
        return b
    def nstage():
        sb = stage[st_rr[0] % 2]; st_rr[0] += 1
        return sb
    w1h = ar("w1h_s", [2, 32, 64], BF16)
    for kv in range(2):
        for rh in range(2):
            sb = nstage()
            srcv = View(D['w1h'][l, :, kv, rh * 16:(rh + 1) * 16, :].rearrange("p a b -> p (a b)"), Res('c'))
            k.dma('sp', sb[0:64, 0:1024], srcv)
            k.dma('sp', sb[64:128, 0:1024], srcv)
            k.copy('dve', w1h[:, kv, rh * 16:(rh + 1) * 16, :].rr("p a b -> p (a b)"), sb[:, 0:1024])
    pet = ar("pet_s", [2, 32], BF16)
    sb = nstage()
    k.dma('sp', sb[0:64, 0:64], View(D['pe_t'][l].rearrange("p a b -> p (a b)"), Res('c')))
    k.copy('dve', pet[0:64].rr("p a b -> p (a b)"), sb[0:64, 0:64])
    w2p = ar("w2p_s", [2, 2, 128], BF16)
    k.memset('pool', w2p[:], 0.0)
    sb = nstage()
    k.dma('sp', sb[0:64, 0:128], View(D['w2'][l].rearrange("p a b -> p (a b)"), Res('c')))
    for kv in range(2):
        k.copy('dve', w2p[0:64, kv, 0, 0:64], sb[0:64, kv * 64:(kv + 1) * 64])
        k.copy('dve', w2p[0:64, kv, 1, 64:128], sb[0:64, kv * 64:(kv + 1) * 64])
    PEB = ar("PEB_s", [2], F32)
    for kv in range(2):
        pp = sc_bank()
        for r in range(32):
            k.mm(pp.ap()[0:64, 0:1], w1h[0:64, kv, r, :], pet[0:64, kv, r:r + 1], r == 0, r == 31)
        k.copy('dve', PEB[0:64, kv:kv + 1], pp.ap()[0:64, 0:1])
    VCXs = ar("VCXs", [2, 98], BF16)
    k.memset('pool', VCXs[:], 1.0)
    sb = nstage()
    k.dma('sp', sb[0:127, 0:33], DV('c_ov_s'))
    for g in range(2):
        k.copy('dve', VCXs[0:127, g, 65:98], sb[0:127, 0:33])
    GRP = ar("GRP", [32], F32); k.dma('sp', GRP[:], DV('c_grp'))
    NFs = ar("NFs", [33], F32); k.dma('sp', NFs[0:32], DV('c_nfs'))
    ADDs = ar("ADDs", [33], F32); k.dma('sp', ADDs[0:32], DV('c_adds'))
    E33 = ar("E33", [128], F32); k.dma('sp', E33[0:33], DV('c_e33'))
    SEL8 = ar("SEL8", [128], F32); k.dma('sp', SEL8[0:16], DV('c_sel8'))
    PMOD = ar("PMOD", [8], F32); k.dma('sp', PMOD[:, 0:1], DV('c_pmod8'))
    PTI = ar("PTI", [16], I32)
    PTF = ar("PTF", [16], F32)
    IDXF = ar("IDXF", [16], F32)
    IDX = ar("IDX", [16], I32)
    k.dma('sp', PTI[0:16], DV('ptab_t'))
    k.copy('dve', PTF[0:16], PTI[0:16])
    pp = sc_bank()
    k.mm(pp.ap()[:, 0:16], SEL8[0:16, :], PTF[0:16, :], True, True)
    k.ts('dve', IDXF[:], pp.ap()[:, 0:16], 8.0, float(l * n_phys * 8), ALU.mult, ALU.add)
    k.ts('dve', IDXF[:], IDXF[:], PMOD[:, 0:1], None, ALU.add)
    k.copy('dve', IDX[:], IDXF[:])
    stop_at('q0')
    PJ2 = ar("PJ2", [DIN], F32)
    k.dma('sp', PJ2[0:N], View(D['pjs_d'], RD('pjs', l)))
    MIXs = ar("MIXs", [DM], F32)
    KO = 1792
    ar_off = E['ar_off']
    mark2 = ar_off[0]
    wpw = ar("wpw_s", [2, 256], BF16)
    wpl = ar("wpl_s", [2, 128], BF16)
    for kc in range(2):
        sb = nstage()
        k.dma('sp', sb[:, 0:256], DV('w_pw')[l, kc * 128:(kc + 1) * 128, :])
        k.copy('dve', wpw[:, kc, :], sb[:, 0:256])
    sb = nstage()
    k.dma('sp', sb[:, 0:256], View(D['wpool_bd'][l].rearrange("p a b -> p (a b)"), Res('c')))
    k.copy('dve', wpl[:].rr("p a b -> p (a b)"), sb[:, 0:256])
    ROWS = ar("ROWS", [5, 256], F32)
    k.dma('sp', ROWS[0:N].rr("p a b -> p (a b)"), View(D['rows_s'][l].rearrange("a b -> (a b)").partition_broadcast(N), Res('c')))
    XC = ar("XCs", [31, 128], F32)
    WD = ar("WDs", [31, 128], F32)
    CVs = ar("CVs", [256], F32)
    T1 = ar("T1s", [256], F32)
    ST = ar("STs", [8], F32)
    for ch in range(2):
        cs_ = slice(ch * 128, (ch + 1) * 128)
        k.dma('sp', XC[0:N, 0:30, :], DV('state_conv')[l, :, :, cs_])
        k.copy('pool', XC[0:N, 30, :], PJ2[0:N, cs_])
        k.dma('sp', WD[0:N], View(D['w_dw'][l, :, cs_].partition_broadcast(N), Res('c')))
        k.tt('dve', XC[0:N], XC[0:N], WD[0:N], ALU.mult)
        k.red('dve', CVs[0:N, cs_], XC[0:N].rr("p j c -> p c j"))
    k.tt('dve', CVs[0:N], CVs[0:N], ROWS[0:N, 0, :], ALU.add)
    k.red('dve', ST[0:N, 0:1], CVs[0:N])
    k.ts('dve', ST[0:N, 0:1], ST[0:N, 0:1], 1.0 / 256, None, ALU.mult)
    k.ts('dve', CVs[0:N], CVs[0:N], ST[0:N, 0:1], None, ALU.subtract)
    k.tt('dve', T1[0:N], CVs[0:N], CVs[0:N], ALU.mult)
    k.red('dve', ST[0:N, 1:2], T1[0:N])
    k.rsqrt(ST[0:N, 2:3], ST[0:N, 1:2], 1.0 / 256, EPS)
    k.ts('dve', CVs[0:N], CVs[0:N], ST[0:N, 2:3], None, ALU.mult)
    k.tt('dve', CVs[0:N], CVs[0:N], ROWS[0:N, 1, :], ALU.mult)
    k.tt('dve', CVs[0:N], CVs[0:N], ROWS[0:N, 2, :], ALU.add)
    ACs = ar("ACs", [256], BF16)
    k.act(ACs[0:N], CVs[0:N], AF.Silu)
    ATs = ar("ATs", [2, N], BF16)
    pp = sc_bank()
    for c in range(2):
        k.tr(pp.ap(BF16)[:, c * N:(c + 1) * N], ACs[0:N, c * 128:(c + 1) * 128], identb[0:N, 0:N])
    k.copy('act', ATs[:].rr("p a b -> p (a b)"), pp.ap(BF16)[:, 0:2 * N])
    pp = sc_bank()
    for c in range(2):
        k.mm(pp.ap()[0:N, 0:256], ATs[:, c, :], wpw[:, c, :], c == 0, c == 1)
    k.tt('dve', T1[0:N], pp.ap()[0:N, 0:256], ROWS[0:N, 3, :], ALU.add)
    k.tt('dve', MIXs[0:N, 0:256], T1[0:N], PJ2[0:N, 512:768], ALU.mult)
    stop_at('q1')
    XP = ar("XPs", [16, 256], F32)
    k.dma('sp', XP[0:N, 0:15, :], DV('state_pool')[l])
    k.copy('pool', XP[0:N, 15, :], PJ2[0:N, 768:1024])
    Ds = ar("Ds", [256], F32)
    for gi, w in enumerate((2, 4, 8, 16)):
        gs = slice(gi * 64, (gi + 1) * 64)
        k.red('dve', Ds[0:N, gs], XP[0:N, 16 - w:16, gs].rr("p j c -> p c j"))
        k.stt('dve', Ds[0:N, gs], Ds[0:N, gs], 1.0 / w, PJ2[0:N, 768 + gi * 64:768 + (gi + 1) * 64], ALU.mult, ALU.subtract)
    Dsb = ar("Dsb", [256], BF16)
    k.copy('dve', Dsb[0:N], Ds[0:N])
    DTs = ar("DTs", [2, N], BF16)
    pp = sc_bank()
    for c in range(2):
        k.tr(pp.ap(BF16)[:, c * N:(c + 1) * N], Dsb[0:N, c * 128:(c + 1) * 128], identb[0:N, 0:N])
    k.copy('act', DTs[:].rr("p a b -> p (a b)"), pp.ap(BF16)[:, 0:2 * N])
    pp = sc_bank()
    for c in range(2):
        k.mm(pp.ap()[0:N, c * 128:(c + 1) * 128], DTs[:, c, :], wpl[:, c, :], c == 0, c == 1)
    k.tt('dve', T1[0:N], pp.ap()[0:N, 0:256], ROWS[0:N, 4, :], ALU.mult)
    k.tt('dve', MIXs[0:N, 256:512], T1[0:N], PJ2[0:N, 1024:1280], ALU.mult)
    stop_at('q2')
    S.barrier(); ar_off[0] = mark2
    NHS = ar("NHS", [8, 387], F32)
    k.copy('pool', NHS[0:N, :, 0:64], PJ2[0:N, 1280:1792].rr("p (h d) -> p h d", d=64))
    for j, off in enumerate((KO + 256, KO + 384, KO + 512, KO + 640)):
        k.copy('pool', NHS[0:N, :, 64 + j * 64:128 + j * 64].rr("p (g r) d -> p g r d", g=2),
               PJ2[0:N, off:off + 128].rr("p (g d) -> p g d", g=2).us(2).bc([N, 2, 4, 64]))
    k.copy('pool', NHS[0:N, :, 320:384], PJ2[0:N, 2584:3096].rr("p (h d) -> p h d", d=64))
    k.copy('pool', NHS[0:N, :, 384:387], PJ2[0:N, 2560:2584].rr("p (b h) -> p h b", b=3))
    k.dma('sp', View(D['nh_d'], RD('nh', l)), NHS[0:N].rr("p a b -> p (a b)"))
    NH = ar("NH", [387], F32)
    k.dma('sp', NH[:], View(D['nh_d'].rearrange("n (h x) -> (n h) x", h=8), RD('nh', l)))
    QPAD = ar("QPAD", [2, 8, 128], BF16)
    k.memset('pool', QPAD[0:N], 0.0)
    k.copy('pool', QPAD[0:N, 0, :, 0:64], PJ2[0:N, 1280:1792].rr("p (h d) -> p h d", d=64))
    k.copy('pool', QPAD[0:N, 1, :, 64:128], PJ2[0:N, 1280:1792].rr("p (h d) -> p h d", d=64))
    QTZ = ar("QTZ", [2, 8, N], BF16)
    pp = sc_bank()
    for par in range(2):
        for h in range(8):
            k.tr(pp.ap(BF16)[:, (par * 8 + h) * N:(par * 8 + h + 1) * N], QPAD[0:N, par, h, :], identb[0:N, 0:N])
    k.copy('act', QTZ[:].rr("p a b c -> p (a b c)"), pp.ap(BF16)[:, 0:16 * N])
    stop_at('q3')
    AC = ar("AC", [4096], F32)
    XTc = ar("XTc", [2, 16, 128], BF16)
    U = ar("Us", [254], F32)
    U2 = ar("U2s", [254], F32)
    H = ar("Hs", [2, 254], BF16)
    KC2s = ar("KC2s", [127], BF16)
    PTs = ar("PTs", [8], BF16)
    OCT = PS[2]; OST = PS[3]; OWT = PS[4]
    cflat = View(D['cache_cmp'], Res('c'))
    sflat = View(D['cache_slc'], Res('c'))
    def gather(dst, src, n):
        S.dma('pool', None, None, reads=_rs([IDX[:]]), writes=_rs([dst[:]]),
              fn=lambda e: e.indirect_dma_start(out=dst.t[:, :], out_offset=None, in_=src.ap[:, :],
                                                in_offset=bass.IndirectOffsetOnAxis(ap=IDX.t[:, n:n + 1], axis=0)))
    for n in range(N):
        gather(AC, cflat, n)
        stop_at('q4')
        for kv in range(2):
            for r0 in range(0, 16, 4):
                pp = PS[5 + ((kv * 4 + r0 // 4) % 2)]
                for rr in range(4):
                    r = r0 + rr
                    k.tr(pp.ap()[:, rr * 128:(rr + 1) * 128], AC[:, r * 256 + kv * 128:r * 256 + kv * 128 + 128], identf[:])
                k.copy('act' if (r0 // 4) % 2 == 0 else 'dve', XTc[:, kv, r0:r0 + 4, :].rr("p a b -> p (a b)"), pp.ap()[:, 0:512])
        stop_at('q5')
        for kv in range(2):
            for g in range(2):
                pp = sc_bank()
                pr = slice(g * 64, g * 64 + 64)
                i = 0
                for jp in range(2):
                    for r in range(16):
                        k.mm(pp.ap()[0:64, 0:127], w1h[pr, kv, jp * 16 + r, :], XTc[pr, kv, r, jp:jp + 127], i == 0, i == 31)
                        i += 1
                k.act(U[0:64, g * 127:(g + 1) * 127], pp.ap()[0:64, 0:127], AF.Identity, bias=PEB[0:64, kv:kv + 1])
            k.tt('dve', U2[0:64], U[0:64], U[0:64], ALU.mult)
            k.ts('dve', U2[0:64], U2[0:64], 0.044715, 1.0, ALU.mult, ALU.add)
            k.tt('dve', U2[0:64], U2[0:64], U[0:64], ALU.mult)
            k.act(U2[0:64], U2[0:64], AF.Sigmoid, scale=1.5957691216057308)
            k.tt('dve', H[0:64, kv, :], U[0:64], U2[0:64], ALU.mult)
        stop_at('q6')
        pp = sc_bank()
        for g in range(2):
            k.mm(pp.ap()[:, 0:127], w2p[0:64, 0, g, :], H[0:64, 0, g * 127:(g + 1) * 127], g == 0, g == 1)
        k.copy('dve', KC2s[:], pp.ap()[:, 0:127])
        for g in range(2):
            pp = sc_bank()
            k.mm(pp.ap()[0:127, 0:64], H[0:64, 1, g * 127:(g + 1) * 127], w2p[0:64, 1, 0, 0:64], True, True)
            k.copy('dve', VCXs[0:127, g, 0:64], pp.ap()[0:127, 0:64])
        pp = sc_bank()
        for g in range(2):
            k.mm(pp.ap()[0:127, g * 4:(g + 1) * 4], KC2s[:, 0:127], QTZ[:, g, 4 * g:4 * g + 4, n], True, True)
        k.act(PTs[0:127, :], pp.ap()[0:127, 0:8], AF.Exp, scale=0.125)
        for g in range(2):
            k.mm(OCT.ap()[0:98, n * 8 + 4 * g:n * 8 + 4 * g + 4], VCXs[0:127, g, :], PTs[0:127, 4 * g:4 * g + 4], True, True)
    stop_at('q7')
    OT = ar("OT", [128], F32)
    OCN = ar("OCN", [98], F32)
    k.copy('dve', OT[0:98], OCT.ap()[0:98, 0:128])
    pp = sc_bank()
    k.tr(pp.ap()[:, 0:98], OT[0:98, :], identf[0:98, 0:98])
    k.copy('dve', OCN[:], pp.ap()[:, 0:98])
    RZ = ar("RZs", [8], F32)
    k.ts('dve', RZ[:, 0:1], OCN[:, 64:65], 1e-30, None, ALU.max)
    k.recip(RZ[:, 0:1], RZ[:, 0:1])
    IMPN = ar("IMPN", [33], F32)
    k.ts('dve', IMPN[:], OCN[:, 65:98], RZ[:, 0:1], None, ALU.mult)
    pp = sc_bank()
    k.mm(pp.ap()[0:32, 0:33], GRP[:, :], IMPN[:, :], True, True)
    SC = ar("SCs", [33], F32)
    k.tt('dve', SC[0:32], pp.ap()[0:32, 0:33], NFs[0:32], ALU.mult)
    k.tt('dve', SC[0:32], SC[0:32], ADDs[0:32], ALU.add)
    M8 = ar("M8s", [8], F32)
    k.max8(M8[0:32], SC[0:32])
    NEG = ar("NEGs", [33], F32)
    k.ts('dve', NEG[0:32], SC[0:32], M8[0:32, 7:8], NEGV, ALU.is_lt, ALU.mult)
    pp = sc_bank()
    k.tr(pp.ap()[0:33, 0:32], NEG[0:32, :], identf[0:32, 0:32])
    NEGT = ar("NEGT", [32], F32)
    k.copy('dve', NEGT[0:33], pp.ap()[0:33, 0:32])
    pp = sc_bank()
    k.mm(pp.ap()[:, 0:32], E33[0:33, :], NEGT[0:33, :], True, True)
    NEGPC = ar("NEGPC", [32], F32)
    k.copy('dve', NEGPC[:], pp.ap()[:, 0:32])
    stop_at('q8')
    XTs = ar("XTs", [16, 128], BF16)
    Vs = ar("Vs", [16, 2, 65], BF16)
    k.memset('pool', Vs[:], 1.0)
    PT2 = ar("PT2", [2, 64], BF16)
    Ws = ar("Ws", [2, 256], F32)
    KTw = ar("KTw", [2, 128], BF16)
    Vw = ar("Vw", [2, 2, 65], BF16)
    k.memset('pool', Vw[:], 1.0)
    PTw = ar("PTw", [16], BF16)
    for n in range(N):
        gather(AC, sflat, n)
        for r0 in range(0, 16, 4):
            pp = PS[5 + (r0 // 4) % 2]
            for rr in range(4):
                r = r0 + rr
                k.tr(pp.ap()[:, rr * 128:(rr + 1) * 128], AC[:, r * 256:r * 256 + 128], identf[:])
            k.copy('act' if (r0 // 4) % 2 == 0 else 'dve', XTs[:, r0:r0 + 4, :].rr("p a b -> p (a b)"), pp.ap()[:, 0:512])
        k.copy('dve' if n % 2 == 0 else 'act', Vs[:, :, :, 0:64], AC[:].rr("p (r x) -> p r x", r=16)[:, :, 128:256].rr("p r (g d) -> p r g d", g=2))
        pp = sc_bank()
        for g in range(2):
            for r in range(16):
                k.mm(pp.ap()[:, (g * 16 + r) * 4:(g * 16 + r) * 4 + 4], XTs[:, r, :], QTZ[:, g, 4 * g:4 * g + 4, n], True, True)
        for g in range(2):
            k.act(PT2[:, g, :], pp.ap()[:, g * 64:(g + 1) * 64], AF.Exp, scale=0.125, bias=NEGPC[:, 2 * n + g:2 * n + g + 1])
        for g in range(2):
            for r in range(16):
                k.mm(OST.ap()[0:65, n * 8 + 4 * g:n * 8 + 4 * g + 4], Vs[:, r, g, :], PT2[:, g, r * 4:(r + 1) * 4], r == 0, r == 15)
        stop_at('q9')
        k.dma('sp', Ws[:], DV('cache_win')[l, n].rr("(t p) f -> p t f", t=2))
        pp = PS[7]
        for t in range(2):
            k.tr(pp.ap()[:, t * 128:(t + 1) * 128], Ws[:, t, 0:128], identf[:])
        k.copy('act', KTw[:].rr("p a b -> p (a b)"), pp.ap()[:, 0:256])
        k.copy('dve', Vw[:, :, :, 0:64], Ws[:, :, 128:256].rr("p t (g d) -> p t g d", g=2))
        k.memset('pool', Vw[0:1, 0, :, :], 0.0)
        pp = sc_bank()
        for g in range(2):
            for t in range(2):
                k.mm(pp.ap()[:, (g * 2 + t) * 4:(g * 2 + t) * 4 + 4], KTw[:, t, :], QTZ[:, g, 4 * g:4 * g + 4, n], True, True)
        k.act(PTw[:], pp.ap()[:, 0:16], AF.Exp, scale=0.125)
        for g in range(2):
            for t in range(2):
                k.mm(OWT.ap()[0:65, n * 8 + 4 * g:n * 8 + 4 * g + 4], Vw[:, t, g, :], PTw[:, (g * 2 + t) * 4:(g * 2 + t) * 4 + 4], t == 0, t == 1)
    stop_at('q10')
    ONH = ar("ONH", [64], F32)
    OBR = ar("OBR", [65], F32)
    PN = ar("PN", [8], F32)
    T64 = ar("T64", [64], F32)
    k.tt('dve', PN[:, 0:1], RZ[:, 0:1], NH[:, 384:385], ALU.mult)
    k.ts('dve', ONH[:], OCN[:, 0:64], PN[:, 0:1], None, ALU.mult)
    for bi, (bank, ko, vo, go) in enumerate(((OST, 64, 128, 385), (OWT, 192, 256, 386))):
        k.copy('dve', OT[0:65], bank.ap()[0:65, 0:128])
        pp = sc_bank()
        k.tr(pp.ap()[:, 0:65], OT[0:65, :], identf[0:65, 0:65])
        k.copy('dve', OBR[:], pp.ap()[:, 0:65])
        k.tt('dve', T64[:], NH[:, 0:64], NH[:, ko:ko + 64], ALU.mult)
        k.red('dve', PN[:, 1:2], T64[:])
        k.act(PN[:, 2:3], PN[:, 1:2], AF.Exp, scale=0.125)
        k.stt('dve', OBR[:, 0:64], NH[:, vo:vo + 64], PN[:, 2:3], OBR[:, 0:64], ALU.mult, ALU.add)
        k.tt('dve', PN[:, 3:4], OBR[:, 64:65], PN[:, 2:3], ALU.add)
        k.recip(PN[:, 3:4], PN[:, 3:4])
        k.tt('dve', PN[:, 4:5], PN[:, 3:4], NH[:, go:go + 1], ALU.mult)
        k.stt('dve', ONH[:], OBR[:, 0:64], PN[:, 4:5], ONH[:], ALU.mult, ALU.add)
    k.tt('dve', ONH[:], ONH[:], NH[:, 320:384], ALU.mult)
    k.dma('sp', View(D['mixnh_d'], RD('mixnh', l)), ONH[:])
    k.dma('sp', MIXs[0:N, 512:1024], View(D['mixnh_d'].rearrange("(n h) d -> n (h d)", h=8), RD('mixnh', l)))
    stop_at('q11')
    MB = ar("MBs", [DM], BF16)
    k.copy('dve', MB[0:N], MIXs[0:N])
    MT = ar("MTs", [8, N], BF16)
    pp = sc_bank()
    for c in range(8):
        k.tr(pp.ap(BF16)[:, c * N:(c + 1) * N], MB[0:N, c * 128:(c + 1) * 128], identb[0:N, 0:N])
    k.copy('act', MT[:].rr("p a b -> p (a b)"), pp.ap(BF16)[:, 0:8 * N])
    XS2 = ar("XS2", [DM], F32)
    xsrc = DV('xs') if l == 0 else View(D['xs1'], RD('xs1', 0))
    k.dma('sp', XS2[0:N], xsrc)
    for nb in range(2):
        pp = sc_bank()
        for mc in range(8):
            k.mm(pp.ap()[0:N, :], MT[:, mc, :], WOUTB[:, mc, nb * 512:(nb + 1) * 512], mc == 0, mc == 7)
        k.tt('dve', XS2[0:N, nb * 512:(nb + 1) * 512], XS2[0:N, nb * 512:(nb + 1) * 512], pp.ap()[0:N, :], ALU.add)
    if l == depth - 1:
        k.dma('sp', DV('y_s'), XS2[0:N], outp=True)
    else:
        k.dma('sp', View(D['xs1'], RD('xs1', 0)), XS2[0:N])


def kernel(x_prompt, x_sample, state_conv, state_pool, cache_win_kv, cache_cmp_kv, cache_slc_kv, page_table,
           w_norm, w_in, w_out, w_dw, b_dw, ln_g, ln_b, w_pw, b_pw, w_pool, pool_scale,
           g_q, g_k, cmp_pe, cmp_w1, cmp_w2, _cores=None, _flags=None):
    f32 = np.float32
    inp = dict(w_norm=np.asarray(w_norm, f32), w_dw=np.asarray(w_dw, f32), b_dw=np.asarray(b_dw, f32),
               ln_g=np.asarray(ln_g, f32), ln_b=np.asarray(ln_b, f32), b_pw=np.asarray(b_pw, f32),
               w_pool=np.asarray(w_pool, f32), pool_scale=np.asarray(pool_scale, f32),
               cmp_pe=np.asarray(cmp_pe, f32), cmp_w1=np.asarray(cmp_w1, f32), cmp_w2=np.asarray(cmp_w2, f32))
    lay = _layout_params(inp)
    consts = _consts()
    flags = _flags or {}
    n_phys = cache_cmp_kv.shape[1]
    nc = build_nc(n_phys, **flags)
    cores = list(range(NCORES)) if _cores is None else _cores
    cc = np.ascontiguousarray(np.asarray(cache_cmp_kv, f32)).reshape(2 * n_phys * 8, 4096)
    csl = np.ascontiguousarray(np.asarray(cache_slc_kv, f32)).reshape(2 * n_phys * 8, 4096)
    shared = dict(w_in=np.asarray(w_in, f32), w_out=np.asarray(w_out, f32), w_pw=np.asarray(w_pw, f32),
                  g_q=np.asarray(g_q, f32), g_k=np.asarray(g_k, f32), w_dw=inp['w_dw'],
                  cache_cmp=cc, cache_slc=csl)
    for kk_, v in lay.items():
        shared[kk_] = v
    for kk_, v in consts.items():
        shared['c_' + kk_] = v
    in_maps = []
    for c in cores:
        m = dict(shared)
        m['xp'] = np.ascontiguousarray(np.asarray(x_prompt[c], f32))
        sl = slice(c * NSMP, (c + 1) * NSMP)
        m['xs'] = np.ascontiguousarray(np.asarray(x_sample[sl, 0], f32))
        m['state_conv'] = np.ascontiguousarray(np.asarray(state_conv[:, sl], f32))
        m['state_pool'] = np.ascontiguousarray(np.asarray(state_pool[:, sl], f32))
        m['cache_win'] = np.ascontiguousarray(np.asarray(cache_win_kv[:, sl], f32)).reshape(2, NSMP, 256, 256)
        m['ptab_t'] = np.ascontiguousarray(np.asarray(page_table[sl], np.int32).T)
        in_maps.append(m)
    res = run_bass_kernel_spmd(nc, in_maps, core_ids=list(range(len(cores))))
    R = res.results
    nb = len(cores)
    def gat(name):
        return [np.asarray(r[name]) for r in R]
    y_p = np.stack(gat('y_p'), 0)
    y_s = np.concatenate(gat('y_s'), 0).reshape(nb * NSMP, 1, DM)
    conv_p = np.stack(gat('conv_p'), 1)
    pool_p = np.stack(gat('pool_p'), 1)
    win_p = np.stack(gat('win_p'), 1).reshape(2, nb, 256, 2, 2, 64)
    cmp_p = np.stack(gat('cmp_p'), 1).reshape(2, nb, T, 2, 2, 64)
    slc_p = np.stack(gat('slc_p'), 1).reshape(2, nb, T, 2, 2, 64)
    conv_s = np.concatenate(gat('conv_s'), 1)
    pool_s = np.concatenate(gat('pool_s'), 1)
    win_s = np.concatenate(gat('win_s'), 1).reshape(2, nb * NSMP, 256, 2, 2, 64)
    cmp_s = np.concatenate(gat('cmp_s'), 1).reshape(2, nb * NSMP, 1, 2, 2, 64)
    slc_s = np.concatenate(gat('slc_s'), 1).reshape(2, nb * NSMP, 1, 2, 2, 64)
    return (y_p, y_s, conv_p, pool_p, win_p, cmp_p, slc_p, conv_s, pool_s, win_s, cmp_s, slc_s)
```

```python
import numpy as np
from contextlib import ExitStack
import concourse.bass as bass
import concourse.mybir as mybir
from concourse.bass_utils import run_bass_kernel_spmd

F32 = mybir.dt.float32
BF16 = mybir.dt.bfloat16
I32 = mybir.dt.int32
AF = mybir.ActivationFunctionType
ALU = mybir.AluOpType
AX = mybir.AxisListType

NCORES = 8
T = 2048
DM = 1024
DIN = 3096
NT = T // 128
NSMP = 16
EPS = 1e-6
NEGV = -30000.0
import os as _os
STOP = _os.environ.get('KSTOP', '')


class StopBuild(Exception):
    pass


def stop_at(tag):
    if STOP == tag:
        raise StopBuild(tag)
SEM_EPOCH = 8000
N_DMA_SEMS = 12


class Res:
    def __init__(self, name, t=None):
        self.name = name
        self.t = t
        self.last_writer = None
        self.readers = {}
        self.excl = False

    def __getitem__(self, idx):
        return View(self.t[idx], self)

    def ap(self, dtype=None):
        a = self.t[:]
        if dtype is not None and dtype != F32:
            a = a.bitcast(dtype)
        return View(a, self)


class View:
    def __init__(self, ap, res):
        self.ap = ap
        self.res = res

    def __getitem__(self, idx):
        return View(self.ap[idx], self.res)

    def rr(self, s, **kw):
        return View(self.ap.rearrange(s, **kw), self.res)

    def bc(self, shape):
        return View(self.ap.to_broadcast(list(shape)), self.res)

    def us(self, axis):
        return View(self.ap.unsqueeze(axis), self.res)

    def bitcast(self, dt):
        return View(self.ap.bitcast(dt), self.res)


class Sched:
    ENG = ('pe', 'act', 'dve', 'pool', 'sp')

    def __init__(self, nc, es):
        self.nc = nc
        self.es = es
        self.streams = {e: [] for e in self.ENG}
        self.count = {e: 0 for e in self.ENG}
        self.waited = {e: {} for e in self.ENG}
        self.pending = {e: [] for e in self.ENG}
        self.sems = {}
        self.dma_sems = {e: [] for e in self.ENG}
        self.dma_rr = {e: 0 for e in self.ENG}
        self.out_events = []
        self.last_ev = {}
        self.nops = 0
        self.ps_rr = 0

    def sbuf(self, name, shape, dtype):
        t = self.es.enter_context(self.nc.sbuf_tensor(name, list(shape), dtype))
        return Res(name, t)

    def psum(self, name):
        t = self.es.enter_context(self.nc.psum_tensor(name, [128, 512], F32))
        r = Res(name, t)
        r.excl = True
        return r

    def _sem(self, key):
        if key not in self.sems:
            nm = "s_" + "_".join(str(k) for k in key)
            self.sems[key] = self.es.enter_context(self.nc.semaphore(nm))
        return self.sems[key]

    def _deps(self, reads, writes):
        deps = []
        for r in reads:
            if r.last_writer is not None:
                deps.append(r.last_writer)
            if r.excl:
                deps.extend(r.readers.items())
        for w in writes:
            if w.last_writer is not None:
                deps.append(w.last_writer)
            deps.extend(w.readers.items())
        return deps

    def _waits_for(self, eng, deps):
        need = {}
        wd = self.waited[eng]
        for k, v in deps:
            if wd.get(k, 0) >= v:
                continue
            if need.get(k, 0) < v:
                need[k] = v
        for k, v in need.items():
            wd[k] = v
        return list(need.items())

    def _commit(self, ev, reads, writes):
        k, v = ev
        for r in reads:
            if r.readers.get(k, 0) < v:
                r.readers[k] = v
        for w in writes:
            w.last_writer = ev
            w.readers = {}
        self.last_ev[k] = max(self.last_ev.get(k, 0), v)

    def op(self, eng, fn, reads=(), writes=()):
        deps = self._deps(reads, writes) + self.pending[eng]
        self.pending[eng] = []
        if eng == 'pe':
            deps = [d for d in deps if d[0][0] != 'pe']
        waits = self._waits_for(eng, deps)
        self.count[eng] += 1
        c = self.count[eng]
        key = (eng, (c - 1) // SEM_EPOCH)
        val = (c - 1) % SEM_EPOCH + 1
        self._sem(key)
        ev = (key, val)
        self.streams[eng].append((waits, fn, key, 1))
        self._commit(ev, reads, writes)
        self.nops += 1
        return ev

    def dma(self, q, out_ap, in_ap, reads=(), writes=(), out=False, fn=None):
        deps = self._deps(reads, writes) + self.pending[q]
        self.pending[q] = []
        pool = self.dma_sems[q]
        if len(pool) < N_DMA_SEMS:
            ent = [('d' + q, len(pool)), 0]
            pool.append(ent)
        else:
            ent = pool[self.dma_rr[q] % N_DMA_SEMS]
            self.dma_rr[q] += 1
            deps.append((ent[0], ent[1]))
        waits = self._waits_for(q, deps)
        ent[1] += 16
        key = ent[0]
        self._sem(key)
        ev = (key, ent[1])
        if fn is None:
            fn = lambda e: e.dma_start(out=out_ap, in_=in_ap)
        self.streams[q].append((waits, fn, key, 16))
        self._commit(ev, reads, writes)
        if out:
            self.out_events.append(ev)
        self.nops += 1
        return ev

    def barrier(self):
        evs = list(self.last_ev.items())
        for e in self.ENG:
            self.pending[e] = list(evs)

    def finish(self):
        nc = self.nc
        fin = self._waits_for('sp', self.out_events)
        streams = self.streams
        sems = self.sems

        def emit(engobj, name, final_waits=()):
            for waits, fn, key, inc in streams[name]:
                for k, v in waits:
                    engobj.wait_ge(sems[k], v)
                fn(engobj).then_inc(sems[key], inc)
            for k, v in final_waits:
                engobj.wait_ge(sems[k], v)

        with nc.Block() as block:
            @block.sync
            def _(e):
                emit(e, 'sp', fin)

            @block.tensor
            def _(e):
                emit(e, 'pe')

            @block.scalar
            def _(e):
                emit(e, 'act')

            @block.vector
            def _(e):
                emit(e, 'dve')

            @block.gpsimd
            def _(e):
                emit(e, 'pool')


def _rs(views):
    out = []
    for v in views:
        if v is None or isinstance(v, (int, float)):
            continue
        r = v.res if isinstance(v, View) else v
        if r not in out:
            out.append(r)
    return out


class K:
    def __init__(self, S):
        self.S = S

    def tt(self, eng, out, a, b, op):
        self.S.op(eng, lambda e: e.tensor_tensor(out.ap, a.ap, b.ap, op), reads=_rs([a, b]), writes=_rs([out]))

    def ts(self, eng, out, a, s1, s2, op0, op1=None):
        s1a = s1.ap if isinstance(s1, View) else s1
        s2a = s2.ap if isinstance(s2, View) else s2
        if op1 is None:
            self.S.op(eng, lambda e: e.tensor_scalar(out.ap, a.ap, s1a, s2a, op0), reads=_rs([a, s1, s2]), writes=_rs([out]))
        else:
            self.S.op(eng, lambda e: e.tensor_scalar(out.ap, a.ap, s1a, s2a, op0, op1), reads=_rs([a, s1, s2]), writes=_rs([out]))

    def stt(self, eng, out, a, s, b, op0, op1):
        sa = s.ap if isinstance(s, View) else s
        self.S.op(eng, lambda e: e.scalar_tensor_tensor(out.ap, a.ap, sa, b.ap, op0, op1), reads=_rs([a, s, b]), writes=_rs([out]))

    def copy(self, eng, out, a):
        if eng == 'act':
            self.S.op(eng, lambda e: e.copy(out.ap, a.ap), reads=_rs([a]), writes=_rs([out]))
        else:
            self.S.op(eng, lambda e: e.tensor_copy(out.ap, a.ap), reads=_rs([a]), writes=_rs([out]))

    def memset(self, eng, out, val):
        self.S.op(eng, lambda e: e.memset(out.ap, val), writes=_rs([out]))

    def act(self, out, a, func, bias=None, scale=None, accum=None):
        kw = {}
        if bias is not None:
            kw['bias'] = bias.ap if isinstance(bias, View) else bias
        if scale is not None:
            kw['scale'] = scale.ap if isinstance(scale, View) else scale
        if accum is not None:
            kw['accum_out'] = accum.ap
        self.S.op('act', lambda e: e.activation(out.ap, a.ap, func, **kw), reads=_rs([a, bias, scale]), writes=_rs([out, accum]))

    def red(self, eng, out, a, op=ALU.add):
        self.S.op(eng, lambda e: e.tensor_reduce(out.ap, a.ap, AX.X, op), reads=_rs([a]), writes=_rs([out]))

    def recip(self, out, a):
        self.S.op('dve', lambda e: e.reciprocal(out.ap, a.ap), reads=_rs([a]), writes=_rs([out]))

    def max8(self, out, a):
        self.S.op('dve', lambda e: e.max(out=out.ap, in_=a.ap), reads=_rs([a]), writes=_rs([out]))

    def mm(self, out, lhsT, rhs, start, stop):
        self.S.op('pe', lambda e: e.matmul(out.ap, lhsT.ap, rhs.ap, start=start, stop=stop), reads=_rs([lhsT, rhs]), writes=_rs([out]))

    def tr(self, out, a, ident):
        self.S.op('pe', lambda e: e.transpose(out.ap, a.ap, ident.ap), reads=_rs([a, ident]), writes=_rs([out]))

    def dma(self, q, out, a, outp=False):
        self.S.dma(q, out.ap, a.ap, reads=_rs([a]), writes=_rs([out]), out=outp)

    def rsqrt(self, out, a, scale, eps):
        self.act(out, a, AF.Sqrt, bias=eps, scale=scale)
        self.recip(out, out)


def _consts():
    c = {}
    half = 8
    inv = 500000.0 ** (-np.arange(half, dtype=np.float32) * 2.0 / 16.0)
    pos = np.arange(T + 1, dtype=np.float32)
    ang = pos[:, None] * inv[None, :].astype(np.float32)
    cs = np.concatenate([np.cos(ang), np.sin(ang)], axis=1).astype(np.float32)
    csp = cs[:T].reshape(NT, 128, 16).transpose(1, 0, 2)
    c['cs_p'] = np.ascontiguousarray(csp)
    c['cs_s'] = np.ascontiguousarray(np.broadcast_to(cs[T][None, :], (NSMP, 16)))
    p = np.arange(128)[:, None]
    i = np.arange(128)[None, :]
    tri = np.where(p <= i, 0.0, NEGV).astype(np.float32)
    triu = np.where(p > i, 0.0, NEGV).astype(np.float32)
    c['tri4'] = np.ascontiguousarray(np.tile(tri, (1, 4)))
    c['triu4'] = np.ascontiguousarray(np.tile(triu, (1, 4)))
    cm = np.zeros((128, 128), np.float32)
    for cp in range(8):
        cm[cp] = np.where(np.arange(128) >= 16 * cp + 15, 0.0, NEGV)
    c['cmrel4'] = np.ascontiguousarray(np.tile(cm, (1, 4)))
    sh = np.zeros((128, NT, 127), np.float32)
    for qt in range(NT):
        for cp in range(8):
            cc = 8 * qt - 1 + cp
            if 0 <= cc < 127:
                sh[cp, qt, cc] = 1.0
    c['shift'] = sh
    nf = np.zeros((128, NT, 32), np.float32)
    addc = np.zeros((128, NT, 32), np.float32)
    for qt in range(NT):
        for ii in range(128):
            cur = (qt * 128 + ii) // 64
            for j in range(32):
                if j > cur:
                    addc[ii, qt, j] = -1e4
                elif j == 0 or j == cur or j == cur - 1:
                    addc[ii, qt, j] = 1e4
                else:
                    nf[ii, qt, j] = 1.0
    c['nf'] = nf
    c['addc'] = addc
    ec = np.zeros((32, T), np.float32)
    for j in range(32):
        ec[j, j * 64:(j + 1) * 64] = 1.0
    c['ec'] = ec
    def overlap(n_cmp, n_slc):
        cs_ = np.arange(n_cmp) * 16
        ce = cs_ + 31
        ss = np.arange(n_slc) * 64
        return ((ce[:, None] >= ss[None, :]) & (cs_[:, None] < ss[None, :] + 64)).astype(np.float32)
    c['ov_p'] = overlap(127, 32)
    c['ov_s'] = overlap(127, 33)
    invc = np.zeros((128, 2, 16), np.float32)
    for ch in range(2):
        for pp in range(128):
            w = (2, 4, 8, 16)[ch * 2 + pp // 64]
            invc[pp, ch] = 1.0 / np.minimum(w, np.arange(16) + 1)
    c['invc'] = invc
    grp = np.zeros((128, 32), np.float32)
    for n in range(16):
        for h in range(8):
            grp[n * 8 + h, n * 2 + h // 4] = 1.0
    c['grp'] = grp
    nfs = np.ones((32, 33), np.float32)
    adds = np.zeros((32, 33), np.float32)
    for j in (0, 31, 32):
        nfs[:, j] = 0.0
        adds[:, j] = 1e4
    c['nfs'] = nfs
    c['adds'] = adds
    e33 = np.zeros((33, 128), np.float32)
    for pc in range(128):
        e33[pc // 4, pc] = 1.0
    c['e33'] = e33
    sel8 = np.zeros((16, 128), np.float32)
    for pc in range(128):
        sel8[pc // 8, pc] = 1.0
    c['sel8'] = sel8
    c['pmod8'] = (np.arange(128) % 8).astype(np.float32).reshape(128, 1)
    return c


CONST_SHAPES = None


def _layout_params(inp):
    d = {}
    d['wnorm'] = np.ascontiguousarray(inp['w_norm'].reshape(2, 8, 128).transpose(0, 2, 1))
    d['wdw'] = np.ascontiguousarray(inp['w_dw'].reshape(2, 31, 2, 128).transpose(0, 3, 2, 1))
    def pc(a):
        return a.reshape(2, 2, 128).transpose(0, 2, 1)
    d['cvec'] = np.ascontiguousarray(np.stack([pc(inp['b_dw']), pc(inp['ln_g']), pc(inp['ln_b']),
                                               pc(inp['b_pw']), pc(inp['pool_scale'])], axis=-1))
    wp = inp['w_pool']
    bd = np.zeros((2, 2, 128, 128), np.float32)
    for g in range(4):
        ch, o = g // 2, (g % 2) * 64
        bd[:, ch, o:o + 64, o:o + 64] = wp[:, g]
    d['wpool_bd'] = np.ascontiguousarray(bd.transpose(0, 2, 1, 3))
    d['pe_t'] = np.ascontiguousarray(inp['cmp_pe'].transpose(0, 3, 1, 2))
    w1 = inp['cmp_w1'].reshape(2, 2, 32, 64, 64)
    d['w1h'] = np.ascontiguousarray(w1.transpose(0, 3, 1, 2, 4))
    w1p = inp['cmp_w1'].reshape(2, 2, 16, 128, 64)
    d['w1p'] = np.ascontiguousarray(w1p.transpose(0, 3, 1, 2, 4))
    d['w2'] = np.ascontiguousarray(inp['cmp_w2'].transpose(0, 2, 1, 3))
    d['rows_s'] = np.ascontiguousarray(np.stack([inp['b_dw'], inp['ln_g'], inp['ln_b'], inp['b_pw'],
                                                 inp['pool_scale']], axis=1))
    return d


def build_nc(n_phys, do_prompt=True, do_sample=True, depth=2):
    nc = bass.Bass("TRN2", target_bir_lowering=False)
    consts = _consts()
    D = {}

    def din(name, shape, dt=F32):
        D[name] = nc.dram_tensor(name, list(shape), dt, kind="ExternalInput").ap()
        return D[name]

    def dout(name, shape):
        D[name] = nc.dram_tensor(name, list(shape), F32, kind="ExternalOutput").ap()
        return D[name]

    def dscr(name, shape, dt=F32):
        D[name] = nc.dram_tensor(name, list(shape), dt, kind="Internal").ap()
        return D[name]

    din('xp', [T, DM]); din('xs', [NSMP, DM])
    din('w_in', [2, DM, DIN]); din('w_out', [2, DM, DM])
    din('wnorm', [2, 128, 8]); din('wdw', [2, 128, 2, 31]); din('cvec', [2, 128, 2, 5])
    din('w_pw', [2, 256, 256]); din('wpool_bd', [2, 128, 2, 128])
    din('g_q', [2, 64]); din('g_k', [2, 3, 64])
    din('pe_t', [2, 64, 2, 32]); din('w1h', [2, 64, 2, 32, 64]); din('w1p', [2, 128, 2, 16, 64]); din('w2', [2, 64, 2, 64])
    din('rows_s', [2, 5, 256]); din('w_dw', [2, 31, 256])
    din('state_conv', [2, NSMP, 30, 256]); din('state_pool', [2, NSMP, 15, 256])
    din('cache_win', [2, NSMP, 256, 256])
    din('cache_cmp', [2 * n_phys * 8, 4096]); din('cache_slc', [2 * n_phys * 8, 4096])
    din('ptab_t', [16, NSMP], I32)
    for k, v in consts.items():
        din('c_' + k, v.shape)
    dout('y_p', [T, DM]); dout('y_s', [NSMP, DM])
    dout('conv_p', [2, 30, 256]); dout('pool_p', [2, 15, 256]); dout('win_p', [2, 256, 256])
    dout('cmp_p', [2, T, 256]); dout('slc_p', [2, T, 256])
    dout('conv_s', [2, NSMP, 30, 256]); dout('pool_s', [2, NSMP, 15, 256]); dout('win_s', [2, NSMP, 256, 256])
    dout('cmp_s', [2, NSMP, 256]); dout('slc_s', [2, NSMP, 256])
    dscr('x1', [T, DM]); dscr('xs1', [NSMP, DM]); dscr('pjs_d', [NSMP, DIN]); dscr('nh_d', [NSMP, 8 * 387]); dscr('mixnh_d', [128, 64])
    dscr('qb_d', [T, 512], BF16); dscr('szc_d', [128, 4, T], BF16); dscr('mix_d', [128, 4, T], BF16)

    with ExitStack() as es:
        S = Sched(nc, es)
        k = K(S)
        RDICT = {}
        def RD(name, idx):
            if (name, idx) not in RDICT:
                RDICT[(name, idx)] = Res('%s_%s' % (name, idx))
            return RDICT[(name, idx)]
        def DV(name, res=None):
            return View(D[name], res if res is not None else Res('dram_' + name))
        PS = [S.psum("ps%d" % i) for i in range(8)]

        def ps_next():
            b = PS[S.ps_rr % 8]
            S.ps_rr += 1
            return b

        try:
          identf = S.sbuf("identf", [128, 128], F32)
          identb = S.sbuf("identb", [128, 128], BF16)
          onesb = S.sbuf("onesb", [128, 128], BF16)
          k.memset('pool', identf[:], 0.0)
          S.op('pool', lambda e: e.affine_select(identf.t[:], identf.t[:], pattern=[[-1, 128]], compare_op=ALU.not_equal,
                                                 fill=1.0, base=0, channel_multiplier=1), reads=[identf], writes=[identf])
          k.copy('dve', identb[:], identf[:])
          k.memset('pool', onesb[:], 1.0)
          stop_at('s1')
          stage = [S.sbuf("stage%d" % i, [128, 1024], F32) for i in range(2)]
          st_rr = [0]

          def load_const_bf(name, shape_free):
              src = D['c_' + name]
              npart = src.shape[0]
              t = S.sbuf("k_" + name, [128] + list(shape_free), BF16)
              nfree = int(np.prod(shape_free))
              sb = stage[st_rr[0] % 2]; st_rr[0] += 1
              flat_src = src if len(src.shape) == 2 else src.rearrange("p a b -> p (a b)")
              k.dma('sp', sb[0:npart, 0:nfree], View(flat_src, Res('c')))
              dst = t[0:npart]
              if len(shape_free) == 2:
                  dst = dst.rr("p a b -> p (a b)")
              k.copy('dve', dst, sb[0:npart, 0:nfree])
              return t

          def load_const_f(name, shape_free, npart=128):
              src = D['c_' + name]
              t = S.sbuf("k_" + name, [128] + list(shape_free), F32)
              k.dma('sp', t[0:npart], View(src, Res('c')))
              return t

          if do_prompt:
              CS = load_const_f('cs_p', [NT, 16])
              TRI4 = load_const_bf('tri4', [512])
              TRIU4 = load_const_bf('triu4', [512])
              CMREL4 = load_const_bf('cmrel4', [512])
              SHIFT = S.sbuf("k_shift", [128, NT, 127], BF16)
              for h0 in range(0, NT, 8):
                  sb = stage[st_rr[0] % 2]; st_rr[0] += 1
                  k.dma('sp', sb[:, 0:8 * 127], View(D['c_shift'][:, h0:h0 + 8, :].rearrange("p a b -> p (a b)"), Res('c')))
                  k.copy('dve', SHIFT[:, h0:h0 + 8, :].rr("p a b -> p (a b)"), sb[:, 0:8 * 127])
              NF = load_const_f('nf', [NT, 32])
              ADDC = load_const_f('addc', [NT, 32])
              INVC = load_const_f('invc', [2, 16])
              stop_at('s2')
              KT2 = S.sbuf("KT2", [128, 3, T], BF16)
              KA = S.sbuf("KA", [128, 2, T], BF16)
              VX = S.sbuf("VX", [128, NT, 2, 2, 65], BF16)
              GT = S.sbuf("GT", [128, NT, 24], F32)
              VCX = S.sbuf("VCX", [128, 2, 97], BF16)
              KC2 = S.sbuf("KC2", [128, 127], BF16)
              k.memset('pool', KA[:], 0.0)
              k.memset('pool', VX[:], 1.0)
              k.memset('pool', VCX[:], 1.0)
              for h0 in range(0, T, 1024):
                  sb = stage[st_rr[0] % 2]; st_rr[0] += 1
                  k.dma('sp', sb[64:96, 0:1024], View(D['c_ec'][:, h0:h0 + 1024], Res('c')))
                  k.dma('sp', sb[0:32, 0:1024], View(D['c_ec'][:, h0:h0 + 1024], Res('c')))
                  k.copy('dve', KA[64:96, 0, h0:h0 + 1024], sb[64:96, 0:1024])
                  k.copy('dve', KA[0:32, 1, h0:h0 + 1024], sb[0:32, 0:1024])
              sb = stage[st_rr[0] % 2]; st_rr[0] += 1
              k.dma('sp', sb[0:127, 0:32], View(D['c_ov_p'], Res('c')))
              for g in range(2):
                  k.copy('dve', VCX[0:127, g, 65:97], sb[0:127, 0:32])
              QZ = [S.sbuf("QZ%d" % g, [128, 4, 128], BF16) for g in range(2)]
              QA = [S.sbuf("QA%d" % g, [128, 4, 128], BF16) for g in range(2)]
              for g in range(2):
                  k.memset('pool', QZ[g][:], 0.0)
                  k.memset('pool', QA[g][:], 0.0)

          stop_at('s3')
          ARENA_F32 = 37000
          arena = S.sbuf("arena", [128, ARENA_F32], F32)
          ar_off = [0]

          def ar_reset():
              ar_off[0] = 0

          def ar(name, shape, dt):
              nel = int(np.prod(shape))
              nwords = (nel * (4 if dt == F32 or dt == I32 else 2) + 3) // 4
              nwords = (nwords + 7) // 8 * 8
              a = arena.t[:, ar_off[0]:ar_off[0] + nwords]
              ar_off[0] += nwords
              assert ar_off[0] <= ARENA_F32, (name, ar_off[0])
              if dt != F32:
                  a = a.bitcast(dt)
              a = a[:, 0:nel]
              if len(shape) == 2:
                  a = a.rearrange("p (a b) -> p a b", a=shape[0])
              elif len(shape) == 3:
                  a = a.rearrange("p (a b c) -> p a b c", a=shape[0], b=shape[1])
              elif len(shape) == 4:
                  a = a.rearrange("p (a b c d) -> p a b c d", a=shape[0], b=shape[1], c=shape[2])
              return Res(name, a)

          for l in range(depth):
              def xsrc_t(ti, l=l):
                  rows = slice(ti * 128, (ti + 1) * 128)
                  return View(D['xp'][rows, :], Res('c')) if l == 0 else View(D['x1'][rows, :], RD('x1', ti))
              def xdst_t(ti, l=l):
                  rows = slice(ti * 128, (ti + 1) * 128)
                  return View(D['x1'][rows, :], RD('x1', ti)) if (l == 0 and depth > 1) else View(D['y_p'][rows, :], Res('c'))
              S.barrier(); ar_reset()
              WINB = ar("winb", [8, DIN], BF16)
              wn = ar("wn", [8], F32)
              k.dma('sp', wn[:], DV('wnorm')[l])
              cvec = ar("cvec", [2, 5], F32)
              k.dma('sp', cvec[:], DV('cvec')[l])
              wdw = ar("wdw", [2, 31], F32)
              k.dma('sp', wdw[:], DV('wdw')[l])
              gqb = ar("gqb", [64], F32)
              k.dma('sp', gqb[:], View(D['g_q'][l].partition_broadcast(128), Res('c')))
              gkb = ar("gkb", [3, 64], F32)
              k.dma('sp', gkb[:].rr("p a b -> p (a b)"), View(D['g_k'][l].rearrange("a b -> (a b)").partition_broadcast(128), Res('c')))
              stop_at('s4')
              ci = 0
              for kc in range(8):
                  for c0 in range(0, DIN, 1024):
                      cw = min(1024, DIN - c0)
                      if ci >= int(_os.environ.get('WMAX', '1000')):
                          continue
                      sb = stage[st_rr[0] % 2]; st_rr[0] += 1
                      k.dma('sp', sb[:, 0:cw], DV('w_in')[l, kc * 128:(kc + 1) * 128, c0:c0 + cw])
                      wvar = _os.environ.get('WVAR', 'mix')
                      if wvar == 'none':
                          pass
                      elif wvar == 'act' or (wvar == 'mix' and ci % 2 == 0):
                          k.act(WINB[:, kc, c0:c0 + cw], sb[:, 0:cw], AF.Identity, scale=wn[:, kc:kc + 1])
                      else:
                          k.ts('dve', WINB[:, kc, c0:c0 + cw], sb[:, 0:cw], wn[:, kc:kc + 1], None, ALU.mult)
                      ci += 1
              stop_at('w')
              mark = ar_off[0]
              if do_sample:
                  sample_s0(nc, S, k, D, DV, l, locals())
                  S.barrier(); ar_off[0] = mark
              stop_at('s0')
              if do_prompt:
                  prompt_phase_a(nc, S, k, D, DV, l, locals())
              S.barrier(); ar_reset()
              WOUTB = ar("woutb", [8, DM], BF16)
              mark_w = ar_off[0]
              for kc in range(8):
                  sb = stage[st_rr[0] % 2]; st_rr[0] += 1
                  k.dma('sp', sb[:], DV('w_out')[l, kc * 128:(kc + 1) * 128, :])
                  if kc % 2 == 0:
                      k.copy('act', WOUTB[:, kc, :], sb[:])
                  else:
                      k.copy('dve', WOUTB[:, kc, :], sb[:])
              if do_prompt:
                  prompt_phase_c(nc, S, k, D, DV, l, locals())
              stop_at('pc')
              if do_sample:
                  S.barrier(); ar_off[0] = mark_w
                  sample_phase_s(nc, S, k, D, DV, l, locals())
              stop_at('L0end')
        except StopBuild as ex:
            print('build stopped at', ex)
        S.finish()
    return nc


def prompt_phase_a(nc, S, k, D, DV, l, E):
    ar = E['ar']; WINB = E['WINB']; PS = E['PS']; ps_next = E['ps_next']
    identb = E['identb']; identf = E['identf']; onesb = E['onesb']
    cvec = E['cvec']; wdw = E['wdw']; gqb = E['gqb']; gkb = E['gkb']
    CS = E['CS']; KT2 = E['KT2']; KA = E['KA']; VX = E['VX']; GT = E['GT']; INVC = E['INVC']
    VCX = E['VCX']; KC2 = E['KC2']
    xsrc_t = E['xsrc_t']; stage = E['stage']; st_rr = E['st_rr']; RD = E['RD']
    BT = 2
    BW = BT * 128
    NB = NT // BT
    wpw = ar("wpw", [2, 256], BF16)
    wpl = ar("wpl", [2, 128], BF16)
    for kc in range(2):
        sb = stage[st_rr[0] % 2]; st_rr[0] += 1
        k.dma('sp', sb[:, 0:256], DV('w_pw')[l, kc * 128:(kc + 1) * 128, :])
        k.copy('dve', wpw[:, kc, :], sb[:, 0:256])
    sb = stage[st_rr[0] % 2]; st_rr[0] += 1
    k.dma('sp', sb[:, 0:256], View(D['wpool_bd'][l].rearrange("p a b -> p (a b)"), Res('c')))
    k.copy('dve', wpl[:].rr("p a b -> p (a b)"), sb[:, 0:256])
    XT = [ar("XT%d" % i, [DM], F32) for i in range(2)]
    junks = [ar("junk%d" % i, [DM], BF16) for i in range(2)]
    hbs = [ar("hb%d" % i, [DM], BF16) for i in range(2)]
    ssqs = [ar("ssq%d" % i, [8], F32) for i in range(2)]
    HT = ar("HT", [8, BW], BF16)
    PJs = [ar("PJ%d" % i, [1304], F32) for i in range(2)]
    SQ2s = [ar("SQ2%d" % i, [1280], F32) for i in range(2)]
    SS20s = [ar("SS20%d" % i, [20], F32) for i in range(2)]
    RTs_ = [ar("RT%d" % i, [6, 8, 8], F32) for i in range(2)]
    KBs = [ar("KB%d" % i, [4, 128], BF16) for i in range(2)]
    QBts = [ar("QBt%d" % i, [512], BF16) for i in range(2)]
    AV = ar("AV", [2, BW], F32)
    SG = ar("SG", [2, BW], F32)
    GLUB = ar("GLUB", [2, 32 + BW], BF16)
    BIN = ar("BIN", [2, 16 + BW], F32)
    SZA = ar("SZA", [2, BW], BF16)
    SZB = ar("SZB", [2, BW], BF16)
    SZC = ar("SZCb", [4, BW], BF16)
    MIXB = ar("MIXB", [4, BW], BF16)
    DG = ar("DG", [2, 31, 128], BF16)
    CV = ar("CV", [2, BW], F32)
    CVB = ar("CVB", [2, BW], BF16)
    SQB = ar("SQB", [2, BW], BF16)
    MEAN = ar("MEAN", [BW], F32)
    VAR = ar("VAR", [BW], F32)
    ACTV = ar("ACTV", [2, BW], BF16)
    SA = ar("SA", [16 + BW], F32)
    SBb = ar("SBb", [16 + BW], F32)
    DB = ar("DB", [2, BW], BF16)
    TOUT = ar("TOUT", [256], F32)
    k.memset('pool', GLUB[:], 0.0)
    k.memset('pool', BIN[:], 0.0)
    for c in range(2):
        for j in range(31):
            if j % 2 == 0:
                k.act(DG[:, c, j, :], identf[:], AF.Identity, scale=wdw[:, c, j:j + 1])
            else:
                k.ts('dve', DG[:, c, j, :], identf[:], wdw[:, c, j:j + 1], None, ALU.mult)

    for tb in range(NB):
        for tt in range(BT):
            ti = tb * BT + tt
            xt = XT[ti % 2]
            pb = ti % 2
            junk = junks[pb]; hb = hbs[pb]; ssq = ssqs[pb]; PJ = PJs[pb]; SQ2 = SQ2s[pb]; SS20 = SS20s[pb]; RT = RTs_[pb]; KB = KBs[pb]; QBt = QBts[pb]
            k.dma('sp', xt[:], xsrc_t(ti))
            stop_at('t0')
            k.memset('dve', ssq[:, 0:1], 0.0)
            k.act(junk[:], xt[:], AF.Square, accum=ssq[:, 0:1])
            k.rsqrt(ssq[:, 1:2], ssq[:, 0:1], 1.0 / DM, EPS)
            k.ts('dve', hb[:], xt[:], ssq[:, 1:2], None, ALU.mult)
            stop_at('t1')
            pst = ps_next()
            for kc in range(8):
                k.tr(pst.ap(BF16)[:, kc * 128:(kc + 1) * 128], hb[:, kc * 128:(kc + 1) * 128], identb[:])
            k.copy('act', HT[:, :, tt * 128:(tt + 1) * 128], pst.ap(BF16)[:, 0:1024].rr("p (a b) -> p a b", a=8))
            stop_at('t2')
            for (c0, cw, o0) in ((1280, 512, 0), (1792, 512, 512), (2304, 280, 1024)):
                pp = ps_next()
                for kc in range(8):
                    k.mm(pp.ap()[:, 0:cw], HT[:, kc, tt * 128:(tt + 1) * 128], WINB[:, kc, c0:c0 + cw], kc == 0, kc == 7)
                k.copy('act' if o0 != 512 else 'dve', PJ[:, o0:o0 + cw], pp.ap()[:, 0:cw])
            stop_at('t3')
            k.act(SQ2[:], PJ[:, 0:1280], AF.Square)
            k.red('dve', SS20[:], SQ2[:].rr("p (h d) -> p h d", d=64))
            k.rsqrt(SS20[:], SS20[:], 1.0 / 64, EPS)
            Q3 = PJ[:, 0:512].rr("p (h d) -> p h d", d=64)
            k.tt('dve', Q3, Q3, SS20[:, 0:8].us(2).bc([128, 8, 64]), ALU.mult)
            k.tt('dve', Q3, Q3, gqb[:].us(1).bc([128, 8, 64]), ALU.mult)
            K3 = []
            for b in range(3):
                kk = PJ[:, 512 + b * 256:512 + b * 256 + 128].rr("p (g d) -> p g d", d=64)
                k.tt('dve', kk, kk, SS20[:, 8 + b * 4:8 + b * 4 + 2].us(2).bc([128, 2, 64]), ALU.mult)
                k.tt('dve', kk, kk, gkb[:, b, :].us(1).bc([128, 2, 64]), ALU.mult)
                K3.append(kk)
            stop_at('t4')
            cos8 = CS[:, ti, 0:8]
            sin8 = CS[:, ti, 8:16]
            def rope(X, nh, eng):
                x1 = X[:, :, 0:8]; x2 = X[:, :, 8:16]
                cb = cos8.us(1).bc([128, nh, 8]); sbv = sin8.us(1).bc([128, nh, 8])
                k.tt(eng, RT[:, 0, 0:nh, :], x1, cb, ALU.mult)
                k.tt(eng, RT[:, 1, 0:nh, :], x2, sbv, ALU.mult)
                k.tt(eng, RT[:, 2, 0:nh, :], x2, cb, ALU.mult)
                k.tt(eng, RT[:, 3, 0:nh, :], x1, sbv, ALU.mult)
                k.tt(eng, x1, RT[:, 0, 0:nh, :], RT[:, 1, 0:nh, :], ALU.subtract)
                k.tt(eng, x2, RT[:, 2, 0:nh, :], RT[:, 3, 0:nh, :], ALU.add)
            rope(Q3, 8, 'dve')
            for b in range(3):
                rope(K3[b], 2, 'dve')
            stop_at('t5')
            k.dma('sp', DV('cmp_p')[l, ti * 128:(ti + 1) * 128, :], PJ[:, 512:768], outp=True)
            k.dma('sp', DV('slc_p')[l, ti * 128:(ti + 1) * 128, :], PJ[:, 768:1024], outp=True)
            if ti >= NT - 2:
                k.dma('sp', DV('win_p')[l, (ti - NT + 2) * 128:(ti - NT + 3) * 128, :], PJ[:, 1024:1280], outp=True)
            k.copy('act', QBt[:].rr("p (r g d) -> p r g d", r=4, g=2), PJ[:, 0:512].rr("p (g r d) -> p r g d", g=2, r=4))
            k.dma('sp', View(D['qb_d'][ti * 128:(ti + 1) * 128, :], RD('qb', ti)), QBt[:])
            k.act(GT[:, ti, :], PJ[:, 1280:1304], AF.Sigmoid)
            stop_at('t6')
            k.copy('dve', KB[:, 0:3, :], PJ[:, 512:1280].rr("p (b x) -> p b x", b=3)[:, :, 0:128])
            k.copy('act', KB[:, 3, :], PJ[:, 640:768])
            stop_at('t6a')
            pk = ps_next()
            for j in range(4):
                k.tr(pk.ap(BF16)[:, j * 128:(j + 1) * 128], KB[:, j, :], identb[:])
            stop_at('t6b')
            pkv = pk.ap(BF16)
            cols = slice(ti * 128, (ti + 1) * 128)
            k.copy('act', KT2[:, 0, cols], pkv[:, 0:128])
            k.copy('act', KT2[:, 1, cols], pkv[:, 256:384])
            k.copy('act', KT2[:, 2, cols], pkv[:, 384:512])
            stop_at('t6c')
            k.copy('dve', KA[0:64, 0, cols], pkv[0:64, 128:256])
            stop_at('t6d')
            k.copy('dve', KA[64:128, 1, cols], pkv[64:128, 128:256])
            stop_at('t7')
            k.copy('act', VX[:, ti, :, :, 0:64],
                   PJ[:, 768:1280].rr("p (b x) -> p b x", b=2)[:, :, 128:256].rr("p b (g d) -> p b g d", g=2))
        stop_at('a1')
        t0 = tb * BW
        def fm_proj(c0):
            pp = ps_next()
            for kc in range(8):
                k.mm(pp.ap()[:, 0:BW], WINB[:, kc, c0:c0 + 128], HT[:, kc, :], kc == 0, kc == 7)
            return pp
        for c in range(2):
            pp = fm_proj(c * 128)
            k.copy('dve', AV[:, c, :], pp.ap()[:, 0:BW])
            pp = fm_proj(256 + c * 128)
            k.act(SG[:, c, :], pp.ap()[:, 0:BW], AF.Sigmoid)
            k.tt('dve', AV[:, c, :], AV[:, c, :], SG[:, c, :], ALU.mult)
            k.copy('act', GLUB[:, c, 32:32 + BW], AV[:, c, :])
            pp = fm_proj(512 + c * 128)
            k.act(SZA[:, c, :], pp.ap()[:, 0:BW], AF.Silu)
            pp = fm_proj(768 + c * 128)
            k.copy('dve', BIN[:, c, 16:16 + BW], pp.ap()[:, 0:BW])
            pp = fm_proj(1024 + c * 128)
            k.act(SZB[:, c, :], pp.ap()[:, 0:BW], AF.Silu)
        for c in range(4):
            pp = fm_proj(2584 + c * 128)
            k.act(SZC[:, c, :], pp.ap()[:, 0:BW], AF.Silu)
        k.dma('sp', View(D['szc_d'][:, :, t0:t0 + BW], RD('szc', tb)), SZC[:])
        if tb == NB - 1:
            pp = ps_next()
            for c in range(2):
                k.tr(pp.ap()[:, c * 128:(c + 1) * 128], AV[:, c, BW - 128:BW], identf[:])
            k.copy('dve', TOUT[:], pp.ap()[:, 0:256])
            k.dma('sp', DV('conv_p')[l], TOUT[98:128, :], outp=True)
            pp = ps_next()
            for c in range(2):
                k.tr(pp.ap()[:, c * 128:(c + 1) * 128], BIN[:, c, 16 + BW - 128:16 + BW], identf[:])
            k.copy('dve', TOUT[:], pp.ap()[:, 0:256])
            k.dma('sp', DV('pool_p')[l], TOUT[113:128, :], outp=True)
        stop_at('a2p')
        for c in range(2):
            pp = ps_next()
            for j in range(31):
                k.mm(pp.ap()[:, 0:BW], DG[:, c, j, :], GLUB[:, c, 2 + j:2 + j + BW], j == 0, j == 30)
            k.act(CV[:, c, :], pp.ap()[:, 0:BW], AF.Identity, bias=cvec[:, c, 0:1])
            k.copy('dve', CVB[:, c, :], CV[:, c, :])
            k.act(SQB[:, c, :], CV[:, c, :], AF.Square)
        p1 = ps_next()
        for c in range(2):
            k.mm(p1.ap()[:, 0:BW], onesb[:], CVB[:, c, :], c == 0, c == 1)
        k.act(MEAN[:], p1.ap()[:, 0:BW], AF.Identity, scale=1.0 / 256)
        p2 = ps_next()
        for c in range(2):
            k.mm(p2.ap()[:, 0:BW], onesb[:], SQB[:, c, :], c == 0, c == 1)
        k.tt('dve', VAR[:], MEAN[:], MEAN[:], ALU.mult)
        k.stt('dve', VAR[:], p2.ap()[:, 0:BW], 1.0 / 256, VAR[:], ALU.mult, ALU.subtract)
        k.ts('dve', VAR[:], VAR[:], 0.0, None, ALU.max)
        k.rsqrt(VAR[:], VAR[:], 1.0, EPS)
        for c in range(2):
            k.tt('dve', CV[:, c, :], CV[:, c, :], MEAN[:], ALU.subtract)
            k.tt('dve', CV[:, c, :], CV[:, c, :], VAR[:], ALU.mult)
            k.act(ACTV[:, c, :], CV[:, c, :], AF.Silu, bias=cvec[:, c, 2:3], scale=cvec[:, c, 1:2])
        for co in range(2):
            pp = ps_next()
            for ci_ in range(2):
                k.mm(pp.ap()[:, 0:BW], wpw[:, ci_, co * 128:(co + 1) * 128], ACTV[:, ci_, :], ci_ == 0, ci_ == 1)
            k.stt('dve', MIXB[:, co, :], pp.ap()[:, 0:BW], cvec[:, co, 3:4], SZA[:, co, :], ALU.add, ALU.mult)
        k.copy('dve', GLUB[:, :, 2:32], GLUB[:, :, 2 + BW:32 + BW])
        stop_at('a2c')
        for c in range(2):
            X = BIN[:, c, :]
            n = BW + 15
            k.tt('dve', SA[:, 1:16 + BW], X[:, 1:16 + BW], X[:, 0:15 + BW], ALU.add)
            k.tt('dve', SBb[:, 3:16 + BW], SA[:, 3:16 + BW], SA[:, 1:14 + BW], ALU.add)
            if c == 0:
                tot = (SA, SBb)
                ws = (2, 4)
            else:
                k.tt('dve', SA[:, 7:16 + BW], SBb[:, 7:16 + BW], SBb[:, 3:12 + BW], ALU.add)
                k.tt('dve', SBb[:, 15:16 + BW], SA[:, 15:16 + BW], SA[:, 7:8 + BW], ALU.add)
                tot = (SA, SBb)
                ws = (8, 16)
            for hf in range(2):
                pr = slice(hf * 64, hf * 64 + 64)
                k.ts('dve', SG[pr, c, :], tot[hf][pr, 16:16 + BW], 1.0 / ws[hf], None, ALU.mult)
                k.tt('dve', DB[pr, c, :], SG[pr, c, :], X[pr, 16:16 + BW], ALU.subtract)
                if tb == 0:
                    k.tt('dve', SG[pr, c, 0:16], tot[hf][pr, 16:32], INVC[pr, c, :], ALU.mult)
                    k.tt('dve', DB[pr, c, 0:16], SG[pr, c, 0:16], X[pr, 16:32], ALU.subtract)
            pp = ps_next()
            k.mm(pp.ap()[:, 0:BW], wpl[:, c, :], DB[:, c, :], True, True)
            k.stt('dve', MIXB[:, 2 + c, :], pp.ap()[:, 0:BW], cvec[:, c, 4:5], SZB[:, c, :], ALU.mult, ALU.mult)
        k.copy('dve', BIN[:, :, 1:16], BIN[:, :, 1 + BW:16 + BW])
        k.dma('sp', View(D['mix_d'][:, :, t0:t0 + BW], RD('mix', tb)), MIXB[:])

    stop_at('a')
    S.barrier(); E['ar_reset']()
    w1h = ar("w1h", [2, 32, 64], BF16)
    for kv in range(2):
        for rh in range(2):
            sb = stage[st_rr[0] % 2]; st_rr[0] += 1
            srcv = View(D['w1h'][l, :, kv, rh * 16:(rh + 1) * 16, :].rearrange("p a b -> p (a b)"), Res('c'))
            k.dma('sp', sb[0:64, 0:1024], srcv)
            k.dma('sp', sb[64:128, 0:1024], srcv)
            k.copy('dve', w1h[:, kv, rh * 16:(rh + 1) * 16, :].rr("p a b -> p (a b)"), sb[:, 0:1024])
    pet = ar("pet", [2, 32], BF16)
    sb = stage[st_rr[0] % 2]; st_rr[0] += 1
    k.dma('sp', sb[0:64, 0:64], View(D['pe_t'][l].rearrange("p a b -> p (a b)"), Res('c')))
    k.copy('dve', pet[0:64].rr("p a b -> p (a b)"), sb[0:64, 0:64])
    w2p = ar("w2p", [2, 2, 128], BF16)
    k.memset('pool', w2p[:], 0.0)
    sb = stage[st_rr[0] % 2]; st_rr[0] += 1
    k.dma('sp', sb[0:64, 0:128], View(D['w2'][l].rearrange("p a b -> p (a b)"), Res('c')))
    for kv in range(2):
        k.copy('dve', w2p[0:64, kv, 0, 0:64], sb[0:64, kv * 64:(kv + 1) * 64])
        k.copy('dve', w2p[0:64, kv, 1, 64:128], sb[0:64, kv * 64:(kv + 1) * 64])
    PEB = ar("PEB", [2], F32)
    for kv in range(2):
        pp = ps_next()
        for r in range(32):
            k.mm(pp.ap()[0:64, 0:1], w1h[0:64, kv, r, :], pet[0:64, kv, r:r + 1], r == 0, r == 31)
        k.copy('dve', PEB[0:64, kv:kv + 1], pp.ap()[0:64, 0:1])
    U = ar("U", [254], F32)
    U2 = ar("U2", [254], F32)
    H = ar("H", [2, 254], BF16)
    for kv in range(2):
        slot = 0 if kv == 0 else 2
        for g in range(2):
            pp = ps_next()
            pr = slice(g * 64, g * 64 + 64)
            for r in range(32):
                k.mm(pp.ap()[0:64, 0:127], w1h[pr, kv, r, :], KT2[pr, slot, r:r + 16 * 126 + 1:16], r == 0, r == 31)
            k.act(U[0:64, g * 127:(g + 1) * 127], pp.ap()[0:64, 0:127], AF.Identity, bias=PEB[0:64, kv:kv + 1])
        k.tt('dve', U2[0:64], U[0:64], U[0:64], ALU.mult)
        k.ts('dve', U2[0:64], U2[0:64], 0.044715, 1.0, ALU.mult, ALU.add)
        k.tt('dve', U2[0:64], U2[0:64], U[0:64], ALU.mult)
        k.act(U2[0:64], U2[0:64], AF.Sigmoid, scale=1.5957691216057308)
        k.tt('dve', H[0:64, kv, :], U[0:64], U2[0:64], ALU.mult)
    pp = ps_next()
    for g in range(2):
        k.mm(pp.ap()[:, 0:127], w2p[0:64, 0, g, :], H[0:64, 0, g * 127:(g + 1) * 127], g == 0, g == 1)
    k.copy('dve', KC2[:], pp.ap()[:, 0:127])
    for g in range(2):
        pp = ps_next()
        k.mm(pp.ap()[0:127, 0:64], H[0:64, 1, g * 127:(g + 1) * 127], w2p[0:64, 1, 0, 0:64], True, True)
        k.copy('dve', VCX[0:127, g, 0:64], pp.ap()[0:127, 0:64])


def prompt_phase_c(nc, S, k, D, DV, l, E):
    stop_at('b')
    ar = E['ar']; PS = E['PS']
    identb = E['identb']
    KT2 = E['KT2']; KA = E['KA']; VX = E['VX']; GT = E['GT']; VCX = E['VCX']; KC2 = E['KC2']
    TRI4 = E['TRI4']; TRIU4 = E['TRIU4']; CMREL4 = E['CMREL4']; SHIFT = E['SHIFT']; NF = E['NF']; ADDC = E['ADDC']
    QZ = E['QZ']; QA = E['QA']
    xsrc_t = E['xsrc_t']; xdst_t = E['xdst_t']; stage = E['stage']; st_rr = E['st_rr']; RD = E['RD']
    BT = 2
    WOUTB = E['WOUTB']
    QBt = [ar("QBc%d" % i, [512], BF16) for i in range(2)]
    SZCt = [ar("SZCt%d" % i, [4, 128], BF16) for i in range(2)]
    MIXt = [ar("MIXt%d" % i, [8, 128], BF16) for i in range(2)]
    XTc = [ar("XTc%d" % i, [DM], F32) for i in range(2)]
    NPT = 4
    PT = [ar("PT%d" % i, [512], BF16) for i in range(NPT)]
    OACCs = [ar("OACC%d" % i, [8, 64], F32) for i in range(2)]
    TMPs = [ar("TMP%d" % i, [8, 64], F32) for i in range(2)]
    OBs = [ar("OB%d" % i, [512], BF16) for i in range(2)]
    ZRs = [ar("ZR%d" % i, [3, 8], F32) for i in range(2)]
    COEFs = [ar("COEF%d" % i, [3, 8], F32) for i in range(2)]
    IMPHs = [ar("IMPH%d" % i, [8, 32], F32) for i in range(2)]
    IMPs = [ar("IMP%d" % i, [2, 32], F32) for i in range(2)]
    M8s = [ar("M8%d" % i, [2, 8], F32) for i in range(2)]
    NSELTs = [ar("NSELT%d" % i, [128], BF16) for i in range(2)]
    for i in range(2):
        k.memset('pool', NSELTs[i][:], 0.0)
    pt_rr = [0]
    sc_rr = [0]
    def sc_bank():
        b = PS[sc_rr[0] % 2]; sc_rr[0] += 1
        return b

    for qt in range(NT):
        cols = slice(qt * 128, (qt + 1) * 128)
        qb = QBt[qt % 2]; szc = SZCt[qt % 2]; mixt = MIXt[qt % 2]; xt = XTc[qt % 2]
        pb = qt % 2
        OACC = OACCs[pb]; TMP = TMPs[pb]; OB = OBs[pb]; ZR = ZRs[pb]; COEF = COEFs[pb]; IMPH = IMPHs[pb]; IMP = IMPs[pb]; M8 = M8s[pb]; NSELT = NSELTs[pb]
        k.dma('sp', qb[:], View(D['qb_d'][cols, :], RD('qb', qt)))
        k.dma('sp', szc[:], View(D['szc_d'][:, :, cols], RD('szc', qt // BT)))
        k.dma('sp', mixt[:, 0:4, :], View(D['mix_d'][:, :, cols], RD('mix', qt // BT)))
        k.dma('sp', xt[:], xsrc_t(qt))
        pq = sc_bank()
        for r in range(4):
            k.tr(pq.ap(BF16)[:, r * 128:(r + 1) * 128], qb[:, r * 128:(r + 1) * 128], identb[:])
        pqv = pq.ap(BF16)[:, 0:512].rr("p (r q) -> p r q", r=4)
        k.copy('act', QZ[0][0:64], pqv[0:64])
        k.copy('dve', QZ[1][64:128], pqv[64:128])
        k.copy('act', QA[0][0:64], pqv[0:64])
        k.copy('dve', QA[1][64:128], pqv[64:128])
        nv = min(127, 8 * qt + 7)
        for g in range(2):
            ps_s = sc_bank()
            k.mm(ps_s.ap()[0:nv, :], KC2[:, 0:nv], QZ[g][:].rr("p a b -> p (a b)"), True, False)
            k.mm(ps_s.ap()[0:nv, :], SHIFT[:, qt, 0:nv], CMREL4[:], False, True)
            pt = PT[pt_rr[0] % NPT]; pt_rr[0] += 1
            k.act(pt[0:nv, :], ps_s.ap()[0:nv, :], AF.Exp, scale=0.125)
            for h4 in range(4):
                k.mm(PS[2 + g].ap()[:, h4 * 97:(h4 + 1) * 97], pt[0:nv, h4 * 128:(h4 + 1) * 128], VCX[0:nv, g, :], h4 == 0, h4 == 3)
        for g in range(2):
            oc = PS[2 + g].ap()[:, 0:388].rr("p (h x) -> p h x", h=4)
            hs = slice(g * 4, g * 4 + 4)
            k.ts('dve', ZR[:, 0, hs], oc[:, :, 64], 1e-30, None, ALU.max)
            k.recip(ZR[:, 0, hs], ZR[:, 0, hs])
            k.tt('dve', COEF[:, 0, hs], ZR[:, 0, hs], GT[:, qt, hs], ALU.mult)
            k.tt('dve', OACC[:, hs, :], oc[:, :, 0:64], COEF[:, 0, hs].us(2).bc([128, 4, 64]), ALU.mult)
            k.tt('dve', IMPH[:, hs, :], oc[:, :, 65:97], ZR[:, 0, hs].us(2).bc([128, 4, 32]), ALU.mult)
        k.red('dve', IMP[:], IMPH[:].rr("p (g h) j -> p g j h", g=2))
        k.tt('dve', IMP[:], IMP[:], NF[:, qt, :].us(1).bc([128, 2, 32]), ALU.mult)
        k.tt('dve', IMP[:], IMP[:], ADDC[:, qt, :].us(1).bc([128, 2, 32]), ALU.add)
        for g in range(2):
            k.max8(M8[:, g, :], IMP[:, g, :])
            dstc = NSELT[:, 64:96] if g == 0 else NSELT[:, 0:32]
            k.ts('dve', dstc, IMP[:, g, :], M8[:, g, 7:8], NEGV, ALU.is_lt, ALU.mult)
        stop_at('c1')
        items = []
        for g in range(2):
            kts = [kt for kt in (qt - 2, qt - 1, qt) if kt >= 0]
            for idx, kt in enumerate(kts):
                items.append(dict(lhsT=KT2[:, 1, kt * 128:(kt + 1) * 128], rhs=QZ[g], mask=(TRI4 if kt == qt else (TRIU4 if kt == qt - 2 else None)),
                                  v=VX[:, kt, 1, g, :], bank=PS[6 + g], first=(idx == 0), last=(idx == len(kts) - 1), pre=False))
        stop_at('c2')
        for g in range(2):
            for kt in range(qt + 1):
                items.append(dict(lhsT=KA[:, g, kt * 128:(kt + 1) * 128], rhs=QA[g], mask=(TRI4 if kt == qt else None),
                                  v=VX[:, kt, 0, g, :], bank=PS[4 + g], first=(kt == 0), last=(kt == qt), pre=(g == 0 and kt == 0)))
        pend = None
        for it in items + [None]:
            if it is not None:
                if it['pre']:
                    pn = PS[sc_rr[0] % 2]
                    k.tr(pn.ap(BF16)[:, 0:128], NSELT[:], identb[:])
                    k.copy('act', QA[0][64:96], pn.ap(BF16)[64:96, 0:128].us(1).bc([32, 4, 128]))
                    k.copy('dve', QA[1][0:32], pn.ap(BF16)[0:32, 0:128].us(1).bc([32, 4, 128]))
                ps_s = sc_bank()
                k.mm(ps_s.ap(), it['lhsT'], it['rhs'][:].rr("p a b -> p (a b)"), True, it['mask'] is None)
                if it['mask'] is not None:
                    k.mm(ps_s.ap(), identb[:], it['mask'][:], False, True)
                it['ps'] = ps_s
            if pend is not None:
                pt = PT[pt_rr[0] % NPT]; pt_rr[0] += 1
                k.act(pt[:], pend['ps'].ap(), AF.Exp, scale=0.125)
                for h4 in range(4):
                    k.mm(pend['bank'].ap()[:, h4 * 65:(h4 + 1) * 65], pt[:, h4 * 128:(h4 + 1) * 128], pend['v'],
                         pend['first'] and h4 == 0, pend['last'] and h4 == 3)
            pend = it
        stop_at('c3')
        for (br, pb) in ((1, 4), (2, 6)):
            for g in range(2):
                o = PS[pb + g].ap()[:, 0:260].rr("p (h x) -> p h x", h=4)
                hs = slice(g * 4, g * 4 + 4)
                k.recip(ZR[:, br, hs], o[:, :, 64])
                k.tt('dve', COEF[:, br, hs], ZR[:, br, hs], GT[:, qt, br * 8 + g * 4:br * 8 + g * 4 + 4], ALU.mult)
                k.tt('dve', TMP[:, hs, :], o[:, :, 0:64], COEF[:, br, hs].us(2).bc([128, 4, 64]), ALU.mult)
            k.tt('dve', OACC[:], OACC[:], TMP[:], ALU.add)
        k.copy('act', OB[:], OACC[:].rr("p h d -> p (h d)"))
        po = sc_bank()
        for c in range(4):
            k.tr(po.ap(BF16)[:, c * 128:(c + 1) * 128], OB[:, c * 128:(c + 1) * 128], identb[:])
        k.tt('dve', mixt[:, 4:8, :], po.ap(BF16)[:, 0:512].rr("p (c q) -> p c q", c=4), szc[:], ALU.mult)
        for nb in range(2):
            for mc in range(8):
                k.mm(PS[2 + nb].ap(), mixt[:, mc, :], WOUTB[:, mc, nb * 512:(nb + 1) * 512], mc == 0, mc == 7)
            k.tt('dve', xt[:, nb * 512:(nb + 1) * 512], xt[:, nb * 512:(nb + 1) * 512], PS[2 + nb].ap(), ALU.add)
        k.dma('sp', xdst_t(qt), xt[:], outp=True)


def sample_s0(nc, S, k, D, DV, l, E):
    ar = E['ar']; WINB = E['WINB']; ps_next = E['ps_next']; identb = E['identb']
    gqb = E['gqb']; gkb = E['gkb']; RD = E['RD']; depth = E['depth']
    N = NSMP
    XS = ar("XS", [DM], F32)
    junk = ar("junk_s", [DM], BF16)
    hb = ar("hb_s", [DM], BF16)
    ssq = ar("ssq_s", [8], F32)
    HTs = ar("HTs", [8, N], BF16)
    PJS = ar("PJS", [DIN], F32)
    SQ2 = ar("SQ2s", [1280], F32)
    SS20 = ar("SS20s", [20], F32)
    RT = ar("RTs", [6, 8, 8], F32)
    CSs = ar("CSs", [16], F32)
    k.dma('sp', CSs[0:N], DV('c_cs_s'))
    xsrc = DV('xs') if l == 0 else View(D['xs1'], RD('xs1', 0))
    k.dma('sp', XS[0:N], xsrc)
    k.memset('pool', ssq[0:N, 0:1], 0.0)
    k.act(junk[0:N], XS[0:N], AF.Square, accum=ssq[0:N, 0:1])
    k.rsqrt(ssq[0:N, 1:2], ssq[0:N, 0:1], 1.0 / DM, EPS)
    k.ts('dve', hb[0:N], XS[0:N], ssq[0:N, 1:2], None, ALU.mult)
    pst = ps_next()
    for kc in range(8):
        k.tr(pst.ap(BF16)[:, kc * N:(kc + 1) * N], hb[0:N, kc * 128:(kc + 1) * 128], identb[0:N, 0:N])
    k.copy('act', HTs[:].rr("p a b -> p (a b)"), pst.ap(BF16)[:, 0:8 * N])
    for c0 in range(0, DIN, 512):
        cw = min(512, DIN - c0)
        pp = ps_next()
        for kc in range(8):
            k.mm(pp.ap()[0:N, 0:cw], HTs[:, kc, :], WINB[:, kc, c0:c0 + cw], kc == 0, kc == 7)
        k.copy('act' if (c0 // 512) % 2 == 0 else 'dve', PJS[0:N, c0:c0 + cw], pp.ap()[0:N, 0:cw])
    QO = 1280; KO = 1792
    k.act(SQ2[0:N], PJS[0:N, QO:QO + 1280], AF.Square)
    k.red('dve', SS20[0:N], SQ2[0:N].rr("p (h d) -> p h d", d=64))
    k.rsqrt(SS20[0:N], SS20[0:N], 1.0 / 64, EPS)
    Q3 = PJS[0:N, QO:QO + 512].rr("p (h d) -> p h d", d=64)
    k.tt('dve', Q3, Q3, SS20[0:N, 0:8].us(2).bc([N, 8, 64]), ALU.mult)
    k.tt('dve', Q3, Q3, gqb[0:N].us(1).bc([N, 8, 64]), ALU.mult)
    K3 = []
    for b in range(3):
        kk = PJS[0:N, KO + b * 256:KO + b * 256 + 128].rr("p (g d) -> p g d", d=64)
        k.tt('dve', kk, kk, SS20[0:N, 8 + b * 4:8 + b * 4 + 2].us(2).bc([N, 2, 64]), ALU.mult)
        k.tt('dve', kk, kk, gkb[0:N, b, :].us(1).bc([N, 2, 64]), ALU.mult)
        K3.append(kk)
    cos8 = CSs[0:N, 0:8]; sin8 = CSs[0:N, 8:16]
    def rope(X, nh):
        x1 = X[:, :, 0:8]; x2 = X[:, :, 8:16]
        cb = cos8.us(1).bc([N, nh, 8]); sbv = sin8.us(1).bc([N, nh, 8])
        k.tt('dve', RT[0:N, 0, 0:nh, :], x1, cb, ALU.mult)
        k.tt('dve', RT[0:N, 1, 0:nh, :], x2, sbv, ALU.mult)
        k.tt('dve', RT[0:N, 2, 0:nh, :], x2, cb, ALU.mult)
        k.tt('dve', RT[0:N, 3, 0:nh, :], x1, sbv, ALU.mult)
        k.tt('dve', x1, RT[0:N, 0, 0:nh, :], RT[0:N, 1, 0:nh, :], ALU.subtract)
        k.tt('dve', x2, RT[0:N, 2, 0:nh, :], RT[0:N, 3, 0:nh, :], ALU.add)
    rope(Q3, 8)
    for b in range(3):
        rope(K3[b], 2)
    k.act(SQ2[0:N, 0:256], PJS[0:N, 256:512], AF.Sigmoid)
    k.tt('dve', PJS[0:N, 0:256], PJS[0:N, 0:256], SQ2[0:N, 0:256], ALU.mult)
    k.act(PJS[0:N, 512:768], PJS[0:N, 512:768], AF.Silu)
    k.act(PJS[0:N, 1024:1280], PJS[0:N, 1024:1280], AF.Silu)
    k.act(PJS[0:N, 2560:2584], PJS[0:N, 2560:2584], AF.Sigmoid)
    k.act(PJS[0:N, 2584:3096], PJS[0:N, 2584:3096], AF.Silu)
    k.dma('sp', View(D['pjs_d'], RD('pjs', l)), PJS[0:N])
    k.dma('sp', DV('cmp_s')[l], PJS[0:N, KO:KO + 256], outp=True)
    k.dma('sp', DV('slc_s')[l], PJS[0:N, KO + 256:KO + 512], outp=True)
    k.dma('sp', DV('win_s')[l, :, 255, :], PJS[0:N, KO + 512:KO + 768], outp=True)
    k.dma('sp', DV('conv_s')[l, :, 29, :], PJS[0:N, 0:256], outp=True)
    k.dma('sp', DV('pool_s')[l, :, 14, :], PJS[0:N, 768:1024], outp=True)
    k.dma('sp', DV('conv_s')[l, :, 0:29, :], DV('state_conv')[l, :, 1:30, :], outp=True)
    k.dma('sp', DV('pool_s')[l, :, 0:14, :], DV('state_pool')[l, :, 1:15, :], outp=True)
    k.dma('sp', DV('win_s')[l, :, 0:255, :], DV('cache_win')[l, :, 1:256, :], outp=True)


def sample_phase_s(nc, S, k, D, DV, l, E):
    ar = E['ar']; PS = E['PS']; identb = E['identb']; identf = E['identf']
    stage = E['stage']; st_rr = E['st_rr']; RD = E['RD']; WOUTB = E['WOUTB']; depth = E['depth']
    n_phys = E['n_phys']
    N = NSMP
    sc_rr = [0]
    def sc_bank():
        b = PS[sc_rr[0] % 2]; sc_rr[0] += 1
        return b
    def nstage():
        sb = stage[st_rr[0] % 2]; st_rr[0] += 1
        return sb
    w1h = ar("w1h_s", [2, 32, 64], BF16)
    for kv in range(2):
        for rh in range(2):
            sb = nstage()
            srcv = View(D['w1h'][l, :, kv, rh * 16:(rh + 1) * 16, :].rearrange("p a b -> p (a b)"), Res('c'))
            k.dma('sp', sb[0:64, 0:1024], srcv)
            k.dma('sp', sb[64:128, 0:1024], srcv)
            k.copy('dve', w1h[:, kv, rh * 16:(rh + 1) * 16, :].rr("p a b -> p (a b)"), sb[:, 0:1024])
    pet = ar("pet_s", [2, 32], BF16)
    sb = nstage()
    k.dma('sp', sb[0:64, 0:64], View(D['pe_t'][l].rearrange("p a b -> p (a b)"), Res('c')))
    k.copy('dve', pet[0:64].rr("p a b -> p (a b)"), sb[0:64, 0:64])
    w2p = ar("w2p_s", [2, 2, 128], BF16)
    k.memset('pool', w2p[:], 0.0)
    sb = nstage()
    k.dma('sp', sb[0:64, 0:128], View(D['w2'][l].rearrange("p a b -> p (a b)"), Res('c')))
    for kv in range(2):
        k.copy('dve', w2p[0:64, kv, 0, 0:64], sb[0:64, kv * 64:(kv + 1) * 64])
        k.copy('dve', w2p[0:64, kv, 1, 64:128], sb[0:64, kv * 64:(kv + 1) * 64])
    PEB = ar("PEB_s", [2], F32)
    for kv in range(2):
        pp = sc_bank()
        for r in range(32):
            k.mm(pp.ap()[0:64, 0:1], w1h[0:64, kv, r, :], pet[0:64, kv, r:r + 1], r == 0, r == 31)
        k.copy('dve', PEB[0:64, kv:kv + 1], pp.ap()[0:64, 0:1])
    VCXs = ar("VCXs", [2, 98], BF16)
    k.memset('pool', VCXs[:], 1.0)
    sb = nstage()
    k.dma('sp', sb[0:127, 0:33], DV('c_ov_s'))
    for g in range(2):
        k.copy('dve', VCXs[0:127, g, 65:98], sb[0:127, 0:33])
    GRP = ar("GRP", [32], F32); k.dma('sp', GRP[:], DV('c_grp'))
    NFs = ar("NFs", [33], F32); k.dma('sp', NFs[0:32], DV('c_nfs'))
    ADDs = ar("ADDs", [33], F32); k.dma('sp', ADDs[0:32], DV('c_adds'))
    E33 = ar("E33", [128], F32); k.dma('sp', E33[0:33], DV('c_e33'))
    SEL8 = ar("SEL8", [128], F32); k.dma('sp', SEL8[0:16], DV('c_sel8'))
    PMOD = ar("PMOD", [8], F32); k.dma('sp', PMOD[:, 0:1], DV('c_pmod8'))
    PTI = ar("PTI", [16], I32)
    PTF = ar("PTF", [16], F32)
    IDXF = ar("IDXF", [16], F32)
    IDX = ar("IDX", [16], I32)
    k.dma('sp', PTI[0:16], DV('ptab_t'))
    k.copy('dve', PTF[0:16], PTI[0:16])
    pp = sc_bank()
    k.mm(pp.ap()[:, 0:16], SEL8[0:16, :], PTF[0:16, :], True, True)
    k.ts('dve', IDXF[:], pp.ap()[:, 0:16], 8.0, float(l * n_phys * 8), ALU.mult, ALU.add)
    k.ts('dve', IDXF[:], IDXF[:], PMOD[:, 0:1], None, ALU.add)
    k.copy('dve', IDX[:], IDXF[:])
    stop_at('q0')
    PJ2 = ar("PJ2", [DIN], F32)
    k.dma('sp', PJ2[0:N], View(D['pjs_d'], RD('pjs', l)))
    MIXs = ar("MIXs", [DM], F32)
    KO = 1792
    ar_off = E['ar_off']
    mark2 = ar_off[0]
    wpw = ar("wpw_s", [2, 256], BF16)
    wpl = ar("wpl_s", [2, 128], BF16)
    for kc in range(2):
        sb = nstage()
        k.dma('sp', sb[:, 0:256], DV('w_pw')[l, kc * 128:(kc + 1) * 128, :])
        k.copy('dve', wpw[:, kc, :], sb[:, 0:256])
    sb = nstage()
    k.dma('sp', sb[:, 0:256], View(D['wpool_bd'][l].rearrange("p a b -> p (a b)"), Res('c')))
    k.copy('dve', wpl[:].rr("p a b -> p (a b)"), sb[:, 0:256])
    ROWS = ar("ROWS", [5, 256], F32)
    k.dma('sp', ROWS[0:N].rr("p a b -> p (a b)"), View(D['rows_s'][l].rearrange("a b -> (a b)").partition_broadcast(N), Res('c')))
    XC = ar("XCs", [31, 128], F32)
    WD = ar("WDs", [31, 128], F32)
    CVs = ar("CVs", [256], F32)
    T1 = ar("T1s", [256], F32)
    ST = ar("STs", [8], F32)
    for ch in range(2):
        cs_ = slice(ch * 128, (ch + 1) * 128)
        k.dma('sp', XC[0:N, 0:30, :], DV('state_conv')[l, :, :, cs_])
        k.copy('pool', XC[0:N, 30, :], PJ2[0:N, cs_])
        k.dma('sp', WD[0:N], View(D['w_dw'][l, :, cs_].partition_broadcast(N), Res('c')))
        k.tt('dve', XC[0:N], XC[0:N], WD[0:N], ALU.mult)
        k.red('dve', CVs[0:N, cs_], XC[0:N].rr("p j c -> p c j"))
    k.tt('dve', CVs[0:N], CVs[0:N], ROWS[0:N, 0, :], ALU.add)
    k.red('dve', ST[0:N, 0:1], CVs[0:N])
    k.ts('dve', ST[0:N, 0:1], ST[0:N, 0:1], 1.0 / 256, None, ALU.mult)
    k.ts('dve', CVs[0:N], CVs[0:N], ST[0:N, 0:1], None, ALU.subtract)
    k.tt('dve', T1[0:N], CVs[0:N], CVs[0:N], ALU.mult)
    k.red('dve', ST[0:N, 1:2], T1[0:N])
    k.rsqrt(ST[0:N, 2:3], ST[0:N, 1:2], 1.0 / 256, EPS)
    k.ts('dve', CVs[0:N], CVs[0:N], ST[0:N, 2:3], None, ALU.mult)
    k.tt('dve', CVs[0:N], CVs[0:N], ROWS[0:N, 1, :], ALU.mult)
    k.tt('dve', CVs[0:N], CVs[0:N], ROWS[0:N, 2, :], ALU.add)
    ACs = ar("ACs", [256], BF16)
    k.act(ACs[0:N], CVs[0:N], AF.Silu)
    ATs = ar("ATs", [2, N], BF16)
    pp = sc_bank()
    for c in range(2):
        k.tr(pp.ap(BF16)[:, c * N:(c + 1) * N], ACs[0:N, c * 128:(c + 1) * 128], identb[0:N, 0:N])
    k.copy('act', ATs[:].rr("p a b -> p (a b)"), pp.ap(BF16)[:, 0:2 * N])
    pp = sc_bank()
    for c in range(2):
        k.mm(pp.ap()[0:N, 0:256], ATs[:, c, :], wpw[:, c, :], c == 0, c == 1)
    k.tt('dve', T1[0:N], pp.ap()[0:N, 0:256], ROWS[0:N, 3, :], ALU.add)
    k.tt('dve', MIXs[0:N, 0:256], T1[0:N], PJ2[0:N, 512:768], ALU.mult)
    stop_at('q1')
    XP = ar("XPs", [16, 256], F32)
    k.dma('sp', XP[0:N, 0:15, :], DV('state_pool')[l])
    k.copy('pool', XP[0:N, 15, :], PJ2[0:N, 768:1024])
    Ds = ar("Ds", [256], F32)
    for gi, w in enumerate((2, 4, 8, 16)):
        gs = slice(gi * 64, (gi + 1) * 64)
        k.red('dve', Ds[0:N, gs], XP[0:N, 16 - w:16, gs].rr("p j c -> p c j"))
        k.stt('dve', Ds[0:N, gs], Ds[0:N, gs], 1.0 / w, PJ2[0:N, 768 + gi * 64:768 + (gi + 1) * 64], ALU.mult, ALU.subtract)
    Dsb = ar("Dsb", [256], BF16)
    k.copy('dve', Dsb[0:N], Ds[0:N])
    DTs = ar("DTs", [2, N], BF16)
    pp = sc_bank()
    for c in range(2):
        k.tr(pp.ap(BF16)[:, c * N:(c + 1) * N], Dsb[0:N, c * 128:(c + 1) * 128], identb[0:N, 0:N])
    k.copy('act', DTs[:].rr("p a b -> p (a b)"), pp.ap(BF16)[:, 0:2 * N])
    pp = sc_bank()
    for c in range(2):
        k.mm(pp.ap()[0:N, c * 128:(c + 1) * 128], DTs[:, c, :], wpl[:, c, :], c == 0, c == 1)
    k.tt('dve', T1[0:N], pp.ap()[0:N, 0:256], ROWS[0:N, 4, :], ALU.mult)
    k.tt('dve', MIXs[0:N, 256:512], T1[0:N], PJ2[0:N, 1024:1280], ALU.mult)
    stop_at('q2')
    S.barrier(); ar_off[0] = mark2
    NHS = ar("NHS", [8, 387], F32)
    k.copy('pool', NHS[0:N, :, 0:64], PJ2[0:N, 1280:1792].rr("p (h d) -> p h d", d=64))
    for j, off in enumerate((KO + 256, KO + 384, KO + 512, KO + 640)):
        k.copy('pool', NHS[0:N, :, 64 + j * 64:128 + j * 64].rr("p (g r) d -> p g r d", g=2),
               PJ2[0:N, off:off + 128].rr("p (g d) -> p g d", g=2).us(2).bc([N, 2, 4, 64]))
    k.copy('pool', NHS[0:N, :, 320:384], PJ2[0:N, 2584:3096].rr("p (h d) -> p h d", d=64))
    k.copy('pool', NHS[0:N, :, 384:387], PJ2[0:N, 2560:2584].rr("p (b h) -> p h b", b=3))
    k.dma('sp', View(D['nh_d'], RD('nh', l)), NHS[0:N].rr("p a b -> p (a b)"))
    NH = ar("NH", [387], F32)
    k.dma('sp', NH[:], View(D['nh_d'].rearrange("n (h x) -> (n h) x", h=8), RD('nh', l)))
    QPAD = ar("QPAD", [2, 8, 128], BF16)
    k.memset('pool', QPAD[0:N], 0.0)
    k.copy('pool', QPAD[0:N, 0, :, 0:64], PJ2[0:N, 1280:1792].rr("p (h d) -> p h d", d=64))
    k.copy('pool', QPAD[0:N, 1, :, 64:128], PJ2[0:N, 1280:1792].rr("p (h d) -> p h d", d=64))
    QTZ = ar("QTZ", [2, 8, N], BF16)
    pp = sc_bank()
    for par in range(2):
        for h in range(8):
            k.tr(pp.ap(BF16)[:, (par * 8 + h) * N:(par * 8 + h + 1) * N], QPAD[0:N, par, h, :], identb[0:N, 0:N])
    k.copy('act', QTZ[:].rr("p a b c -> p (a b c)"), pp.ap(BF16)[:, 0:16 * N])
    stop_at('q3')
    AC = ar("AC", [4096], F32)
    XTc = ar("XTc", [2, 16, 128], BF16)
    U = ar("Us", [254], F32)
    U2 = ar("U2s", [254], F32)
    H = ar("Hs", [2, 254], BF16)
    KC2s = ar("KC2s", [127], BF16)
    PTs = ar("PTs", [8], BF16)
    OCT = PS[2]; OST = PS[3]; OWT = PS[4]
    cflat = View(D['cache_cmp'], Res('c'))
    sflat = View(D['cache_slc'], Res('c'))
    def gather(dst, src, n):
        S.dma('pool', None, None, reads=_rs([IDX[:]]), writes=_rs([dst[:]]),
              fn=lambda e: e.indirect_dma_start(out=dst.t[:, :], out_offset=None, in_=src.ap[:, :],
                                                in_offset=bass.IndirectOffsetOnAxis(ap=IDX.t[:, n:n + 1], axis=0)))
    for n in range(N):
        gather(AC, cflat, n)
        stop_at('q4')
        for kv in range(2):
            for r0 in range(0, 16, 4):
                pp = PS[5 + ((kv * 4 + r0 // 4) % 2)]
                for rr in range(4):
                    r = r0 + rr
                    k.tr(pp.ap()[:, rr * 128:(rr + 1) * 128], AC[:, r * 256 + kv * 128:r * 256 + kv * 128 + 128], identf[:])
                k.copy('act' if (r0 // 4) % 2 == 0 else 'dve', XTc[:, kv, r0:r0 + 4, :].rr("p a b -> p (a b)"), pp.ap()[:, 0:512])
        stop_at('q5')
        for kv in range(2):
            for g in range(2):
                pp = sc_bank()
                pr = slice(g * 64, g * 64 + 64)
                i = 0
                for jp in range(2):
                    for r in range(16):
                        k.mm(pp.ap()[0:64, 0:127], w1h[pr, kv, jp * 16 + r, :], XTc[pr, kv, r, jp:jp + 127], i == 0, i == 31)
                        i += 1
                k.act(U[0:64, g * 127:(g + 1) * 127], pp.ap()[0:64, 0:127], AF.Identity, bias=PEB[0:64, kv:kv + 1])
            k.tt('dve', U2[0:64], U[0:64], U[0:64], ALU.mult)
            k.ts('dve', U2[0:64], U2[0:64], 0.044715, 1.0, ALU.mult, ALU.add)
            k.tt('dve', U2[0:64], U2[0:64], U[0:64], ALU.mult)
            k.act(U2[0:64], U2[0:64], AF.Sigmoid, scale=1.5957691216057308)
            k.tt('dve', H[0:64, kv, :], U[0:64], U2[0:64], ALU.mult)
        stop_at('q6')
        pp = sc_bank()
        for g in range(2):
            k.mm(pp.ap()[:, 0:127], w2p[0:64, 0, g, :], H[0:64, 0, g * 127:(g + 1) * 127], g == 0, g == 1)
        k.copy('dve', KC2s[:], pp.ap()[:, 0:127])
        for g in range(2):
            pp = sc_bank()
            k.mm(pp.ap()[0:127, 0:64], H[0:64, 1, g * 127:(g + 1) * 127], w2p[0:64, 1, 0, 0:64], True, True)
            k.copy('dve', VCXs[0:127, g, 0:64], pp.ap()[0:127, 0:64])
        pp = sc_bank()
        for g in range(2):
            k.mm(pp.ap()[0:127, g * 4:(g + 1) * 4], KC2s[:, 0:127], QTZ[:, g, 4 * g:4 * g + 4, n], True, True)
        k.act(PTs[0:127, :], pp.ap()[0:127, 0:8], AF.Exp, scale=0.125)
        for g in range(2):
            k.mm(OCT.ap()[0:98, n * 8 + 4 * g:n * 8 + 4 * g + 4], VCXs[0:127, g, :], PTs[0:127, 4 * g:4 * g + 4], True, True)
    stop_at('q7')
    OT = ar("OT", [128], F32)
    OCN = ar("OCN", [98], F32)
    k.copy('dve', OT[0:98], OCT.ap()[0:98, 0:128])
    pp = sc_bank()
    k.tr(pp.ap()[:, 0:98], OT[0:98, :], identf[0:98, 0:98])
    k.copy('dve', OCN[:], pp.ap()[:, 0:98])
    RZ = ar("RZs", [8], F32)
    k.ts('dve', RZ[:, 0:1], OCN[:, 64:65], 1e-30, None, ALU.max)
    k.recip(RZ[:, 0:1], RZ[:, 0:1])
    IMPN = ar("IMPN", [33], F32)
    k.ts('dve', IMPN[:], OCN[:, 65:98], RZ[:, 0:1], None, ALU.mult)
    pp = sc_bank()
    k.mm(pp.ap()[0:32, 0:33], GRP[:, :], IMPN[:, :], True, True)
    SC = ar("SCs", [33], F32)
    k.tt('dve', SC[0:32], pp.ap()[0:32, 0:33], NFs[0:32], ALU.mult)
    k.tt('dve', SC[0:32], SC[0:32], ADDs[0:32], ALU.add)
    M8 = ar("M8s", [8], F32)
    k.max8(M8[0:32], SC[0:32])
    NEG = ar("NEGs", [33], F32)
    k.ts('dve', NEG[0:32], SC[0:32], M8[0:32, 7:8], NEGV, ALU.is_lt, ALU.mult)
    pp = sc_bank()
    k.tr(pp.ap()[0:33, 0:32], NEG[0:32, :], identf[0:32, 0:32])
    NEGT = ar("NEGT", [32], F32)
    k.copy('dve', NEGT[0:33], pp.ap()[0:33, 0:32])
    pp = sc_bank()
    k.mm(pp.ap()[:, 0:32], E33[0:33, :], NEGT[0:33, :], True, True)
    NEGPC = ar("NEGPC", [32], F32)
    k.copy('dve', NEGPC[:], pp.ap()[:, 0:32])
    stop_at('q8')
    XTs = ar("XTs", [16, 128], BF16)
    Vs = ar("Vs", [16, 2, 65], BF16)
    k.memset('pool', Vs[:], 1.0)
    PT2 = ar("PT2", [2, 64], BF16)
    Ws = ar("Ws", [2, 256], F32)
    KTw = ar("KTw", [2, 128], BF16)
    Vw = ar("Vw", [2, 2, 65], BF16)
    k.memset('pool', Vw[:], 1.0)
    PTw = ar("PTw", [16], BF16)
    for n in range(N):
        gather(AC, sflat, n)
        for r0 in range(0, 16, 4):
            pp = PS[5 + (r0 // 4) % 2]
            for rr in range(4):
                r = r0 + rr
                k.tr(pp.ap()[:, rr * 128:(rr + 1) * 128], AC[:, r * 256:r * 256 + 128], identf[:])
            k.copy('act' if (r0 // 4) % 2 == 0 else 'dve', XTs[:, r0:r0 + 4, :].rr("p a b -> p (a b)"), pp.ap()[:, 0:512])
        k.copy('dve' if n % 2 == 0 else 'act', Vs[:, :, :, 0:64], AC[:].rr("p (r x) -> p r x", r=16)[:, :, 128:256].rr("p r (g d) -> p r g d", g=2))
        pp = sc_bank()
        for g in range(2):
            for r in range(16):
                k.mm(pp.ap()[:, (g * 16 + r) * 4:(g * 16 + r) * 4 + 4], XTs[:, r, :], QTZ[:, g, 4 * g:4 * g + 4, n], True, True)
        for g in range(2):
            k.act(PT2[:, g, :], pp.ap()[:, g * 64:(g + 1) * 64], AF.Exp, scale=0.125, bias=NEGPC[:, 2 * n + g:2 * n + g + 1])
        for g in range(2):
            for r in range(16):
                k.mm(OST.ap()[0:65, n * 8 + 4 * g:n * 8 + 4 * g + 4], Vs[:, r, g, :], PT2[:, g, r * 4:(r + 1) * 4], r == 0, r == 15)
        stop_at('q9')
        k.dma('sp', Ws[:], DV('cache_win')[l, n].rr("(t p) f -> p t f", t=2))
        pp = PS[7]
        for t in range(2):
            k.tr(pp.ap()[:, t * 128:(t + 1) * 128], Ws[:, t, 0:128], identf[:])
        k.copy('act', KTw[:].rr("p a b -> p (a b)"), pp.ap()[:, 0:256])
        k.copy('dve', Vw[:, :, :, 0:64], Ws[:, :, 128:256].rr("p t (g d) -> p t g d", g=2))
        k.memset('pool', Vw[0:1, 0, :, :], 0.0)
        pp = sc_bank()
        for g in range(2):
            for t in range(2):
                k.mm(pp.ap()[:, (g * 2 + t) * 4:(g * 2 + t) * 4 + 4], KTw[:, t, :], QTZ[:, g, 4 * g:4 * g + 4, n], True, True)
        k.act(PTw[:], pp.ap()[:, 0:16], AF.Exp, scale=0.125)
        for g in range(2):
            for t in range(2):
                k.mm(OWT.ap()[0:65, n * 8 + 4 * g:n * 8 + 4 * g + 4], Vw[:, t, g, :], PTw[:, (g * 2 + t) * 4:(g * 2 + t) * 4 + 4], t == 0, t == 1)
    stop_at('q10')
    ONH = ar("ONH", [64], F32)
    OBR = ar("OBR", [65], F32)
    PN = ar("PN", [8], F32)
    T64 = ar("T64", [64], F32)
    k.tt('dve', PN[:, 0:1], RZ[:, 0:1], NH[:, 384:385], ALU.mult)
    k.ts('dve', ONH[:], OCN[:, 0:64], PN[:, 0:1], None, ALU.mult)
    for bi, (bank, ko, vo, go) in enumerate(((OST, 64, 128, 385), (OWT, 192, 256, 386))):
        k.copy('dve', OT[0:65], bank.ap()[0:65, 0:128])
        pp = sc_bank()
        k.tr(pp.ap()[:, 0:65], OT[0:65, :], identf[0:65, 0:65])
        k.copy('dve', OBR[:], pp.ap()[:, 0:65])
        k.tt('dve', T64[:], NH[:, 0:64], NH[:, ko:ko + 64], ALU.mult)
        k.red('dve', PN[:, 1:2], T64[:])
        k.act(PN[:, 2:3], PN[:, 1:2], AF.Exp, scale=0.125)
        k.stt('dve', OBR[:, 0:64], NH[:, vo:vo + 64], PN[:, 2:3], OBR[:, 0:64], ALU.mult, ALU.add)
        k.tt('dve', PN[:, 3:4], OBR[:, 64:65], PN[:, 2:3], ALU.add)
        k.recip(PN[:, 3:4], PN[:, 3:4])
        k.tt('dve', PN[:, 4:5], PN[:, 3:4], NH[:, go:go + 1], ALU.mult)
        k.stt('dve', ONH[:], OBR[:, 0:64], PN[:, 4:5], ONH[:], ALU.mult, ALU.add)
    k.tt('dve', ONH[:], ONH[:], NH[:, 320:384], ALU.mult)
    k.dma('sp', View(D['mixnh_d'], RD('mixnh', l)), ONH[:])
    k.dma('sp', MIXs[0:N, 512:1024], View(D['mixnh_d'].rearrange("(n h) d -> n (h d)", h=8), RD('mixnh', l)))
    stop_at('q11')
    MB = ar("MBs", [DM], BF16)
    k.copy('dve', MB[0:N], MIXs[0:N])
    MT = ar("MTs", [8, N], BF16)
    pp = sc_bank()
    for c in range(8):
        k.tr(pp.ap(BF16)[:, c * N:(c + 1) * N], MB[0:N, c * 128:(c + 1) * 128], identb[0:N, 0:N])
    k.copy('act', MT[:].rr("p a b -> p (a b)"), pp.ap(BF16)[:, 0:8 * N])
    XS2 = ar("XS2", [DM], F32)
    xsrc = DV('xs') if l == 0 else View(D['xs1'], RD('xs1', 0))
    k.dma('sp', XS2[0:N], xsrc)
    for nb in range(2):
        pp = sc_bank()
        for mc in range(8):
            k.mm(pp.ap()[0:N, :], MT[:, mc, :], WOUTB[:, mc, nb * 512:(nb + 1) * 512], mc == 0, mc == 7)
        k.tt('dve', XS2[0:N, nb * 512:(nb + 1) * 512], XS2[0:N, nb * 512:(nb + 1) * 512], pp.ap()[0:N, :], ALU.add)
    if l == depth - 1:
        k.dma('sp', DV('y_s'), XS2[0:N], outp=True)
    else:
        k.dma('sp', View(D['xs1'], RD('xs1', 0)), XS2[0:N])


def kernel(x_prompt, x_sample, state_conv, state_pool, cache_win_kv, cache_cmp_kv, cache_slc_kv, page_table,
           w_norm, w_in, w_out, w_dw, b_dw, ln_g, ln_b, w_pw, b_pw, w_pool, pool_scale,
           g_q, g_k, cmp_pe, cmp_w1, cmp_w2, _cores=None, _flags=None):
    f32 = np.float32
    inp = dict(w_norm=np.asarray(w_norm, f32), w_dw=np.asarray(w_dw, f32), b_dw=np.asarray(b_dw, f32),
               ln_g=np.asarray(ln_g, f32), ln_b=np.asarray(ln_b, f32), b_pw=np.asarray(b_pw, f32),
               w_pool=np.asarray(w_pool, f32), pool_scale=np.asarray(pool_scale, f32),
               cmp_pe=np.asarray(cmp_pe, f32), cmp_w1=np.asarray(cmp_w1, f32), cmp_w2=np.asarray(cmp_w2, f32))
    lay = _layout_params(inp)
    consts = _consts()
    flags = _flags or {}
    n_phys = cache_cmp_kv.shape[1]
    nc = build_nc(n_phys, **flags)
    cores = list(range(NCORES)) if _cores is None else _cores
    cc = np.ascontiguousarray(np.asarray(cache_cmp_kv, f32)).reshape(2 * n_phys * 8, 4096)
    csl = np.ascontiguousarray(np.asarray(cache_slc_kv, f32)).reshape(2 * n_phys * 8, 4096)
    shared = dict(w_in=np.asarray(w_in, f32), w_out=np.asarray(w_out, f32), w_pw=np.asarray(w_pw, f32),
                  g_q=np.asarray(g_q, f32), g_k=np.asarray(g_k, f32), w_dw=inp['w_dw'],
                  cache_cmp=cc, cache_slc=csl)
    for kk_, v in lay.items():
        shared[kk_] = v
    for kk_, v in consts.items():
        shared['c_' + kk_] = v
    in_maps = []
    for c in cores:
        m = dict(shared)
        m['xp'] = np.ascontiguousarray(np.asarray(x_prompt[c], f32))
        sl = slice(c * NSMP, (c + 1) * NSMP)
        m['xs'] = np.ascontiguousarray(np.asarray(x_sample[sl, 0], f32))
        m['state_conv'] = np.ascontiguousarray(np.asarray(state_conv[:, sl], f32))
        m['state_pool'] = np.ascontiguousarray(np.asarray(state_pool[:, sl], f32))
        m['cache_win'] = np.ascontiguousarray(np.asarray(cache_win_kv[:, sl], f32)).reshape(2, NSMP, 256, 256)
        m['ptab_t'] = np.ascontiguousarray(np.asarray(page_table[sl], np.int32).T)
        in_maps.append(m)
    res = run_bass_kernel_spmd(nc, in_maps, core_ids=list(range(len(cores))))
    R = res.results
    nb = len(cores)
    def gat(name):
        return [np.asarray(r[name]) for r in R]
    y_p = np.stack(gat('y_p'), 0)
    y_s = np.concatenate(gat('y_s'), 0).reshape(nb * NSMP, 1, DM)
    conv_p = np.stack(gat('conv_p'), 1)
    pool_p = np.stack(gat('pool_p'), 1)
    win_p = np.stack(gat('win_p'), 1).reshape(2, nb, 256, 2, 2, 64)
    cmp_p = np.stack(gat('cmp_p'), 1).reshape(2, nb, T, 2, 2, 64)
    slc_p = np.stack(gat('slc_p'), 1).reshape(2, nb, T, 2, 2, 64)
    conv_s = np.concatenate(gat('conv_s'), 1)
    pool_s = np.concatenate(gat('pool_s'), 1)
    win_s = np.concatenate(gat('win_s'), 1).reshape(2, nb * NSMP, 256, 2, 2, 64)
    cmp_s = np.concatenate(gat('cmp_s'), 1).reshape(2, nb * NSMP, 1, 2, 2, 64)
    slc_s = np.concatenate(gat('slc_s'), 1).reshape(2, nb * NSMP, 1, 2, 2, 64)
    return (y_p, y_s, conv_p, pool_p, win_p, cmp_p, slc_p, conv_s, pool_s, win_s, cmp_s, slc_s)
```
